# Optimizing a Trainium2 kernel written in Bass

```python
import jax, jax.numpy as jnp
from jax import lax
import numpy as np

D_MODEL = 1024
BATCH = 4
SEQ = 8192
DEPTH = 4
DEC_BATCH = 16
DEC_SEQ = 64
PAST_LEN = 1024

CHUNK = 64
HEAD_DIM = D_MODEL // 16
D_A = 3 * D_MODEL // 8
D_B = 3 * D_MODEL // 8
D_C = D_MODEL - D_A - D_B
N_HEADS_B = D_B // HEAD_DIM
D_IN = 2 * D_A + 2 * D_B + 3 * D_C
CONV_A_WIDTH = 31
CONV_C_WIDTH = 3
GMLP_BLOCK = 128
D_FF = 2816
ALPHA = (2.0 * DEPTH) ** 0.25
BETA = (8.0 * DEPTH) ** -0.25
LN_EPS = 1e-5

kernel_name = "hybrid_streaming_conv_gmlp_encoder_step"


def layer_norm(x, g, b):
    xf = x.astype(jnp.float32)
    mu = jnp.mean(xf, axis=-1, keepdims=True)
    var = jnp.mean(jnp.square(xf - mu), axis=-1, keepdims=True)
    return ((xf - mu) * lax.rsqrt(var + LN_EPS) * g.astype(jnp.float32) + b.astype(jnp.float32)).astype(x.dtype)


def swiglu_ffn(x, w_in, w_out):
    gate, up = jnp.split(x @ w_in, 2, axis=-1)
    return (jax.nn.silu(gate) * up) @ w_out


def depthwise_causal_conv(x_ext, w):
    k, c = w.shape
    return lax.conv_general_dilated(
        x_ext, w[:, None, :].astype(x_ext.dtype), window_strides=(1,), padding='VALID',
        dimension_numbers=('NWC', 'WIO', 'NWC'), feature_group_count=c)


def chunk_causal_mask(length):
    pos = jnp.arange(length)
    return (pos[None, :] // CHUNK) <= (pos[:, None] // CHUNK)


def spatial_gating(u, v, w_s, b_s, block_len):
    bt, t, _ = v.shape
    w = jnp.where(chunk_causal_mask(block_len), w_s[:, :block_len, :block_len], 0.0).astype(v.dtype)
    vh = v.reshape(bt, t // block_len, block_len, N_HEADS_B, HEAD_DIM)
    s = jnp.einsum('hij,bnjhd->bnihd', w, vh) + b_s[:, :block_len].T[None, None, :, :, None]
    return u * s.reshape(bt, t, D_B)


def token_mixers(h, hist_a, hist_c, w_in, conv_a_w, conv_a_b, norm_a_g, norm_a_b,
                 norm_b_g, norm_b_b, w_s, b_s, conv_c_w, w_out, block_len):
    z = h @ w_in
    cuts = [D_A, 2 * D_A, 2 * D_A + D_B, 2 * D_A + 2 * D_B,
            2 * D_A + 2 * D_B + D_C, 2 * D_A + 2 * D_B + 2 * D_C]
    p_a, g_a, u_b, v_b, x_c, gate_b, gate_c = jnp.split(z, cuts, axis=-1)
    a = p_a * jax.nn.sigmoid(g_a)
    a_ext = jnp.concatenate([hist_a, a], axis=1)
    a_conv = depthwise_causal_conv(a_ext, conv_a_w) + conv_a_b
    out_a = jax.nn.silu(layer_norm(a_conv, norm_a_g, norm_a_b))
    new_hist_a = a_ext[:, -(CONV_A_WIDTH - 1):]
    v_n = layer_norm(v_b, norm_b_g, norm_b_b)
    out_b = spatial_gating(u_b, v_n, w_s, b_s, block_len)
    cx = gate_c * x_c
    c_ext = jnp.concatenate([hist_c, cx], axis=1)
    out_c = gate_b * depthwise_causal_conv(c_ext, conv_c_w)
    new_hist_c = c_ext[:, -(CONV_C_WIDTH - 1):]
    y = jnp.concatenate([out_a, out_b, out_c], axis=-1) @ w_out
    return y, new_hist_a, new_hist_c, v_n


def trunk(x, hist_a, hist_c, block_len, weights):
    (w_ffn1_in, w_ffn1_out, ln1_g, ln1_b, w_in, conv_a_w, conv_a_b, norm_a_g, norm_a_b,
     norm_b_g, norm_b_b, w_s, b_s, conv_c_w, w_out, ln2_g, ln2_b,
     w_ffn2_in, w_ffn2_out, ln3_g, ln3_b) = weights
    states_a, states_c, states_v = [], [], []
    for l in range(DEPTH):
        x = layer_norm(ALPHA * x + 0.5 * swiglu_ffn(x, w_ffn1_in[l], w_ffn1_out[l]), ln1_g[l], ln1_b[l])
        y, ha, hc, vn = token_mixers(x, hist_a[l], hist_c[l], w_in[l], conv_a_w[l], conv_a_b[l],
                                     norm_a_g[l], norm_a_b[l], norm_b_g[l], norm_b_b[l],
                                     w_s[l], b_s[l], conv_c_w[l], w_out[l], block_len)
        x = layer_norm(ALPHA * x + y, ln2_g[l], ln2_b[l])
        x = layer_norm(ALPHA * x + 0.5 * swiglu_ffn(x, w_ffn2_in[l], w_ffn2_out[l]), ln3_g[l], ln3_b[l])
        states_a.append(ha)
        states_c.append(hc)
        states_v.append(vn)
    return x, jnp.stack(states_a), jnp.stack(states_c), jnp.stack(states_v)


def setup_inputs(seed: int = 0) -> dict:
    key = jax.random.key(seed)
    ks = iter(jax.random.split(key, 32))
    nrm = lambda shape, s: s * jax.random.normal(next(ks), shape, jnp.float32)
    L = DEPTH
    return {
        "x_prompt": nrm((BATCH, SEQ, D_MODEL), 1.0),
        "x_sample": nrm((DEC_BATCH, DEC_SEQ, D_MODEL), 1.0),
        "cache_conv_a": nrm((L, DEC_BATCH, CONV_A_WIDTH - 1, D_A), 0.5),
        "cache_conv_c": nrm((L, DEC_BATCH, CONV_C_WIDTH - 1, D_C), 0.5),
        "w_ffn1_in": nrm((L, D_MODEL, 2 * D_FF), D_MODEL ** -0.5),
        "w_ffn1_out": nrm((L, D_FF, D_MODEL), BETA * D_FF ** -0.5),
        "ln1_g": 1.0 + nrm((L, D_MODEL), 0.01),
        "ln1_b": nrm((L, D_MODEL), 0.01),
        "w_in": nrm((L, D_MODEL, D_IN), D_MODEL ** -0.5),
        "conv_a_w": nrm((L, CONV_A_WIDTH, D_A), CONV_A_WIDTH ** -0.5),
        "conv_a_b": nrm((L, D_A), 0.01),
        "norm_a_g": 1.0 + nrm((L, D_A), 0.01),
        "norm_a_b": nrm((L, D_A), 0.01),
        "norm_b_g": 1.0 + nrm((L, D_B), 0.01),
        "norm_b_b": nrm((L, D_B), 0.01),
        "w_s": nrm((L, N_HEADS_B, GMLP_BLOCK, GMLP_BLOCK), 0.5 * GMLP_BLOCK ** -0.5),
        "b_s": 1.0 + nrm((L, N_HEADS_B, GMLP_BLOCK), 0.01),
        "conv_c_w": nrm((L, CONV_C_WIDTH, D_C), CONV_C_WIDTH ** -0.5),
        "w_out": nrm((L, D_MODEL, D_MODEL), BETA * D_MODEL ** -0.5),
        "ln2_g": 1.0 + nrm((L, D_MODEL), 0.01),
        "ln2_b": nrm((L, D_MODEL), 0.01),
        "w_ffn2_in": nrm((L, D_MODEL, 2 * D_FF), D_MODEL ** -0.5),
        "w_ffn2_out": nrm((L, D_FF, D_MODEL), BETA * D_FF ** -0.5),
        "ln3_g": 1.0 + nrm((L, D_MODEL), 0.01),
        "ln3_b": nrm((L, D_MODEL), 0.01),
    }


def reference(x_prompt, x_sample, cache_conv_a, cache_conv_c, w_ffn1_in, w_ffn1_out, ln1_g, ln1_b,
              w_in, conv_a_w, conv_a_b, norm_a_g, norm_a_b, norm_b_g, norm_b_b, w_s, b_s,
              conv_c_w, w_out, ln2_g, ln2_b, w_ffn2_in, w_ffn2_out, ln3_g, ln3_b):
    weights = (w_ffn1_in, w_ffn1_out, ln1_g, ln1_b, w_in, conv_a_w, conv_a_b, norm_a_g, norm_a_b,
               norm_b_g, norm_b_b, w_s, b_s, conv_c_w, w_out, ln2_g, ln2_b,
               w_ffn2_in, w_ffn2_out, ln3_g, ln3_b)
    bp = x_prompt.shape[0]
    zeros_a = jnp.zeros((DEPTH, bp, CONV_A_WIDTH - 1, D_A), x_prompt.dtype)
    zeros_c = jnp.zeros((DEPTH, bp, CONV_C_WIDTH - 1, D_C), x_prompt.dtype)
    y_prompt, st_a_p, st_c_p, _ = trunk(x_prompt, zeros_a, zeros_c, GMLP_BLOCK, weights)
    y_sample, st_a_s, st_c_s, st_v_s = trunk(x_sample, cache_conv_a, cache_conv_c, x_sample.shape[1], weights)
    return (y_prompt, y_sample, st_a_p, st_c_p, st_a_s, st_c_s, st_v_s)
```

```python
import math
from contextlib import ExitStack

import numpy as np
import concourse.bass as bass
import concourse.mybir as mybir
from concourse.bass_utils import run_bass_kernel_spmd

F32 = mybir.dt.float32
BF16 = mybir.dt.bfloat16
AF = mybir.ActivationFunctionType
ALU = mybir.AluOpType

D = 1024
DFF = 2816
DA = 384
DB = 384
DC = 256
DIN = 2304
NFC = 22
NFH = 11
ALPHA = 8.0 ** 0.25
EPS = 1e-5
KA = 31
KC = 3

ENGS = ("pe", "act", "dve", "pool", "sp")
EPOCH = 20000
XT_DEFER = 2
PA_HOLD = 3


class Op:
    __slots__ = ("eng", "idx", "fn", "deps", "is_dma", "semkey", "dcount", "signal", "seq")


class Prog:
    def __init__(self):
        self.ops = {e: [] for e in ENGS}
        self.lastw = {}
        self.readers = {}
        self.dma_counts = {}

    def _add(self, eng, fn, reads, writes, is_dma=False, semkey=None):
        op = Op()
        op.eng = eng
        op.idx = len(self.ops[eng])
        op.fn = fn
        op.is_dma = is_dma
        op.semkey = semkey
        op.signal = False
        op.seq = 0
        op.dcount = 0
        if any(isinstance(k, tuple) and k[0] == "scr" for k in list(reads) + list(writes)):
            reads = list(reads) + ["scrfence"]
        deps = {}
        for k in reads:
            w = self.lastw.get(k)
            if w is not None:
                deps[id(w)] = (w, "raw")
        for k in writes:
            w = self.lastw.get(k)
            if w is not None and id(w) not in deps:
                deps[id(w)] = (w, "waw")
            rd = self.readers.get(k)
            if rd is not None:
                for r in rd[0].values():
                    if id(r) not in deps:
                        deps[id(r)] = (r, "war")
                for r in rd[1]:
                    if id(r) not in deps:
                        deps[id(r)] = (r, "war")
        final = []
        for p, kind in deps.values():
            if p.is_dma:
                need = True
            elif p.eng != eng:
                need = True
            elif is_dma:
                need = True
            else:
                need = eng != "pe"
            if need:
                if not p.is_dma:
                    p.signal = True
                final.append(p)
        op.deps = final
        for k in reads:
            rd = self.readers.get(k)
            if rd is None:
                rd = ({}, [])
                self.readers[k] = rd
            if is_dma:
                rd[1].append(op)
            else:
                rd[0][eng] = op
        for k in writes:
            self.lastw[k] = op
            self.readers[k] = ({}, [])
        if is_dma:
            c = self.dma_counts.get(semkey, 0) + 1
            self.dma_counts[semkey] = c
            op.dcount = c
        self.ops[eng].append(op)
        return op

    def pe(self, fn, reads=(), writes=()):
        return self._add("pe", fn, reads, writes)

    def act(self, fn, reads=(), writes=()):
        return self._add("act", fn, reads, writes)

    def dve(self, fn, reads=(), writes=()):
        return self._add("dve", fn, reads, writes)

    def pool(self, fn, reads=(), writes=()):
        return self._add("pool", fn, reads, writes)

    def dma(self, eng, out_ap, in_ap, reads=(), writes=(), semkey=None):
        def fn(e, out_ap=out_ap, in_ap=in_ap):
            return e.dma_start(out=out_ap, in_=in_ap)
        return self._add(eng, fn, reads, writes, is_dma=True, semkey=semkey)

    def emit(self, nc, es):
        nsig = {}
        for eng in ENGS:
            cnt = 0
            for op in self.ops[eng]:
                if op.signal and not op.is_dma:
                    cnt += 1
                    op.seq = cnt
            nsig[eng] = cnt
        eng_sems = {}
        for eng in ENGS:
            n = max(1, math.ceil(nsig[eng] / EPOCH))
            eng_sems[eng] = [es.enter_context(nc.semaphore(f"s_{eng}_{i}")) for i in range(n)]
        dma_sems = {}
        for i, k in enumerate(self.dma_counts):
            dma_sems[k] = es.enter_context(nc.semaphore(f"d_{i}"))

        def resolve(p):
            if p.is_dma:
                return dma_sems[p.semkey], 16 * p.dcount
            ep = (p.seq - 1) // EPOCH
            return eng_sems[p.eng][ep], p.seq - ep * EPOCH

        def run(eng, e):
            waited = {}
            for op in self.ops[eng]:
                for p in op.deps:
                    sem, val = resolve(p)
                    if waited.get(id(sem), 0) < val:
                        e.wait_ge(sem, val)
                        waited[id(sem)] = val
                ins = op.fn(e)
                if op.is_dma:
                    ins.then_inc(dma_sems[op.semkey], 16)
                elif op.signal:
                    ep = (op.seq - 1) // EPOCH
                    ins.then_inc(eng_sems[eng][ep], 1)
            if eng == "sp":
                for k, c in self.dma_counts.items():
                    e.wait_ge(dma_sems[k], 16 * c)

        with nc.Block() as block:
            @block.tensor
            def _(e):
                run("pe", e)

            @block.scalar
            def _(e):
                run("act", e)

            @block.vector
            def _(e):
                run("dve", e)

            @block.gpsimd
            def _(e):
                run("pool", e)

            @block.sync
            def _(e):
                run("sp", e)


def make_cfg(L, tiles, ring=7):
    nb_max = 0
    gb = 0
    out = []
    n_s = 0
    for t in tiles:
        tt = []
        b = 0
        for st in t:
            d = dict(kind=st["kind"], nblk=st.get("nblk", 1), mask_after=st.get("mask_after", False),
                     final=st.get("final", False), b0=b)
            if d["kind"] == "S":
                d["nblk"] = 1
                n_s += 1
            d["n"] = d["nblk"] * 128
            d["c0"] = b * 128
            b += d["nblk"]
            tt.append(d)
        out.append(dict(subs=tt, nb=b, gb0=gb))
        gb += b
        nb_max = max(nb_max, b)
    nmax = max(st["n"] for t in out for st in t["subs"])
    assert n_s <= 1
    return dict(L=L, tiles=out, NB=nb_max, NBLK=gb, RING=ring, NMAX=nmax, has_s=(n_s == 1))


def build_program(cfg):
    L = cfg["L"]
    NB = cfg["NB"]
    NBLK = cfg["NBLK"]
    RING = cfg["RING"]
    NMAX = cfg["NMAX"]
    NT = NB * 128
    nc = bass.Bass("TRN2", target_bir_lowering=False)
    P = Prog()
    es = ExitStack()

    def din(name, shape):
        return nc.dram_tensor(name, list(shape), F32, kind="ExternalInput").ap()

    def dout(name, shape):
        return nc.dram_tensor(name, list(shape), F32, kind="ExternalOutput").ap()

    xin = din("xin", [NBLK * 128, D])
    cache_a = din("cache_a", [L, 2, 30, DA])
    cache_c = din("cache_c", [L, 2, 2, DC])
    maskin = din("maskin", [128, 1])
    maskT_in = din("maskT", [128, 128])
    W = {}
    for nm, shp in (("w_ffn1_in", [L, D, 2 * DFF]), ("w_ffn1_out", [L, DFF, D]), ("ln1_g", [L, D]),
                    ("ln1_b", [L, D]), ("w_in", [L, D, DIN]), ("conv_a_w", [L, KA, DA]),
                    ("conv_a_b", [L, DA]), ("norm_a_g", [L, DA]), ("norm_a_b", [L, DA]),
                    ("norm_b_g", [L, DB]), ("norm_b_b", [L, DB]), ("w_s", [L, 6, 128, 128]),
                    ("b_s", [L, 6, 128]), ("conv_c_w", [L, KC, DC]), ("w_out", [L, D, D]),
                    ("ln2_g", [L, D]), ("ln2_b", [L, D]), ("w_ffn2_in", [L, D, 2 * DFF]),
                    ("w_ffn2_out", [L, DFF, D]), ("ln3_g", [L, D]), ("ln3_b", [L, D])):
        W[nm] = din(nm, shp)
    yout = dout("yout", [NBLK * 128, D])
    st_a_p = dout("st_a_p", [L, 30, DA])
    st_c_p = dout("st_c_p", [L, 2, DC])
    st_a_s = dout("st_a_s", [L, 2, 30, DA])
    st_c_s = dout("st_c_s", [L, 2, 2, DC])
    st_v_s = dout("st_v_s", [L, 128, DB])

    def sb(name, shape, dt=F32):
        return es.enter_context(nc.sbuf_tensor(name, list(shape), dt))

    def ps(name, shape, dt=F32):
        return es.enter_context(nc.psum_tensor(name, list(shape), dt))

    xres = sb("xres", [128, NB, D])
    xT = sb("xT", [128, 8, NT], BF16)
    hT = sb("hT", [128, NFH, NT], BF16)
    ring = sb("ring", [128, RING, 4096], BF16)
    NA = NMAX + 60
    SCR = max((3 * NA + 1) // 2 + 180 + 8 * NA + 8, 5 * NMAX + 2000, 8 * NMAX + 64)
    scr = sb("scr", [128, SCR])
    lng = sb("lng", [128, D])
    lnb = sb("lnb", [128, D])
    tmpA = [sb(f"tmpA{i}", [128, D]) for i in range(2)]
    xbf = [sb(f"xbf{i}", [128, D], BF16) for i in range(XT_DEFER + 1)]
    sgt = [sb(f"sgt{i}", [128, 512]) for i in range(2)]
    small = sb("small", [128, 8, 16])
    sqjunk = sb("sqjunk", [128, D], BF16)
    identB = sb("identB", [128, 128], BF16)
    identF = sb("identF", [128, 128])
    onesF = sb("onesF", [128, 128])
    maskT = sb("maskTs", [128, 128])
    maskc = sb("maskc", [128, 1])
    Wn = sb("Wn", [128, 6, 128])
    WT = sb("WT", [128, 6, 128], BF16)
    WTs = sb("WTs", [128, 6, 128], BF16)
    bsb = sb("bsb", [128, 3, 128])
    bsbs = sb("bsbs", [128, 3, 128])
    nbg = sb("nbg", [128, DB])
    nbb = sb("nbb", [128, DB])
    cwa = sb("cwa", [128, 3, KA])
    pva = sb("pva", [128, 3, 3])
    cwc = sb("cwc", [128, 2, KC])
    hist_a = sb("hist_a", [128, L, 3, 30])
    hist_c = sb("hist_c", [128, L, 2, 2])
    stA = sb("stA", [32, DA])
    stB = sb("stB", [32, DA])
    stC = sb("stC", [4, DC])
    stD = sb("stD", [4, DC])
    Iq = sb("Iq", [128, 32])
    dqt = sb("dq", [128, 3, KA, 32], BF16)
    dq = dqt[:, :, :, :]
    dq_keys = ["dq"]
    dummy = sb("dummyt", [128, 2])
    epsb = sb("epsb", [128, 2])

    acc = [ps(f"acc{i}", [128, 1024]) for i in range(3)]
    trb = [ps(f"trb{i}", [128, 512]) for i in range(2)]

    class Rot:
        def __init__(self, n):
            self.n = n
            self.i = 0

        def next(self):
            i = self.i
            self.i = (i + 1) % self.n
            return i

    rot_pair = Rot(3)
    rot_tr = Rot(2)
    rot_tmpA = Rot(2)
    rot_xbf = Rot(XT_DEFER + 1)
    rot_sgt = Rot(2)
    rot_small = Rot(8)

    def single(i):
        return acc[i // 2][:, (i % 2) * 512:(i % 2 + 1) * 512], ("ps", i)

    def pair(i):
        return acc[i], [("ps", 2 * i), ("ps", 2 * i + 1)]

    def trbank(i):
        return trb[i], ("ps", 6 + i)

    def bcast_rows(ap2d_row, nparts):
        n = ap2d_row.shape[-1]
        return bass.AP(ap2d_row.tensor, ap2d_row.offset, [[0, nparts], [1, n]])

    def mm(out, lhsT, rhs, start, stop):
        return lambda e: e.matmul(out, lhsT, rhs, start=start, stop=stop)

    def tp(out, in_, ident):
        return lambda e: e.transpose(out, in_, ident)

    def actf(out, in_, func, bias=None, scale=None):
        kw = {}
        if bias is not None:
            kw["bias"] = bias
        if scale is not None:
            kw["scale"] = scale
        return lambda e: e.activation(out, in_, func, **kw)

    def tt(out, in0, in1, op):
        return lambda e: e.tensor_tensor(out, in0, in1, op)

    def ts(out, in0, s1, s2, op0, op1=None):
        if op1 is None:
            return lambda e: e.tensor_scalar(out, in0, s1, None, op0)
        return lambda e: e.tensor_scalar(out, in0, s1, s2, op0, op1)

    def stt(out, in0, scalar, in1, op0, op1):
        return lambda e: e.scalar_tensor_tensor(out, in0, scalar, in1, op0, op1)

    def cp(out, in_):
        return lambda e: e.tensor_copy(out, in_)

    def mset(ap, v):
        return lambda e: e.memset(ap, v)

    identin = din("identin", [128, 128])
    P.dma("sp", identF[:, :], identin, writes=["identF"], semkey="identF")
    P.dma("sp", maskT[:, :], maskT_in, writes=["maskT"], semkey="maskT")
    P.dma("sp", maskc[:, :], maskin, writes=["maskc"], semkey="maskc")
    P.act(actf(identB[:, :], identF[:, :], AF.Copy), reads=["identF"], writes=["identB"])
    for q in range(4):
        P.dve(cp(Iq[32 * q:32 * q + 32, :], identF[32 * q:32 * q + 32, 32 * q:32 * q + 32]), reads=["identF"],
              writes=["Iq"])
    P.dve(mset(onesF[:, :], 1.0), writes=["onesF"])
    P.dve(mset(epsb[:, 0:1], EPS), writes=["epsb"])
    P.dve(mset(epsb[:, 1:2], EPS / (ALPHA * ALPHA)), writes=["epsb"])
    P.dve(mset(hist_a[:, :, :, :].rearrange("p a b c -> p (a b c)"), 0.0), writes=[("hist_a", l) for l in range(L)])
    P.dve(mset(hist_c[:, :, :, :].rearrange("p a b c -> p (a b c)"), 0.0), writes=[("hist_c", l) for l in range(L)])
    P.dve(mset(dummy[:, :], 0.0), writes=["scrfence"])

    units = []

    def ffn_units(l, which):
        wi = W["w_ffn%d_in" % which]
        wo = W["w_ffn%d_out" % which]
        for half in range(2):
            j = half * NFH
            while j < (half + 1) * NFH:
                nch = min(2, (half + 1) * NFH - j)
                units.append(("fin", wi, l, j, nch))
                j += nch
            k = 0
            while k < NFH:
                nk = min(4, NFH - k)
                units.append(("fout", wo, l, half * NFH + k, nk))
                k += nk

    MIXCOLS = [(0, 384), (384, 768), (768, 1152), (1152, 1536), (1536, 2048), (2048, 2304)]

    def mixer_units(l):
        for c0, c1 in MIXCOLS:
            units.append(("mix", W["w_in"], l, c0, c1))
        for h in range(2):
            units.append(("mo", W["w_out"], l, h))

    for _t in cfg["tiles"]:
        for l in range(L):
            ffn_units(l, 1)
            mixer_units(l)
            ffn_units(l, 2)

    class WS:
        issued = 0
        acquired = 0
        released = 0

    def slot_view(s, kind):
        flat = ring[:, s, :]
        if kind == "fin":
            return flat.rearrange("p (k t f) -> p k t f", k=8, t=2)
        if kind == "fout":
            return flat.rearrange("p (k m) -> p k m", k=4)
        return flat.rearrange("p (k c) -> p k c", k=8)

    def w_issue(u):
        s = u % RING
        d = units[u]
        kind = d[0]
        v = slot_view(s, kind)
        if kind == "fin":
            _, w, l, j, nch = d
            wsrc = w[l].rearrange("(kc p) (two f) -> p kc two f", p=128, two=2)
            for part in range(2):
                P.dma("pool", v[:, :, part, 0:nch * 128], wsrc[:, :, part, j * 128:(j + nch) * 128],
                      writes=[("ring", s)], semkey=("ring", s))
            return
        elif kind == "fout":
            _, w, l, k0, nk = d
            src = w[l].rearrange("(kc p) m -> p kc m", p=128)[:, k0:k0 + nk, :]
            dst = v[:, 0:nk, :]
        elif kind == "mix":
            _, w, l, c0, c1 = d
            src = w[l].rearrange("(kc p) c -> p kc c", p=128)[:, :, c0:c1]
            dst = v[:, :, 0:c1 - c0]
        else:
            _, w, l, h = d
            src = w[l].rearrange("(kc p) c -> p kc c", p=128)[:, :, h * 512:(h + 1) * 512]
            dst = v[:, :, :]
        P.dma("pool", dst, src, writes=[("ring", s)], semkey=("ring", s))

    def w_prefetch():
        while WS.issued < len(units) and WS.issued < WS.released + RING:
            w_issue(WS.issued)
            WS.issued += 1

    def w_acquire(kind):
        u = WS.acquired
        assert u < WS.issued, "weight ring too small"
        assert units[u][0] == kind, (units[u][0], kind)
        WS.acquired += 1
        s = u % RING
        return slot_view(s, kind), ("ring", s)

    def w_release(n=1):
        WS.released += n
        w_prefetch()

    w_prefetch()

    pending = []

    def cast_x(b):
        i = rot_xbf.next()
        assert all(pi != i for _, pi in pending)
        P.act(actf(xbf[i][:, :], xres[:, b, :], AF.Copy), reads=[("xres", b)], writes=[("xbf", i)])
        pending.append((b, i))

    def flush_xT(keep=0):
        while len(pending) > keep:
            b, i = pending.pop(0)
            finish_xT(b, i)

    chain_live = set()

    def need_xT(blocks):
        if any(b in chain_live for b in blocks):
            LNP.drain()
        if any(pb in blocks for pb, _ in pending):
            flush_xT(0)

    def finish_xT(b, i):
        j = rot_tr.next()
        trt, trk = trbank(j)
        trv = trt[:, :].bitcast(BF16).rearrange("p (a b) -> p a b", a=8)
        for kc in range(8):
            P.pe(tp(trv[:, kc, :], xbf[i][:, kc * 128:(kc + 1) * 128], identB[:, :]),
                 reads=[("xbf", i), "identB"], writes=[trk])
        P.act(actf(xT[:, :, b * 128:(b + 1) * 128], trv, AF.Copy), reads=[trk], writes=[("xT", b)])

    def make_xT(b):
        cast_x(b)
        flush_xT(0)

    class Pipe:
        def __init__(self):
            self.live = []

        def push(self, gen):
            self.live.insert(0, gen)
            self._advance()

        def _advance(self):
            nxt = []
            for g in self.live:
                try:
                    next(g)
                    nxt.append(g)
                except StopIteration:
                    pass
            self.live = nxt

        def drain(self):
            while self.live:
                self._advance()

    LNP = Pipe()
    LN_NEXT = [None]

    def ln_tick():
        if LNP.live:
            LNP._advance()
        if LN_NEXT[0] is not None and not LNP.live:
            load_ln(*LN_NEXT[0])
            LN_NEXT[0] = None

    def ln_params_ready():
        if LN_NEXT[0] is not None:
            LNP.drain()
            load_ln(*LN_NEXT[0])
            LN_NEXT[0] = None

    def ln_slot():
        i = rot_small.next()
        return small[:, i, :], ("small", i)

    def ln_chain(b, epsp, need_xT_, out_gb, sm, ks):
        kx = ("xres", b)
        chain_live.add(b)
        P.act(lambda e: e.activation(sqjunk[:, :], xres[:, b, :], AF.Square, scale=1.0 / 32.0,
                                     accum_out=sm[:, 1:2]), reads=[kx], writes=["sqjunk", ks])
        yield
        P.dve(tt(sm[:, 2:3], sm[:, 0:1], sm[:, 0:1], ALU.mult), reads=[ks], writes=[ks])
        P.dve(stt(sm[:, 13:14], sm[:, 2:3], -1.0 / (1024.0 * 1024.0), sm[:, 1:2], ALU.mult, ALU.add),
              reads=[ks], writes=[ks])
        eb = epsb[:, 0:1] if epsp == EPS else epsb[:, 1:2]
        P.act(actf(sm[:, 14:15], sm[:, 13:14], AF.Sqrt, bias=eb), reads=[ks, "epsb"], writes=[ks])
        yield
        P.dve(lambda e: e.reciprocal(sm[:, 14:15], sm[:, 14:15]), reads=[ks], writes=[ks])
        P.dve(stt(sm[:, 15:16], sm[:, 0:1], -1.0 / 1024.0, sm[:, 14:15], ALU.mult, ALU.mult), reads=[ks], writes=[ks])
        t = rot_tmpA.next()
        kt = ("tmpA", t)
        P.act(actf(tmpA[t][:, :], xres[:, b, :], AF.Identity, bias=sm[:, 15:16], scale=sm[:, 14:15]),
              reads=[kx, ks], writes=[kt])
        yield
        P.dve(tt(tmpA[t][:, :], tmpA[t][:, :], lng[:, :], ALU.mult), reads=[kt, "lng"], writes=[kt])
        P.dve(tt(xres[:, b, :], tmpA[t][:, :], lnb[:, :], ALU.add), reads=[kt, "lnb"], writes=[kx])
        chain_live.discard(b)
        if need_xT_:
            cast_x(b)
            flush_xT(XT_DEFER)
        if out_gb is not None:
            P.dma("sp", yout[out_gb * 128:(out_gb + 1) * 128, :], xres[:, b, :], reads=[kx], semkey=kx)

    def load_ln(gname, bname, l):
        P.dma("sp", lng[:, :], bcast_rows(W[gname][l], 128), writes=["lng"], semkey="lng")
        P.dma("sp", lnb[:, :], bcast_rows(W[bname][l], 128), writes=["lnb"], semkey="lnb")

    def ffn(tile, l, which, last):
        subs = tile["subs"]
        nb = tile["nb"]
        cscale = 0.5 / ALPHA
        epsp = EPS / (ALPHA * ALPHA)
        LN_NEXT[0] = ("ln1_g" if which == 1 else "ln3_g", "ln1_b" if which == 1 else "ln3_b", l)
        if not LNP.live:
            ln_tick()
        if which == 1:
            issue_param_dmas(tile, l)
        for half in range(2):
            if which == 1 and half == 1:
                finish_params(tile, l)
            def phase_a(wv, wk, jl, nch, sts):
                for st in sts:
                    c0, n, b0, nblk = st["c0"], st["n"], st["b0"], st["nblk"]
                    need_xT(range(b0, b0 + nblk))
                    xkeys = [("xT", b0 + i) for i in range(nblk)]
                    for ci in range(nch):
                        pi = rot_pair.next()
                        pt, pk = pair(pi)
                        for part in range(2):
                            for kc in range(8):
                                P.pe(mm(pt[:, part * 512:part * 512 + n], wv[:, kc, part, ci * 128:(ci + 1) * 128],
                                        xT[:, kc, c0:c0 + n], kc == 0, kc == 7),
                                     reads=[wk] + xkeys, writes=[pk[part]])
                        si = rot_sgt.next()
                        P.act(actf(sgt[si][:, 0:n], pt[:, 0:n], AF.Silu), reads=[pk[0]], writes=[("sgt", si)])
                        P.dve(tt(hT[:, jl + ci, c0:c0 + n], sgt[si][:, 0:n], pt[:, 512:512 + n], ALU.mult),
                              reads=[("sgt", si), pk[1]], writes=[("hT", jl + ci, b0 + i) for i in range(nblk)])
                    ln_tick()

            ulist = []
            jl = 0
            while jl < NFH:
                nch = min(2, NFH - jl)
                ulist.append((jl, nch))
                jl += nch
            ui = 0
            if half == 0 and LNP.live and len(subs) > 1:
                held = []
                for (jl, nch) in ulist[:PA_HOLD]:
                    wv, wk = w_acquire("fin")
                    held.append((wv, wk, jl, nch))
                    phase_a(wv, wk, jl, nch, subs[:-1])
                for (wv, wk, jl, nch) in held:
                    phase_a(wv, wk, jl, nch, subs[-1:])
                w_release(len(held))
                ui = len(held)
            for (jl, nch) in ulist[ui:]:
                wv, wk = w_acquire("fin")
                phase_a(wv, wk, jl, nch, subs)
                w_release()
            wviews = []
            k = 0
            while k < NFH:
                nk = min(4, NFH - k)
                wviews.append(w_acquire("fout"))
                k += nk
            if half == 1:
                ln_params_ready()
            for b in range(nb):
                pi = rot_pair.next()
                pt, pk = pair(pi)
                for hc in range(2):
                    for k in range(NFH):
                        wv, wk = wviews[k // 4]
                        P.pe(mm(pt[:, hc * 512:(hc + 1) * 512], hT[:, k, b * 128:(b + 1) * 128],
                                wv[:, k % 4, hc * 512:(hc + 1) * 512], k == 0, k == NFH - 1),
                             reads=[wk, ("hT", k, b)], writes=[pk[hc]])
                if half == 0:
                    P.dve(stt(xres[:, b, :], pt[:, :], cscale, xres[:, b, :], ALU.mult, ALU.add),
                          reads=[pk[0], pk[1], ("xres", b)], writes=[("xres", b)])
                else:
                    sm, ks = ln_slot()
                    P.dve(lambda e, b=b, pt=pt, sm=sm: e.scalar_tensor_tensor(
                        xres[:, b, :], pt[:, :], cscale, xres[:, b, :], ALU.mult, ALU.add, accum_out=sm[:, 0:1]),
                        reads=[pk[0], pk[1], ("xres", b)], writes=[("xres", b), ks])
                    LNP.push(ln_chain(b, epsp, not last, (tile["gb0"] + b) if last else None, sm, ks))
            if last:
                LNP.drain()
            w_release(len(wviews))

    class ScrAlloc:
        def __init__(self):
            self.off = 0

        def take(self, n):
            o = self.off
            self.off += n
            assert self.off <= SCR, (self.off, SCR)
            return o

    def fence():
        P.dve(mset(dummy[:, 0:1], 0.0), writes=["scrfence"])

    def seg_layout(st, hl):
        if st["kind"] == "S":
            return [(0, 0, 64), (hl + 64, 64, 64)], 2 * (hl + 64)
        return [(0, 0, st["n"])], hl + st["n"]

    def issue_param_dmas(tile, l):
        has_p = any(s["kind"] == "P" for s in tile["subs"])
        has_s = any(s["kind"] == "S" for s in tile["subs"])

        def row(ap1d):
            n = ap1d.shape[-1]
            return bass.AP(ap1d.tensor, ap1d.offset, [[n, 1], [1, n]])
        P.dma("sp", stA[0:KA, :], W["conv_a_w"][l], writes=["stA"], semkey="stA")
        for r, nm in enumerate(("conv_a_b", "norm_a_g", "norm_a_b")):
            P.dma("sp", stB[r:r + 1, :], row(W[nm][l]), writes=["stB"], semkey="stB")
        P.dma("sp", stC[0:KC, :], W["conv_c_w"][l], writes=["stC"], semkey="stC")
        P.dma("sp", nbg[:, :], bcast_rows(W["norm_b_g"][l], 128), writes=["nbg"], semkey="nbg")
        P.dma("sp", nbb[:, :], bcast_rows(W["norm_b_b"][l], 128), writes=["nbb"], semkey="nbb")
        bs_t = W["b_s"].tensor
        if has_p:
            P.dma("sp", Wn[:, :, :], W["w_s"][l].rearrange("h i j -> i h j"), writes=["Wn"], semkey="Wn")
            for h2 in range(2):
                src_ = bass.AP(bs_t, W["b_s"][l].offset + h2 * 128, [[0, 64], [256, 3], [1, 128]])
                P.dma("sp", bsb[h2 * 64:(h2 + 1) * 64, :, :], src_, writes=["bsb"], semkey="bsb")
        if has_s:
            for h2 in range(2):
                src_ = bass.AP(bs_t, W["b_s"][l].offset + h2 * 128, [[0, 64], [256, 3], [1, 64]])
                for r in range(2):
                    P.dma("sp", bsbs[h2 * 64:(h2 + 1) * 64, :, r * 64:(r + 1) * 64], src_,
                          writes=["bsbs"], semkey="bsbs")

    def finish_params(tile, l):
        has_p = any(s["kind"] == "P" for s in tile["subs"])
        has_s = any(s["kind"] == "S" for s in tile["subs"])
        trt, trk = trbank(rot_tr.next())
        for c in range(3):
            P.pe(tp(trt[:, c * 32:c * 32 + KA], stA[0:KA, c * 128:(c + 1) * 128], identF[0:KA, 0:KA]),
                 reads=["stA", "identF"], writes=[trk])
        P.act(actf(cwa[:, :, :], trt[:, 0:96].rearrange("p (c k) -> p c k", c=3)[:, :, 0:KA], AF.Copy),
              reads=[trk], writes=["cwa"])
        trt, trk = trbank(rot_tr.next())
        for c in range(3):
            P.pe(tp(trt[:, c * 32:c * 32 + 3], stB[0:3, c * 128:(c + 1) * 128], identF[0:3, 0:3]),
                 reads=["stB", "identF"], writes=[trk])
        P.act(actf(pva[:, :, :], trt[:, 0:96].rearrange("p (c k) -> p c k", c=3)[:, :, 0:3], AF.Copy),
              reads=[trk], writes=["pva"])
        trt, trk = trbank(rot_tr.next())
        for c in range(2):
            P.pe(tp(trt[:, c * 32:c * 32 + KC], stC[0:KC, c * 128:(c + 1) * 128], identF[0:KC, 0:KC]),
                 reads=["stC", "identF"], writes=[trk])
        P.act(actf(cwc[:, :, :], trt[:, 0:64].rearrange("p (c k) -> p c k", c=2)[:, :, 0:KC], AF.Copy),
              reads=[trk], writes=["cwc"])
        for c in range(3):
            cw = cwa[:, c, :]
            i0 = bass.AP(Iq, 0, [list(Iq[:, :].ap[0]), [0, KA], [1, 32]])
            i1 = bass.AP(cwa, cw.offset, [list(cw.ap[0]), [1, KA], [0, 32]])
            P.dve(tt(dq[:, c, :, :], i0, i1, ALU.mult), reads=["Iq", "cwa"], writes=dq_keys)
        if has_p:
            for h in range(6):
                trt, trk = trbank(rot_tr.next())
                P.pe(tp(trt[:, 0:128], Wn[:, h, :], identF[:, :]), reads=["Wn", "identF"], writes=[trk])
                P.dve(tt(WT[:, h, :], trt[:, 0:128], maskT[:, :], ALU.mult), reads=[trk, "maskT"], writes=["WT"])
        if has_s:
            P.dve(mset(Wn[0:64, :, 64:128], 0.0), reads=["Wn"], writes=["Wn"])
            P.dve(mset(Wn[64:128, :, 0:64], 0.0), reads=["Wn"], writes=["Wn"])
            for q in range(2):
                P.dma("sp", Wn[q * 64:(q + 1) * 64, :, q * 64:(q + 1) * 64],
                      W["w_s"][l][:, 0:64, 0:64].rearrange("h i j -> i h j"), writes=["Wn"], semkey="Wn")
            for h in range(6):
                trt, trk = trbank(rot_tr.next())
                P.pe(tp(trt[:, 0:128], Wn[:, h, :], identF[:, :]), reads=["Wn", "identF"], writes=[trk])
                P.act(actf(WTs[:, h, :], trt[:, 0:128], AF.Copy), reads=[trk], writes=["WTs"])

    def hist_fill_and_state(st, l, ext, nch, hl, segs, hist, cache, cst, stg, out_p, out_s, kname, kext):
        if st["kind"] == "P":
            P.act(actf(ext[:, :, 0:hl], hist[:, l, :, :], AF.Copy), reads=[(kname, l)], writes=[kext + ("h",)])
        else:
            for s in range(2):
                cstt, cstk = cst
                P.dma("sp", cstt[0:hl, :], cache[l, s], writes=[cstk], semkey=cstk)
                j = rot_tr.next()
                trt, trk = trbank(j)
                for c in range(nch):
                    P.pe(tp(trt[:, c * 32:c * 32 + hl], cstt[0:hl, c * 128:(c + 1) * 128], identF[0:hl, 0:hl]),
                         reads=[cstk, "identF"], writes=[trk])
                eo = segs[s][0]
                P.act(actf(ext[:, :, eo:eo + hl], trt[:, 0:nch * 32].rearrange("p (c k) -> p c k", c=nch)[:, :, 0:hl],
                           AF.Copy), reads=[trk], writes=[kext + ("h",)])

    def state_out(st, l, ext, nch, hl, segs, hist, stg, out_p, out_s, kname, kext):
        rk = [kext + ("h",), kext + ("d",)]
        if st["kind"] == "P":
            eo, _, ntok = segs[0]
            src = ext[:, :, eo + ntok:eo + ntok + hl]
            if st["mask_after"]:
                P.act(actf(hist[:, l, :, :], src, AF.Copy, scale=maskc[:, 0:1]), reads=rk + ["maskc"],
                      writes=[(kname, l)])
            else:
                P.act(actf(hist[:, l, :, :], src, AF.Copy), reads=rk, writes=[(kname, l)])
            outs = [(src, out_p[l])] if st["final"] else []
        else:
            outs = []
            for s in range(2):
                eo, _, ntok = segs[s]
                outs.append((ext[:, :, eo + ntok:eo + ntok + hl], out_s[l, s]))
        for src, dst in outs:
            j = rot_tr.next()
            trt, trk = trbank(j)
            for c in range(nch):
                P.pe(tp(trt[0:hl, c * 128:(c + 1) * 128], src[:, c, :], identF[:, :]), reads=rk + ["identF"],
                     writes=[trk])
            stgt, stgk = stg
            P.act(actf(stgt[0:hl, :], trt[0:hl, 0:nch * 128], AF.Copy), reads=[trk], writes=[stgk])
            P.dma("sp", dst, stgt[0:hl, :], reads=[stgk], semkey=stgk)

    def mixer(tile, l):
        subs = tile["subs"]
        nb = tile["nb"]
        epsp = EPS / (ALPHA * ALPHA)
        LN_NEXT[0] = ("ln2_g", "ln2_b", l)
        if not LNP.live:
            ln_tick()
        fence()
        sa = ScrAlloc()
        o_aext = sa.take((3 * NA + 1) // 2)
        o_tail = sa.take(3 * 2 * 30)
        o_co = sa.take(3 * (NMAX + 60))
        o_f = [sa.take(NMAX + 60) for _ in range(5)]
        aext = scr[:, o_aext:o_aext + (3 * NA + 1) // 2].bitcast(BF16)[:, 0:3 * NA].rearrange("p (c k) -> p c k", c=3)
        tail = scr[:, o_tail:o_tail + 180].rearrange("p (c s k) -> p c s k", c=3, s=2)
        co = scr[:, o_co:o_co + 3 * (NMAX + 60)].rearrange("p (c k) -> p c k", c=3)
        fbuf = [scr[:, o:o + NMAX + 60] for o in o_f]
        fkey = lambda i: ("scr", "A", "f", i)
        wp, wpk = w_acquire("mix")
        wg, wgk = w_acquire("mix")
        rot_s3 = Rot(3)
        kp = rot_pair.i
        border = [2 * ((kp + d) % 3) + h for d in range(3) for h in range(2)]
        proj_banks, conv_banks = border[0:3], border[3:6]
        kext = ("scr", "A", "aext")
        ktl = ("scr", "A", "tail")
        kco = ("scr", "A", "co")
        def ma_body(st):
            c0, n, b0, nblk = st["c0"], st["n"], st["b0"], st["nblk"]
            need_xT(range(b0, b0 + nblk))
            xkeys = [("xT", b0 + i) for i in range(nblk)]
            segs, na = seg_layout(st, 30)
            nco = na - 30
            is_s = st["kind"] == "S"
            if not is_s:
                P.act(actf(aext[:, :, 0:30], hist_a[:, l, :, :], AF.Copy), reads=[("hist_a", l)],
                      writes=[kext + ("h",)])
            else:
                for s in range(2):
                    P.dma("sp", stA[0:30, :], cache_a[l, s], writes=["stA"], semkey="stA")
                    trt, trk = trbank(rot_tr.next())
                    for c in range(3):
                        P.pe(tp(trt[:, c * 32:c * 32 + 30], stA[0:30, c * 128:(c + 1) * 128], identF[0:30, 0:30]),
                             reads=["stA", "identF"], writes=[trk])
                    eo = segs[s][0]
                    P.act(actf(aext[:, :, eo:eo + 30],
                               trt[:, 0:96].rearrange("p (c k) -> p c k", c=3)[:, :, 0:30], AF.Copy),
                          reads=[trk], writes=[kext + ("h",)])
            for c in range(3):
                gi = proj_banks[rot_s3.next()]
                gt, gk = single(gi)
                for kc in range(8):
                    P.pe(mm(gt[:, 0:n], wg[:, kc, c * 128:(c + 1) * 128], xT[:, kc, c0:c0 + n], kc == 0, kc == 7),
                         reads=[wgk] + xkeys, writes=[gk])
                sgi = c % 2
                P.act(actf(fbuf[sgi][:, 0:n], gt[:, 0:n], AF.Sigmoid), reads=[gk], writes=[fkey(sgi)])
                pi = proj_banks[rot_s3.next()]
                ptt, ppk = single(pi)
                for kc in range(8):
                    P.pe(mm(ptt[:, 0:n], wp[:, kc, c * 128:(c + 1) * 128], xT[:, kc, c0:c0 + n], kc == 0, kc == 7),
                         reads=[wpk] + xkeys, writes=[ppk])
                ln_tick()
                for si, (eo, to, ntok) in enumerate(segs):
                    P.dve(tt(aext[:, c, eo + 30:eo + 30 + ntok], ptt[:, to:to + ntok], fbuf[sgi][:, to:to + ntok],
                             ALU.mult), reads=[ppk, fkey(sgi)], writes=[kext + ("d",)])
                    P.dve(tt(tail[:, c, si, :], ptt[:, to + ntok - 30:to + ntok], fbuf[sgi][:, to + ntok - 30:to + ntok],
                             ALU.mult), reads=[ppk, fkey(sgi)], writes=[ktl])
            if not is_s:
                if st["mask_after"]:
                    P.act(actf(hist_a[:, l, :, :], tail[:, :, 0, :], AF.Copy, scale=maskc[:, 0:1]),
                          reads=[ktl, "maskc"], writes=[("hist_a", l)])
                else:
                    P.act(actf(hist_a[:, l, :, :], tail[:, :, 0, :], AF.Copy), reads=[ktl], writes=[("hist_a", l)])
                outs = [(0, st_a_p[l])] if st["final"] else []
            else:
                outs = [(s, st_a_s[l, s]) for s in range(2)]
            for si, dst in outs:
                trt, trk = trbank(rot_tr.next())
                for c in range(3):
                    P.pe(tp(trt[0:30, c * 128:(c + 1) * 128], tail[:, c, si, :], identF[:, :]), reads=[ktl, "identF"],
                         writes=[trk])
                P.act(actf(stB[0:30, :], trt[0:30, 0:384], AF.Copy), reads=[trk], writes=["stB"])
                P.dma("sp", dst, stB[0:30, :], reads=["stB"], semkey="stB")
            yield
            cbank = [single(conv_banks[c]) for c in range(3)]
            for c in range(3):
                cb, cbk = cbank[c]
                for k in range(KA):
                    for q in range(4):
                        P.pe(lambda e, cb=cb, c=c, k=k, q=q, nco=nco: e.matmul(
                            cb[32 * q:32 * q + 32, 0:nco], dq[32 * q:32 * q + 32, c, k, :],
                            aext[32 * q:32 * q + 32, c, k:k + nco], start=(k == 0), stop=(k == KA - 1),
                            tile_position=(32 * q, 32 * q)),
                            reads=[kext + ("h",), kext + ("d",)] + dq_keys, writes=[cbk])
            yield
            s1t, s1k = trbank(0)
            s2t, s2k = trbank(1)
            for c in range(3):
                cb, cbk = cbank[c]
                sqi = 1 + c % 2
                P.act(actf(co[:, c, 0:nco], cb[:, 0:nco], AF.Identity, bias=pva[:, c, 0:1]), reads=[cbk, "pva"],
                      writes=[kco + (c,)])
                P.act(actf(fbuf[sqi][:, 0:nco], cb[:, 0:nco], AF.Square, bias=pva[:, c, 0:1]), reads=[cbk, "pva"],
                      writes=[fkey(sqi)])
                P.pe(mm(s1t[:, 0:nco], onesF[:, :], co[:, c, 0:nco], c == 0, c == 2),
                     reads=["onesF", kco + (c,)], writes=[s1k])
                P.pe(mm(s2t[:, 0:nco], onesF[:, :], fbuf[sqi][:, 0:nco], c == 0, c == 2),
                     reads=["onesF", fkey(sqi)], writes=[s2k])
            mt, rt = fbuf[3], fbuf[4]
            tm = fbuf[0]
            KM, KR = fkey(3), fkey(4)
            P.dve(ts(mt[:, 0:nco], s1t[:, 0:nco], 1.0 / DA, None, ALU.mult), reads=[s1k], writes=[KM])
            P.dve(tt(tm[:, 0:nco], mt[:, 0:nco], mt[:, 0:nco], ALU.mult), reads=[KM], writes=[fkey(0)])
            P.dve(stt(tm[:, 0:nco], s2t[:, 0:nco], 1.0 / DA, tm[:, 0:nco], ALU.mult, ALU.subtract),
                  reads=[s2k, fkey(0)], writes=[fkey(0)])
            P.act(actf(rt[:, 0:nco], tm[:, 0:nco], AF.Sqrt, bias=epsb[:, 0:1]), reads=[fkey(0), "epsb"],
                  writes=[KR])
            P.dve(lambda e, rt=rt, nco=nco: e.reciprocal(rt[:, 0:nco], rt[:, 0:nco]), reads=[KR], writes=[KR])
            for c in range(3):
                P.dve(tt(co[:, c, 0:nco], co[:, c, 0:nco], mt[:, 0:nco], ALU.subtract), reads=[kco + (c,), KM],
                      writes=[kco + (c,)])
                P.dve(tt(co[:, c, 0:nco], co[:, c, 0:nco], rt[:, 0:nco], ALU.mult), reads=[kco + (c,), KR],
                      writes=[kco + (c,)])
                for (eo, to, ntok) in segs:
                    P.act(actf(hT[:, c, c0 + to:c0 + to + ntok], co[:, c, eo:eo + ntok], AF.Silu,
                               bias=pva[:, c, 2:3], scale=pva[:, c, 1:2]),
                          reads=[kco + (c,), "pva"], writes=[("hT", c, b0 + i) for i in range(nblk)])
        gens = [ma_body(st) for st in subs]
        step = lambda g: next(g, None)
        step(gens[0])
        step(gens[0])
        for s in range(1, len(gens)):
            step(gens[s])
            step(gens[s - 1])
            step(gens[s])
        step(gens[-1])
        w_release(2)
        fence()
        sa = ScrAlloc()
        o_u = sa.take(3 * NMAX)
        o_v32 = [sa.take(DB) for _ in range(3)]
        o_vbf = sa.take(4 * DB // 2)
        o_tb = [sa.take(NMAX) for _ in range(2)]
        uT = scr[:, o_u:o_u + 3 * NMAX].rearrange("p (c k) -> p c k", c=3)
        vn32 = [scr[:, o:o + DB] for o in o_v32]
        vnbf = scr[:, o_vbf:o_vbf + 4 * DB // 2].bitcast(BF16).rearrange("p (b k) -> p b k", b=4)
        tb = [scr[:, o:o + NMAX] for o in o_tb]
        wu, wuk = w_acquire("mix")
        wvv, wvk = w_acquire("mix")
        rot_v = Rot(3)
        rot_v32 = Rot(3)
        for st in subs:
            c0, n, b0, nblk = st["c0"], st["n"], st["b0"], st["nblk"]
            need_xT(range(b0, b0 + nblk))
            xkeys = [("xT", b0 + i) for i in range(nblk)]
            is_s = st["kind"] == "S"
            sbank = [single(c) for c in range(3)]
            wt_sb, wt_k = (WTs, "WTs") if is_s else (WT, "WT")
            pend_s = []

            def emit_s(bi, kvb, sbank=sbank, wt_sb=wt_sb, wt_k=wt_k):
                for c in range(3):
                    stt_, sk = sbank[c]
                    for h2 in range(2):
                        h = 2 * c + h2
                        P.pe(mm(stt_[h2 * 64:(h2 + 1) * 64, bi * 128:(bi + 1) * 128], vnbf[:, bi, h * 64:(h + 1) * 64],
                                wt_sb[:, h, :], True, True), reads=[kvb, wt_k], writes=[sk])

            def mb_chain(bi, b0=b0, is_s=is_s, emit_s=emit_s):
                b = b0 + bi
                vi = 3 + rot_v.next()
                vt, vk = single(vi)
                for kc in range(8):
                    P.pe(mm(vt[:, 0:DB], xT[:, kc, b * 128:(b + 1) * 128], wvv[:, kc, 0:DB], kc == 0, kc == 7),
                         reads=[wvk, ("xT", b)], writes=[vk])
                i = rot_small.next()
                sm = small[:, i, :]
                ks = ("small", i)
                P.dve(lambda e: e.bn_stats(sm[:, 0:6], vt[:, 0:DB]), reads=[vk], writes=[ks])
                P.dve(lambda e: e.bn_aggr(sm[:, 12:14], sm[:, 0:6]), reads=[ks], writes=[ks])
                P.act(actf(sm[:, 14:15], sm[:, 13:14], AF.Sqrt, bias=epsb[:, 0:1]), reads=[ks, "epsb"], writes=[ks])
                yield
                P.dve(lambda e: e.reciprocal(sm[:, 14:15], sm[:, 14:15]), reads=[ks], writes=[ks])
                P.dve(stt(sm[:, 15:16], sm[:, 12:13], -1.0, sm[:, 14:15], ALU.mult, ALU.mult), reads=[ks], writes=[ks])
                q = rot_v32.next()
                kv = ("scr", "B", "vn32", q)
                P.act(actf(vn32[q][:, :], vt[:, 0:DB], AF.Identity, bias=sm[:, 15:16], scale=sm[:, 14:15]),
                      reads=[vk, ks], writes=[kv])
                yield
                P.dve(tt(vn32[q][:, :], vn32[q][:, :], nbg[:, :], ALU.mult), reads=[kv, "nbg"], writes=[kv])
                kvb = ("scr", "B", "vnbf", bi)
                if is_s:
                    P.dve(tt(vn32[q][:, :], vn32[q][:, :], nbb[:, :], ALU.add), reads=[kv, "nbb"], writes=[kv])
                    P.dma("sp", st_v_s[l], vn32[q][:, :], reads=[kv], semkey=kv)
                    P.act(actf(vnbf[:, bi, :], vn32[q][:, :], AF.Copy), reads=[kv], writes=[kvb])
                else:
                    P.dve(tt(vnbf[:, bi, :], vn32[q][:, :], nbb[:, :], ALU.add), reads=[kv, "nbb"], writes=[kvb])
                yield
                emit_s(bi, kvb)

            mbp = Pipe()
            assert nblk <= 3
            for bi in range(nblk):
                mbp.push(mb_chain(bi))
            for c in range(3):
                ut, uk = sbank[c]
                for kc in range(8):
                    P.pe(mm(ut[:, 0:n], wu[:, kc, c * 128:(c + 1) * 128], xT[:, kc, c0:c0 + n], kc == 0, kc == 7),
                         reads=[wuk] + xkeys, writes=[uk])
                P.act(actf(uT[:, c, 0:n], ut[:, 0:n], AF.Copy), reads=[uk], writes=[("scr", "B", "uT", c)])
            mbp.drain()
            bs_sb, bs_k = (bsbs, "bsbs") if is_s else (bsb, "bsb")
            for c in range(3):
                stt_, sk = sbank[c]
                ti = c % 2
                ktb = ("scr", "B", "tb", ti)
                bsrc = bs_sb[:, c, :]
                bb = bass.AP(bs_sb, bsrc.offset, [list(bsrc.ap[0]), [0, nblk], [1, 128]])
                P.dve(tt(tb[ti][:, 0:n].rearrange("p (b i) -> p b i", b=nblk),
                         stt_[:, 0:n].rearrange("p (b i) -> p b i", b=nblk), bb, ALU.add),
                      reads=[sk, bs_k], writes=[ktb])
                P.dve(tt(hT[:, 3 + c, c0:c0 + n], tb[ti][:, 0:n], uT[:, c, 0:n], ALU.mult),
                      reads=[ktb, ("scr", "B", "uT", c)], writes=[("hT", 3 + c, b0 + i) for i in range(nblk)])
        w_release(2)
        fence()
        sa = ScrAlloc()
        o_g = [sa.take(NMAX) for _ in range(2)]
        NCX = NMAX + 8
        o_cx = sa.take(2 * NCX)
        o_gb = sa.take(2 * NMAX)
        o_ca = sa.take(2 * NCX)
        gct = [scr[:, o:o + NMAX] for o in o_g]
        cxext = scr[:, o_cx:o_cx + 2 * NCX].rearrange("p (c k) -> p c k", c=2)
        gbt = scr[:, o_gb:o_gb + 2 * NMAX].rearrange("p (c k) -> p c k", c=2)
        cacc = scr[:, o_ca:o_ca + 2 * NCX].rearrange("p (c k) -> p c k", c=2)
        wc0, wc0k = w_acquire("mix")
        wc1, wc1k = w_acquire("mix")
        rot_s6 = Rot(6)
        for st in subs:
            c0, n, b0, nblk = st["c0"], st["n"], st["b0"], st["nblk"]
            need_xT(range(b0, b0 + nblk))
            xkeys = [("xT", b0 + i) for i in range(nblk)]
            segs, nx = seg_layout(st, 2)
            nco = nx - 2
            kext = ("scr", "C", "cxext")
            hist_fill_and_state(st, l, cxext, 2, 2, segs, hist_c, cache_c, (stC, "stC"), None, st_c_p, st_c_s, "hist_c", kext)
            for c in range(2):
                gi = rot_s6.next()
                gt, gk = single(gi)
                for kc in range(8):
                    P.pe(mm(gt[:, 0:n], wc1[:, kc, c * 128:(c + 1) * 128], xT[:, kc, c0:c0 + n], kc == 0, kc == 7),
                         reads=[wc1k] + xkeys, writes=[gk])
                kg = ("scr", "C", "gct", c)
                P.act(actf(gct[c][:, 0:n], gt[:, 0:n], AF.Copy), reads=[gk], writes=[kg])
                xi = rot_s6.next()
                xt_, xk = single(xi)
                for kc in range(8):
                    P.pe(mm(xt_[:, 0:n], wc0[:, kc, c * 128:(c + 1) * 128], xT[:, kc, c0:c0 + n], kc == 0, kc == 7),
                         reads=[wc0k] + xkeys, writes=[xk])
                for (eo, to, ntok) in segs:
                    P.dve(tt(cxext[:, c, eo + 2:eo + 2 + ntok], xt_[:, to:to + ntok], gct[c][:, to:to + ntok], ALU.mult),
                          reads=[xk, kg], writes=[kext + ("d",)])
                bi_ = rot_s6.next()
                bt, bk = single(bi_)
                for kc in range(8):
                    P.pe(mm(bt[:, 0:n], wc0[:, kc, 256 + c * 128:256 + (c + 1) * 128], xT[:, kc, c0:c0 + n],
                            kc == 0, kc == 7), reads=[wc0k] + xkeys, writes=[bk])
                P.act(actf(gbt[:, c, 0:n], bt[:, 0:n], AF.Copy), reads=[bk], writes=[("scr", "C", "gb", c)])
            state_out(st, l, cxext, 2, 2, segs, hist_c, (stD, "stD"), st_c_p, st_c_s, "hist_c", kext)
            for c in range(2):
                kca = ("scr", "C", "cacc", c)
                rk = [kext + ("h",), kext + ("d",), "cwc"]
                P.dve(ts(cacc[:, c, 0:nco], cxext[:, c, 0:nco], cwc[:, c, 0:1], None, ALU.mult), reads=rk, writes=[kca])
                for k in range(1, KC):
                    P.dve(stt(cacc[:, c, 0:nco], cxext[:, c, k:k + nco], cwc[:, c, k:k + 1], cacc[:, c, 0:nco],
                              ALU.mult, ALU.add), reads=rk + [kca], writes=[kca])
                for (eo, to, ntok) in segs:
                    P.dve(tt(hT[:, 6 + c, c0 + to:c0 + to + ntok], cacc[:, c, eo:eo + ntok], gbt[:, c, to:to + ntok],
                             ALU.mult), reads=[kca, ("scr", "C", "gb", c)],
                          writes=[("hT", 6 + c, b0 + i) for i in range(nblk)])
        w_release(2)
        wo = [w_acquire("mo") for _ in range(2)]
        ln_params_ready()
        for b in range(nb):
            pi = rot_pair.next()
            pt, pk = pair(pi)
            for hc in range(2):
                wv, wk = wo[hc]
                for kc in range(8):
                    P.pe(mm(pt[:, hc * 512:(hc + 1) * 512], hT[:, kc, b * 128:(b + 1) * 128], wv[:, kc, :],
                            kc == 0, kc == 7), reads=[wk, ("hT", kc, b)], writes=[pk[hc]])
            sm, ks = ln_slot()
            P.dve(lambda e, b=b, pt=pt, sm=sm: e.scalar_tensor_tensor(
                xres[:, b, :], pt[:, :], 1.0 / ALPHA, xres[:, b, :], ALU.mult, ALU.add, accum_out=sm[:, 0:1]),
                reads=[pk[0], pk[1], ("xres", b)], writes=[("xres", b), ks])
            LNP.push(ln_chain(b, epsp, True, None, sm, ks))
        w_release(2)

    for tile in cfg["tiles"]:
        nb = tile["nb"]
        for b in range(nb):
            gb = tile["gb0"] + b
            P.dma("sp", xres[:, b, :], xin[gb * 128:(gb + 1) * 128, :], writes=[("xres", b)], semkey=("xres", b))
        for b in range(nb):
            cast_x(b)
            flush_xT(1)
        flush_xT(0)
        for l in range(L):
            ffn(tile, l, 1, last=False)
            mixer(tile, l)
            ffn(tile, l, 2, last=(l == L - 1))
    assert WS.acquired == len(units) and WS.released == len(units)

    P.emit(nc, es)
    es.close()
    return nc, P


WEIGHT_NAMES = ["w_ffn1_in", "w_ffn1_out", "ln1_g", "ln1_b", "w_in", "conv_a_w", "conv_a_b", "norm_a_g",
                "norm_a_b", "norm_b_g", "norm_b_b", "w_s", "b_s", "conv_c_w", "w_out", "ln2_g", "ln2_b",
                "w_ffn2_in", "w_ffn2_out", "ln3_g", "ln3_b"]


def const_inputs():
    ident = np.eye(128, dtype=np.float32)
    j = np.arange(128)[:, None]
    i = np.arange(128)[None, :]
    maskT = ((j // 64) <= (i // 64)).astype(np.float32)
    return ident, maskT


def full_cfg():
    P = lambda n, **kw: dict(kind="P", nblk=n, **kw)
    tiles = [
        [P(3), P(3), P(3)],
        [P(3), P(3), P(3)],
        [P(3), P(3), P(3)],
        [P(3), P(3), P(1, final=True), dict(kind="S")],
    ]
    return make_cfg(4, tiles, ring=7)


NPB = 34
SPLIT = NPB * 128


_CACHE = {}


def kernel(**inputs):
    cfg = full_cfg()
    if "nc" not in _CACHE:
        _CACHE["nc"] = build_program(cfg)[0]
    nc = _CACHE["nc"]
    x_prompt = np.asarray(inputs["x_prompt"], dtype=np.float32)
    x_sample = np.asarray(inputs["x_sample"], dtype=np.float32)
    ca = np.asarray(inputs["cache_conv_a"], dtype=np.float32)
    cc = np.asarray(inputs["cache_conv_c"], dtype=np.float32)
    ident, maskT = const_inputs()
    wts = {k: np.ascontiguousarray(np.asarray(inputs[k], dtype=np.float32)) for k in WEIGHT_NAMES}
    START1 = 8192 - SPLIT
    assert SPLIT - START1 >= 384 and START1 % 128 == 0
    in_maps = []
    for c in range(8):
        seq, half = c // 2, c % 2
        xp = x_prompt[seq, 0:SPLIT] if half == 0 else x_prompt[seq, START1:8192]
        xs = x_sample[2 * c:2 * c + 2].reshape(128, D)
        m = dict(wts)
        m["xin"] = np.ascontiguousarray(np.concatenate([xp, xs], axis=0))
        m["cache_a"] = np.ascontiguousarray(ca[:, 2 * c:2 * c + 2])
        m["cache_c"] = np.ascontiguousarray(cc[:, 2 * c:2 * c + 2])
        m["maskin"] = np.ones((128, 1), np.float32)
        m["maskT"] = maskT
        m["identin"] = ident
        in_maps.append(m)
    res = run_bass_kernel_spmd(nc, in_maps, core_ids=list(range(8)))
    r = res.results
    y_prompt = np.empty((4, 8192, D), np.float32)
    y_sample = np.empty((16, 64, D), np.float32)
    st_a_p = np.empty((4, 4, 30, DA), np.float32)
    st_c_p = np.empty((4, 4, 2, DC), np.float32)
    st_a_s = np.empty((4, 16, 30, DA), np.float32)
    st_c_s = np.empty((4, 16, 2, DC), np.float32)
    st_v_s = np.empty((4, 16, 64, DB), np.float32)
    for c in range(8):
        seq, half = c // 2, c % 2
        yo = r[c]["yout"]
        if half == 0:
            y_prompt[seq, 0:SPLIT] = yo[0:SPLIT]
        else:
            y_prompt[seq, SPLIT:8192] = yo[SPLIT - START1:SPLIT]
        y_sample[2 * c:2 * c + 2] = yo[SPLIT:SPLIT + 128].reshape(2, 64, D)
        if half == 1:
            st_a_p[:, seq] = r[c]["st_a_p"]
            st_c_p[:, seq] = r[c]["st_c_p"]
        st_a_s[:, 2 * c:2 * c + 2] = r[c]["st_a_s"]
        st_c_s[:, 2 * c:2 * c + 2] = r[c]["st_c_s"]
        st_v_s[:, 2 * c:2 * c + 2] = r[c]["st_v_s"].reshape(4, 2, 64, DB)
    return (y_prompt, y_sample, st_a_p, st_c_p, st_a_s, st_c_s, st_v_s)
```

```python
import math
from contextlib import ExitStack

import numpy as np
import concourse.bass as bass
import concourse.mybir as mybir
from concourse.bass_utils import run_bass_kernel_spmd

F32 = mybir.dt.float32
BF16 = mybir.dt.bfloat16
AF = mybir.ActivationFunctionType
ALU = mybir.AluOpType

D = 1024
DFF = 2816
DA = 384
DB = 384
DC = 256
DIN = 2304
NFC = 22
NFH = 11
ALPHA = 8.0 ** 0.25
EPS = 1e-5
KA = 31
KC = 3

ENGS = ("pe", "act", "dve", "pool", "sp")
EPOCH = 20000
XT_DEFER = 2
PA_HOLD = 3


class Op:
    __slots__ = ("eng", "idx", "fn", "deps", "is_dma", "semkey", "dcount", "signal", "seq")


class Prog:
    def __init__(self):
        self.ops = {e: [] for e in ENGS}
        self.lastw = {}
        self.readers = {}
        self.dma_counts = {}

    def _add(self, eng, fn, reads, writes, is_dma=False, semkey=None):
        op = Op()
        op.eng = eng
        op.idx = len(self.ops[eng])
        op.fn = fn
        op.is_dma = is_dma
        op.semkey = semkey
        op.signal = False
        op.seq = 0
        op.dcount = 0
        if any(isinstance(k, tuple) and k[0] == "scr" for k in list(reads) + list(writes)):
            reads = list(reads) + ["scrfence"]
        deps = {}
        for k in reads:
            w = self.lastw.get(k)
            if w is not None:
                deps[id(w)] = (w, "raw")
        for k in writes:
            w = self.lastw.get(k)
            if w is not None and id(w) not in deps:
                deps[id(w)] = (w, "waw")
            rd = self.readers.get(k)
            if rd is not None:
                for r in rd[0].values():
                    if id(r) not in deps:
                        deps[id(r)] = (r, "war")
                for r in rd[1]:
                    if id(r) not in deps:
                        deps[id(r)] = (r, "war")
        final = []
        for p, kind in deps.values():
            if p.is_dma:
                need = True
            elif p.eng != eng:
                need = True
            elif is_dma:
                need = True
            else:
                need = eng != "pe"
            if need:
                if not p.is_dma:
                    p.signal = True
                final.append(p)
        op.deps = final
        for k in reads:
            rd = self.readers.get(k)
            if rd is None:
                rd = ({}, [])
                self.readers[k] = rd
            if is_dma:
                rd[1].append(op)
            else:
                rd[0][eng] = op
        for k in writes:
            self.lastw[k] = op
            self.readers[k] = ({}, [])
        if is_dma:
            c = self.dma_counts.get(semkey, 0) + 1
            self.dma_counts[semkey] = c
            op.dcount = c
        self.ops[eng].append(op)
        return op

    def pe(self, fn, reads=(), writes=()):
        return self._add("pe", fn, reads, writes)

    def act(self, fn, reads=(), writes=()):
        return self._add("act", fn, reads, writes)

    def dve(self, fn, reads=(), writes=()):
        return self._add("dve", fn, reads, writes)

    def pool(self, fn, reads=(), writes=()):
        return self._add("pool", fn, reads, writes)

    def dma(self, eng, out_ap, in_ap, reads=(), writes=(), semkey=None):
        def fn(e, out_ap=out_ap, in_ap=in_ap):
            return e.dma_start(out=out_ap, in_=in_ap)
        return self._add(eng, fn, reads, writes, is_dma=True, semkey=semkey)

    def emit(self, nc, es):
        nsig = {}
        for eng in ENGS:
            cnt = 0
            for op in self.ops[eng]:
                if op.signal and not op.is_dma:
                    cnt += 1
                    op.seq = cnt
            nsig[eng] = cnt
        eng_sems = {}
        for eng in ENGS:
            n = max(1, math.ceil(nsig[eng] / EPOCH))
            eng_sems[eng] = [es.enter_context(nc.semaphore(f"s_{eng}_{i}")) for i in range(n)]
        dma_sems = {}
        for i, k in enumerate(self.dma_counts):
            dma_sems[k] = es.enter_context(nc.semaphore(f"d_{i}"))

        def resolve(p):
            if p.is_dma:
                return dma_sems[p.semkey], 16 * p.dcount
            ep = (p.seq - 1) // EPOCH
            return eng_sems[p.eng][ep], p.seq - ep * EPOCH

        def run(eng, e):
            waited = {}
            for op in self.ops[eng]:
                for p in op.deps:
                    sem, val = resolve(p)
                    if waited.get(id(sem), 0) < val:
                        e.wait_ge(sem, val)
                        waited[id(sem)] = val
                ins = op.fn(e)
                if op.is_dma:
                    ins.then_inc(dma_sems[op.semkey], 16)
                elif op.signal:
                    ep = (op.seq - 1) // EPOCH
                    ins.then_inc(eng_sems[eng][ep], 1)
            if eng == "sp":
                for k, c in self.dma_counts.items():
                    e.wait_ge(dma_sems[k], 16 * c)

        with nc.Block() as block:
            @block.tensor
            def _(e):
                run("pe", e)

            @block.scalar
            def _(e):
                run("act", e)

            @block.vector
            def _(e):
                run("dve", e)

            @block.gpsimd
            def _(e):
                run("pool", e)

            @block.sync
            def _(e):
                run("sp", e)


def make_cfg(L, tiles, ring=7):
    nb_max = 0
    gb = 0
    out = []
    n_s = 0
    for t in tiles:
        tt = []
        b = 0
        for st in t:
            d = dict(kind=st["kind"], nblk=st.get("nblk", 1), mask_after=st.get("mask_after", False),
                     final=st.get("final", False), b0=b)
            if d["kind"] == "S":
                d["nblk"] = 1
                n_s += 1
            d["n"] = d["nblk"] * 128
            d["c0"] = b * 128
            b += d["nblk"]
            tt.append(d)
        out.append(dict(subs=tt, nb=b, gb0=gb))
        gb += b
        nb_max = max(nb_max, b)
    nmax = max(st["n"] for t in out for st in t["subs"])
    assert n_s <= 1
    return dict(L=L, tiles=out, NB=nb_max, NBLK=gb, RING=ring, NMAX=nmax, has_s=(n_s == 1))


def build_program(cfg):
    L = cfg["L"]
    NB = cfg["NB"]
    NBLK = cfg["NBLK"]
    RING = cfg["RING"]
    NMAX = cfg["NMAX"]
    NT = NB * 128
    nc = bass.Bass("TRN2", target_bir_lowering=False)
    P = Prog()
    es = ExitStack()

    def din(name, shape):
        return nc.dram_tensor(name, list(shape), F32, kind="ExternalInput").ap()

    def dout(name, shape):
        return nc.dram_tensor(name, list(shape), F32, kind="ExternalOutput").ap()

    xin = din("xin", [NBLK * 128, D])
    cache_a = din("cache_a", [L, 2, 30, DA])
    cache_c = din("cache_c", [L, 2, 2, DC])
    maskin = din("maskin", [128, 1])
    maskT_in = din("maskT", [128, 128])
    W = {}
    for nm, shp in (("w_ffn1_in", [L, D, 2 * DFF]), ("w_ffn1_out", [L, DFF, D]), ("ln1_g", [L, D]),
                    ("ln1_b", [L, D]), ("w_in", [L, D, DIN]), ("conv_a_w", [L, KA, DA]),
                    ("conv_a_b", [L, DA]), ("norm_a_g", [L, DA]), ("norm_a_b", [L, DA]),
                    ("norm_b_g", [L, DB]), ("norm_b_b", [L, DB]), ("w_s", [L, 6, 128, 128]),
                    ("b_s", [L, 6, 128]), ("conv_c_w", [L, KC, DC]), ("w_out", [L, D, D]),
                    ("ln2_g", [L, D]), ("ln2_b", [L, D]), ("w_ffn2_in", [L, D, 2 * DFF]),
                    ("w_ffn2_out", [L, DFF, D]), ("ln3_g", [L, D]), ("ln3_b", [L, D])):
        W[nm] = din(nm, shp)
    yout = dout("yout", [NBLK * 128, D])
    st_a_p = dout("st_a_p", [L, 30, DA])
    st_c_p = dout("st_c_p", [L, 2, DC])
    st_a_s = dout("st_a_s", [L, 2, 30, DA])
    st_c_s = dout("st_c_s", [L, 2, 2, DC])
    st_v_s = dout("st_v_s", [L, 128, DB])

    def sb(name, shape, dt=F32):
        return es.enter_context(nc.sbuf_tensor(name, list(shape), dt))

    def ps(name, shape, dt=F32):
        return es.enter_context(nc.psum_tensor(name, list(shape), dt))

    xres = sb("xres", [128, NB, D])
    xT = sb("xT", [128, 8, NT], BF16)
    hT = sb("hT", [128, NFH, NT], BF16)
    ring = sb("ring", [128, RING, 4096], BF16)
    NA = NMAX + 60
    SCR = max((3 * NA + 1) // 2 + 180 + 8 * NA + 8, 5 * NMAX + 2000, 8 * NMAX + 64)
    scr = sb("scr", [128, SCR])
    lng = sb("lng", [128, D])
    lnb = sb("lnb", [128, D])
    tmpA = [sb(f"tmpA{i}", [128, D]) for i in range(2)]
    xbf = [sb(f"xbf{i}", [128, D], BF16) for i in range(XT_DEFER + 1)]
    sgt = [sb(f"sgt{i}", [128, 512]) for i in range(2)]
    small = sb("small", [128, 8, 16])
    sqjunk = sb("sqjunk", [128, D], BF16)
    identB = sb("identB", [128, 128], BF16)
    identF = sb("identF", [128, 128])
    onesF = sb("onesF", [128, 128])
    maskT = sb("maskTs", [128, 128])
    maskc = sb("maskc", [128, 1])
    Wn = sb("Wn", [128, 6, 128])
    WT = sb("WT", [128, 6, 128], BF16)
    WTs = sb("WTs", [128, 6, 128], BF16)
    bsb = sb("bsb", [128, 3, 128])
    bsbs = sb("bsbs", [128, 3, 128])
    nbg = sb("nbg", [128, DB])
    nbb = sb("nbb", [128, DB])
    cwa = sb("cwa", [128, 3, KA])
    pva = sb("pva", [128, 3, 3])
    cwc = sb("cwc", [128, 2, KC])
    hist_a = sb("hist_a", [128, L, 3, 30])
    hist_c = sb("hist_c", [128, L, 2, 2])
    stA = sb("stA", [32, DA])
    stB = sb("stB", [32, DA])
    stC = sb("stC", [4, DC])
    stD = sb("stD", [4, DC])
    Iq = sb("Iq", [128, 32])
    dqt = sb("dq", [128, 3, KA, 32], BF16)
    dq = dqt[:, :, :, :]
    dq_keys = ["dq"]
    dummy = sb("dummyt", [128, 2])
    epsb = sb("epsb", [128, 2])

    acc = [ps(f"acc{i}", [128, 1024]) for i in range(3)]
    trb = [ps(f"trb{i}", [128, 512]) for i in range(2)]

    class Rot:
        def __init__(self, n):
            self.n = n
            self.i = 0

        def next(self):
            i = self.i
            self.i = (i + 1) % self.n
            return i

    rot_pair = Rot(3)
    rot_tr = Rot(2)
    rot_tmpA = Rot(2)
    rot_xbf = Rot(XT_DEFER + 1)
    rot_sgt = Rot(2)
    rot_small = Rot(8)

    def single(i):
        return acc[i // 2][:, (i % 2) * 512:(i % 2 + 1) * 512], ("ps", i)

    def pair(i):
        return acc[i], [("ps", 2 * i), ("ps", 2 * i + 1)]

    def trbank(i):
        return trb[i], ("ps", 6 + i)

    def bcast_rows(ap2d_row, nparts):
        n = ap2d_row.shape[-1]
        return bass.AP(ap2d_row.tensor, ap2d_row.offset, [[0, nparts], [1, n]])

    def mm(out, lhsT, rhs, start, stop):
        return lambda e: e.matmul(out, lhsT, rhs, start=start, stop=stop)

    def tp(out, in_, ident):
        return lambda e: e.transpose(out, in_, ident)

    def actf(out, in_, func, bias=None, scale=None):
        kw = {}
        if bias is not None:
            kw["bias"] = bias
        if scale is not None:
            kw["scale"] = scale
        return lambda e: e.activation(out, in_, func, **kw)

    def tt(out, in0, in1, op):
        return lambda e: e.tensor_tensor(out, in0, in1, op)

    def ts(out, in0, s1, s2, op0, op1=None):
        if op1 is None:
            return lambda e: e.tensor_scalar(out, in0, s1, None, op0)
        return lambda e: e.tensor_scalar(out, in0, s1, s2, op0, op1)

    def stt(out, in0, scalar, in1, op0, op1):
        return lambda e: e.scalar_tensor_tensor(out, in0, scalar, in1, op0, op1)

    def cp(out, in_):
        return lambda e: e.tensor_copy(out, in_)

    def mset(ap, v):
        return lambda e: e.memset(ap, v)

    identin = din("identin", [128, 128])
    P.dma("sp", identF[:, :], identin, writes=["identF"], semkey="identF")
    P.dma("sp", maskT[:, :], maskT_in, writes=["maskT"], semkey="maskT")
    P.dma("sp", maskc[:, :], maskin, writes=["maskc"], semkey="maskc")
    P.act(actf(identB[:, :], identF[:, :], AF.Copy), reads=["identF"], writes=["identB"])
    for q in range(4):
        P.dve(cp(Iq[32 * q:32 * q + 32, :], identF[32 * q:32 * q + 32, 32 * q:32 * q + 32]), reads=["identF"],
              writes=["Iq"])
    P.dve(mset(onesF[:, :], 1.0), writes=["onesF"])
    P.dve(mset(epsb[:, 0:1], EPS), writes=["epsb"])
    P.dve(mset(epsb[:, 1:2], EPS / (ALPHA * ALPHA)), writes=["epsb"])
    P.dve(mset(hist_a[:, :, :, :].rearrange("p a b c -> p (a b c)"), 0.0), writes=[("hist_a", l) for l in range(L)])
    P.dve(mset(hist_c[:, :, :, :].rearrange("p a b c -> p (a b c)"), 0.0), writes=[("hist_c", l) for l in range(L)])
    P.dve(mset(dummy[:, :], 0.0), writes=["scrfence"])

    units = []

    def ffn_units(l, which):
        wi = W["w_ffn%d_in" % which]
        wo = W["w_ffn%d_out" % which]
        for half in range(2):
            j = half * NFH
            while j < (half + 1) * NFH:
                nch = min(2, (half + 1) * NFH - j)
                units.append(("fin", wi, l, j, nch))
                j += nch
            k = 0
            while k < NFH:
                nk = min(4, NFH - k)
                units.append(("fout", wo, l, half * NFH + k, nk))
                k += nk

    MIXCOLS = [(0, 384), (384, 768), (768, 1152), (1152, 1536), (1536, 2048), (2048, 2304)]

    def mixer_units(l):
        for c0, c1 in MIXCOLS:
            units.append(("mix", W["w_in"], l, c0, c1))
        for h in range(2):
            units.append(("mo", W["w_out"], l, h))

    for _t in cfg["tiles"]:
        for l in range(L):
            ffn_units(l, 1)
            mixer_units(l)
            ffn_units(l, 2)

    class WS:
        issued = 0
        acquired = 0
        released = 0

    def slot_view(s, kind):
        flat = ring[:, s, :]
        if kind == "fin":
            return flat.rearrange("p (k t f) -> p k t f", k=8, t=2)
        if kind == "fout":
            return flat.rearrange("p (k m) -> p k m", k=4)
        return flat.rearrange("p (k c) -> p k c", k=8)

    def w_issue(u):
        s = u % RING
        d = units[u]
        kind = d[0]
        v = slot_view(s, kind)
        if kind == "fin":
            _, w, l, j, nch = d
            wsrc = w[l].rearrange("(kc p) (two f) -> p kc two f", p=128, two=2)
            for part in range(2):
                P.dma("pool", v[:, :, part, 0:nch * 128], wsrc[:, :, part, j * 128:(j + nch) * 128],
                      writes=[("ring", s)], semkey=("ring", s))
            return
        elif kind == "fout":
            _, w, l, k0, nk = d
            src = w[l].rearrange("(kc p) m -> p kc m", p=128)[:, k0:k0 + nk, :]
            dst = v[:, 0:nk, :]
        elif kind == "mix":
            _, w, l, c0, c1 = d
            src = w[l].rearrange("(kc p) c -> p kc c", p=128)[:, :, c0:c1]
            dst = v[:, :, 0:c1 - c0]
        else:
            _, w, l, h = d
            src = w[l].rearrange("(kc p) c -> p kc c", p=128)[:, :, h * 512:(h + 1) * 512]
            dst = v[:, :, :]
        P.dma("pool", dst, src, writes=[("ring", s)], semkey=("ring", s))

    def w_prefetch():
        while WS.issued < len(units) and WS.issued < WS.released + RING:
            w_issue(WS.issued)
            WS.issued += 1

    def w_acquire(kind):
        u = WS.acquired
        assert u < WS.issued, "weight ring too small"
        assert units[u][0] == kind, (units[u][0], kind)
        WS.acquired += 1
        s = u % RING
        return slot_view(s, kind), ("ring", s)

    def w_release(n=1):
        WS.released += n
        w_prefetch()

    w_prefetch()

    pending = []

    def cast_x(b):
        i = rot_xbf.next()
        assert all(pi != i for _, pi in pending)
        P.act(actf(xbf[i][:, :], xres[:, b, :], AF.Copy), reads=[("xres", b)], writes=[("xbf", i)])
        pending.append((b, i))

    def flush_xT(keep=0):
        while len(pending) > keep:
            b, i = pending.pop(0)
            finish_xT(b, i)

    chain_live = set()

    def need_xT(blocks):
        if any(b in chain_live for b in blocks):
            LNP.drain()
        if any(pb in blocks for pb, _ in pending):
            flush_xT(0)

    def finish_xT(b, i):
        j = rot_tr.next()
        trt, trk = trbank(j)
        trv = trt[:, :].bitcast(BF16).rearrange("p (a b) -> p a b", a=8)
        for kc in range(8):
            P.pe(tp(trv[:, kc, :], xbf[i][:, kc * 128:(kc + 1) * 128], identB[:, :]),
                 reads=[("xbf", i), "identB"], writes=[trk])
        P.act(actf(xT[:, :, b * 128:(b + 1) * 128], trv, AF.Copy), reads=[trk], writes=[("xT", b)])

    def make_xT(b):
        cast_x(b)
        flush_xT(0)

    class Pipe:
        def __init__(self):
            self.live = []

        def push(self, gen):
            self.live.insert(0, gen)
            self._advance()

        def _advance(self):
            nxt = []
            for g in self.live:
                try:
                    next(g)
                    nxt.append(g)
                except StopIteration:
                    pass
            self.live = nxt

        def drain(self):
            while self.live:
                self._advance()

    LNP = Pipe()
    LN_NEXT = [None]

    def ln_tick():
        if LNP.live:
            LNP._advance()
        if LN_NEXT[0] is not None and not LNP.live:
            load_ln(*LN_NEXT[0])
            LN_NEXT[0] = None

    def ln_params_ready():
        if LN_NEXT[0] is not None:
            LNP.drain()
            load_ln(*LN_NEXT[0])
            LN_NEXT[0] = None

    def ln_slot():
        i = rot_small.next()
        return small[:, i, :], ("small", i)

    def ln_chain(b, epsp, need_xT_, out_gb, sm, ks):
        kx = ("xres", b)
        chain_live.add(b)
        P.act(lambda e: e.activation(sqjunk[:, :], xres[:, b, :], AF.Square, scale=1.0 / 32.0,
                                     accum_out=sm[:, 1:2]), reads=[kx], writes=["sqjunk", ks])
        yield
        P.dve(tt(sm[:, 2:3], sm[:, 0:1], sm[:, 0:1], ALU.mult), reads=[ks], writes=[ks])
        P.dve(stt(sm[:, 13:14], sm[:, 2:3], -1.0 / (1024.0 * 1024.0), sm[:, 1:2], ALU.mult, ALU.add),
              reads=[ks], writes=[ks])
        eb = epsb[:, 0:1] if epsp == EPS else epsb[:, 1:2]
        P.act(actf(sm[:, 14:15], sm[:, 13:14], AF.Sqrt, bias=eb), reads=[ks, "epsb"], writes=[ks])
        yield
        P.dve(lambda e: e.reciprocal(sm[:, 14:15], sm[:, 14:15]), reads=[ks], writes=[ks])
        P.dve(stt(sm[:, 15:16], sm[:, 0:1], -1.0 / 1024.0, sm[:, 14:15], ALU.mult, ALU.mult), reads=[ks], writes=[ks])
        t = rot_tmpA.next()
        kt = ("tmpA", t)
        P.act(actf(tmpA[t][:, :], xres[:, b, :], AF.Identity, bias=sm[:, 15:16], scale=sm[:, 14:15]),
              reads=[kx, ks], writes=[kt])
        yield
        P.dve(tt(tmpA[t][:, :], tmpA[t][:, :], lng[:, :], ALU.mult), reads=[kt, "lng"], writes=[kt])
        P.dve(tt(xres[:, b, :], tmpA[t][:, :], lnb[:, :], ALU.add), reads=[kt, "lnb"], writes=[kx])
        chain_live.discard(b)
        if need_xT_:
            cast_x(b)
            flush_xT(XT_DEFER)
        if out_gb is not None:
            P.dma("sp", yout[out_gb * 128:(out_gb + 1) * 128, :], xres[:, b, :], reads=[kx], semkey=kx)

    def load_ln(gname, bname, l):
        P.dma("sp", lng[:, :], bcast_rows(W[gname][l], 128), writes=["lng"], semkey="lng")
        P.dma("sp", lnb[:, :], bcast_rows(W[bname][l], 128), writes=["lnb"], semkey="lnb")

    def ffn(tile, l, which, last):
        subs = tile["subs"]
        nb = tile["nb"]
        cscale = 0.5 / ALPHA
        epsp = EPS / (ALPHA * ALPHA)
        LN_NEXT[0] = ("ln1_g" if which == 1 else "ln3_g", "ln1_b" if which == 1 else "ln3_b", l)
        if not LNP.live:
            ln_tick()
        if which == 1:
            issue_param_dmas(tile, l)
        for half in range(2):
            if which == 1 and half == 1:
                finish_params(tile, l)
            def phase_a(wv, wk, jl, nch, sts):
                for st in sts:
                    c0, n, b0, nblk = st["c0"], st["n"], st["b0"], st["nblk"]
                    need_xT(range(b0, b0 + nblk))
                    xkeys = [("xT", b0 + i) for i in range(nblk)]
                    for ci in range(nch):
                        pi = rot_pair.next()
                        pt, pk = pair(pi)
                        for part in range(2):
                            for kc in range(8):
                                P.pe(mm(pt[:, part * 512:part * 512 + n], wv[:, kc, part, ci * 128:(ci + 1) * 128],
                                        xT[:, kc, c0:c0 + n], kc == 0, kc == 7),
                                     reads=[wk] + xkeys, writes=[pk[part]])
                        si = rot_sgt.next()
                        P.act(actf(sgt[si][:, 0:n], pt[:, 0:n], AF.Silu), reads=[pk[0]], writes=[("sgt", si)])
                        P.dve(tt(hT[:, jl + ci, c0:c0 + n], sgt[si][:, 0:n], pt[:, 512:512 + n], ALU.mult),
                              reads=[("sgt", si), pk[1]], writes=[("hT", jl + ci, b0 + i) for i in range(nblk)])
                    ln_tick()

            ulist = []
            jl = 0
            while jl < NFH:
                nch = min(2, NFH - jl)
                ulist.append((jl, nch))
                jl += nch
            ui = 0
            if half == 0 and LNP.live and len(subs) > 1:
                held = []
                for (jl, nch) in ulist[:PA_HOLD]:
                    wv, wk = w_acquire("fin")
                    held.append((wv, wk, jl, nch))
                    phase_a(wv, wk, jl, nch, subs[:-1])
                for (wv, wk, jl, nch) in held:
                    phase_a(wv, wk, jl, nch, subs[-1:])
                w_release(len(held))
                ui = len(held)
            for (jl, nch) in ulist[ui:]:
                wv, wk = w_acquire("fin")
                phase_a(wv, wk, jl, nch, subs)
                w_release()
            wviews = []
            k = 0
            while k < NFH:
                nk = min(4, NFH - k)
                wviews.append(w_acquire("fout"))
                k += nk
            if half == 1:
                ln_params_ready()
            for b in range(nb):
                pi = rot_pair.next()
                pt, pk = pair(pi)
                for hc in range(2):
                    for k in range(NFH):
                        wv, wk = wviews[k // 4]
                        P.pe(mm(pt[:, hc * 512:(hc + 1) * 512], hT[:, k, b * 128:(b + 1) * 128],
                                wv[:, k % 4, hc * 512:(hc + 1) * 512], k == 0, k == NFH - 1),
                             reads=[wk, ("hT", k, b)], writes=[pk[hc]])
                if half == 0:
                    P.dve(stt(xres[:, b, :], pt[:, :], cscale, xres[:, b, :], ALU.mult, ALU.add),
                          reads=[pk[0], pk[1], ("xres", b)], writes=[("xres", b)])
                else:
                    sm, ks = ln_slot()
                    P.dve(lambda e, b=b, pt=pt, sm=sm: e.scalar_tensor_tensor(
                        xres[:, b, :], pt[:, :], cscale, xres[:, b, :], ALU.mult, ALU.add, accum_out=sm[:, 0:1]),
                        reads=[pk[0], pk[1], ("xres", b)], writes=[("xres", b), ks])
                    LNP.push(ln_chain(b, epsp, not last, (tile["gb0"] + b) if last else None, sm, ks))
            if last:
                LNP.drain()
            w_release(len(wviews))

    class ScrAlloc:
        def __init__(self):
            self.off = 0

        def take(self, n):
            o = self.off
            self.off += n
            assert self.off <= SCR, (self.off, SCR)
            return o

    def fence():
        P.dve(mset(dummy[:, 0:1], 0.0), writes=["scrfence"])

    def seg_layout(st, hl):
        if st["kind"] == "S":
            return [(0, 0, 64), (hl + 64, 64, 64)], 2 * (hl + 64)
        return [(0, 0, st["n"])], hl + st["n"]

    def issue_param_dmas(tile, l):
        has_p = any(s["kind"] == "P" for s in tile["subs"])
        has_s = any(s["kind"] == "S" for s in tile["subs"])

        def row(ap1d):
            n = ap1d.shape[-1]
            return bass.AP(ap1d.tensor, ap1d.offset, [[n, 1], [1, n]])
        P.dma("sp", stA[0:KA, :], W["conv_a_w"][l], writes=["stA"], semkey="stA")
        for r, nm in enumerate(("conv_a_b", "norm_a_g", "norm_a_b")):
            P.dma("sp", stB[r:r + 1, :], row(W[nm][l]), writes=["stB"], semkey="stB")
        P.dma("sp", stC[0:KC, :], W["conv_c_w"][l], writes=["stC"], semkey="stC")
        P.dma("sp", nbg[:, :], bcast_rows(W["norm_b_g"][l], 128), writes=["nbg"], semkey="nbg")
        P.dma("sp", nbb[:, :], bcast_rows(W["norm_b_b"][l], 128), writes=["nbb"], semkey="nbb")
        bs_t = W["b_s"].tensor
        if has_p:
            P.dma("sp", Wn[:, :, :], W["w_s"][l].rearrange("h i j -> i h j"), writes=["Wn"], semkey="Wn")
            for h2 in range(2):
                src_ = bass.AP(bs_t, W["b_s"][l].offset + h2 * 128, [[0, 64], [256, 3], [1, 128]])
                P.dma("sp", bsb[h2 * 64:(h2 + 1) * 64, :, :], src_, writes=["bsb"], semkey="bsb")
        if has_s:
            for h2 in range(2):
                src_ = bass.AP(bs_t, W["b_s"][l].offset + h2 * 128, [[0, 64], [256, 3], [1, 64]])
                for r in range(2):
                    P.dma("sp", bsbs[h2 * 64:(h2 + 1) * 64, :, r * 64:(r + 1) * 64], src_,
                          writes=["bsbs"], semkey="bsbs")

    def finish_params(tile, l):
        has_p = any(s["kind"] == "P" for s in tile["subs"])
        has_s = any(s["kind"] == "S" for s in tile["subs"])
        trt, trk = trbank(rot_tr.next())
        for c in range(3):
            P.pe(tp(trt[:, c * 32:c * 32 + KA], stA[0:KA, c * 128:(c + 1) * 128], identF[0:KA, 0:KA]),
                 reads=["stA", "identF"], writes=[trk])
        P.act(actf(cwa[:, :, :], trt[:, 0:96].rearrange("p (c k) -> p c k", c=3)[:, :, 0:KA], AF.Copy),
              reads=[trk], writes=["cwa"])
        trt, trk = trbank(rot_tr.next())
        for c in range(3):
            P.pe(tp(trt[:, c * 32:c * 32 + 3], stB[0:3, c * 128:(c + 1) * 128], identF[0:3, 0:3]),
                 reads=["stB", "identF"], writes=[trk])
        P.act(actf(pva[:, :, :], trt[:, 0:96].rearrange("p (c k) -> p c k", c=3)[:, :, 0:3], AF.Copy),
              reads=[trk], writes=["pva"])
        trt, trk = trbank(rot_tr.next())
        for c in range(2):
            P.pe(tp(trt[:, c * 32:c * 32 + KC], stC[0:KC, c * 128:(c + 1) * 128], identF[0:KC, 0:KC]),
                 reads=["stC", "identF"], writes=[trk])
        P.act(actf(cwc[:, :, :], trt[:, 0:64].rearrange("p (c k) -> p c k", c=2)[:, :, 0:KC], AF.Copy),
              reads=[trk], writes=["cwc"])
        for c in range(3):
            cw = cwa[:, c, :]
            i0 = bass.AP(Iq, 0, [list(Iq[:, :].ap[0]), [0, KA], [1, 32]])
            i1 = bass.AP(cwa, cw.offset, [list(cw.ap[0]), [1, KA], [0, 32]])
            P.dve(tt(dq[:, c, :, :], i0, i1, ALU.mult), reads=["Iq", "cwa"], writes=dq_keys)
        if has_p:
            for h in range(6):
                trt, trk = trbank(rot_tr.next())
                P.pe(tp(trt[:, 0:128], Wn[:, h, :], identF[:, :]), reads=["Wn", "identF"], writes=[trk])
                P.dve(tt(WT[:, h, :], trt[:, 0:128], maskT[:, :], ALU.mult), reads=[trk, "maskT"], writes=["WT"])
        if has_s:
            P.dve(mset(Wn[0:64, :, 64:128], 0.0), reads=["Wn"], writes=["Wn"])
            P.dve(mset(Wn[64:128, :, 0:64], 0.0), reads=["Wn"], writes=["Wn"])
            for q in range(2):
                P.dma("sp", Wn[q * 64:(q + 1) * 64, :, q * 64:(q + 1) * 64],
                      W["w_s"][l][:, 0:64, 0:64].rearrange("h i j -> i h j"), writes=["Wn"], semkey="Wn")
            for h in range(6):
                trt, trk = trbank(rot_tr.next())
                P.pe(tp(trt[:, 0:128], Wn[:, h, :], identF[:, :]), reads=["Wn", "identF"], writes=[trk])
                P.act(actf(WTs[:, h, :], trt[:, 0:128], AF.Copy), reads=[trk], writes=["WTs"])

    def hist_fill_and_state(st, l, ext, nch, hl, segs, hist, cache, cst, stg, out_p, out_s, kname, kext):
        if st["kind"] == "P":
            P.act(actf(ext[:, :, 0:hl], hist[:, l, :, :], AF.Copy), reads=[(kname, l)], writes=[kext + ("h",)])
        else:
            for s in range(2):
                cstt, cstk = cst
                P.dma("sp", cstt[0:hl, :], cache[l, s], writes=[cstk], semkey=cstk)
                j = rot_tr.next()
                trt, trk = trbank(j)
                for c in range(nch):
                    P.pe(tp(trt[:, c * 32:c * 32 + hl], cstt[0:hl, c * 128:(c + 1) * 128], identF[0:hl, 0:hl]),
                         reads=[cstk, "identF"], writes=[trk])
                eo = segs[s][0]
                P.act(actf(ext[:, :, eo:eo + hl], trt[:, 0:nch * 32].rearrange("p (c k) -> p c k", c=nch)[:, :, 0:hl],
                           AF.Copy), reads=[trk], writes=[kext + ("h",)])

    def state_out(st, l, ext, nch, hl, segs, hist, stg, out_p, out_s, kname, kext):
        rk = [kext + ("h",), kext + ("d",)]
        if st["kind"] == "P":
            eo, _, ntok = segs[0]
            src = ext[:, :, eo + ntok:eo + ntok + hl]
            if st["mask_after"]:
                P.act(actf(hist[:, l, :, :], src, AF.Copy, scale=maskc[:, 0:1]), reads=rk + ["maskc"],
                      writes=[(kname, l)])
            else:
                P.act(actf(hist[:, l, :, :], src, AF.Copy), reads=rk, writes=[(kname, l)])
            outs = [(src, out_p[l])] if st["final"] else []
        else:
            outs = []
            for s in range(2):
                eo, _, ntok = segs[s]
                outs.append((ext[:, :, eo + ntok:eo + ntok + hl], out_s[l, s]))
        for src, dst in outs:
            j = rot_tr.next()
            trt, trk = trbank(j)
            for c in range(nch):
                P.pe(tp(trt[0:hl, c * 128:(c + 1) * 128], src[:, c, :], identF[:, :]), reads=rk + ["identF"],
                     writes=[trk])
            stgt, stgk = stg
            P.act(actf(stgt[0:hl, :], trt[0:hl, 0:nch * 128], AF.Copy), reads=[trk], writes=[stgk])
            P.dma("sp", dst, stgt[0:hl, :], reads=[stgk], semkey=stgk)

    def mixer(tile, l):
        subs = tile["subs"]
        nb = tile["nb"]
        epsp = EPS / (ALPHA * ALPHA)
        LN_NEXT[0] = ("ln2_g", "ln2_b", l)
        if not LNP.live:
            ln_tick()
        fence()
        sa = ScrAlloc()
        o_aext = sa.take((3 * NA + 1) // 2)
        o_tail = sa.take(3 * 2 * 30)
        o_co = sa.take(3 * (NMAX + 60))
        o_f = [sa.take(NMAX + 60) for _ in range(5)]
        aext = scr[:, o_aext:o_aext + (3 * NA + 1) // 2].bitcast(BF16)[:, 0:3 * NA].rearrange("p (c k) -> p c k", c=3)
        tail = scr[:, o_tail:o_tail + 180].rearrange("p (c s k) -> p c s k", c=3, s=2)
        co = scr[:, o_co:o_co + 3 * (NMAX + 60)].rearrange("p (c k) -> p c k", c=3)
        fbuf = [scr[:, o:o + NMAX + 60] for o in o_f]
        fkey = lambda i: ("scr", "A", "f", i)
        wp, wpk = w_acquire("mix")
        wg, wgk = w_acquire("mix")
        rot_s3 = Rot(3)
        kp = rot_pair.i
        border = [2 * ((kp + d) % 3) + h for d in range(3) for h in range(2)]
        proj_banks, conv_banks = border[0:3], border[3:6]
        kext = ("scr", "A", "aext")
        ktl = ("scr", "A", "tail")
        kco = ("scr", "A", "co")
        def ma_body(st):
            c0, n, b0, nblk = st["c0"], st["n"], st["b0"], st["nblk"]
            need_xT(range(b0, b0 + nblk))
            xkeys = [("xT", b0 + i) for i in range(nblk)]
            segs, na = seg_layout(st, 30)
            nco = na - 30
            is_s = st["kind"] == "S"
            if not is_s:
                P.act(actf(aext[:, :, 0:30], hist_a[:, l, :, :], AF.Copy), reads=[("hist_a", l)],
                      writes=[kext + ("h",)])
            else:
                for s in range(2):
                    P.dma("sp", stA[0:30, :], cache_a[l, s], writes=["stA"], semkey="stA")
                    trt, trk = trbank(rot_tr.next())
                    for c in range(3):
                        P.pe(tp(trt[:, c * 32:c * 32 + 30], stA[0:30, c * 128:(c + 1) * 128], identF[0:30, 0:30]),
                             reads=["stA", "identF"], writes=[trk])
                    eo = segs[s][0]
                    P.act(actf(aext[:, :, eo:eo + 30],
                               trt[:, 0:96].rearrange("p (c k) -> p c k", c=3)[:, :, 0:30], AF.Copy),
                          reads=[trk], writes=[kext + ("h",)])
            for c in range(3):
                gi = proj_banks[rot_s3.next()]
                gt, gk = single(gi)
                for kc in range(8):
                    P.pe(mm(gt[:, 0:n], wg[:, kc, c * 128:(c + 1) * 128], xT[:, kc, c0:c0 + n], kc == 0, kc == 7),
                         reads=[wgk] + xkeys, writes=[gk])
                sgi = c % 2
                P.act(actf(fbuf[sgi][:, 0:n], gt[:, 0:n], AF.Sigmoid), reads=[gk], writes=[fkey(sgi)])
                pi = proj_banks[rot_s3.next()]
                ptt, ppk = single(pi)
                for kc in range(8):
                    P.pe(mm(ptt[:, 0:n], wp[:, kc, c * 128:(c + 1) * 128], xT[:, kc, c0:c0 + n], kc == 0, kc == 7),
                         reads=[wpk] + xkeys, writes=[ppk])
                ln_tick()
                for si, (eo, to, ntok) in enumerate(segs):
                    P.dve(tt(aext[:, c, eo + 30:eo + 30 + ntok], ptt[:, to:to + ntok], fbuf[sgi][:, to:to + ntok],
                             ALU.mult), reads=[ppk, fkey(sgi)], writes=[kext + ("d",)])
                    P.dve(tt(tail[:, c, si, :], ptt[:, to + ntok - 30:to + ntok], fbuf[sgi][:, to + ntok - 30:to + ntok],
                             ALU.mult), reads=[ppk, fkey(sgi)], writes=[ktl])
            if not is_s:
                if st["mask_after"]:
                    P.act(actf(hist_a[:, l, :, :], tail[:, :, 0, :], AF.Copy, scale=maskc[:, 0:1]),
                          reads=[ktl, "maskc"], writes=[("hist_a", l)])
                else:
                    P.act(actf(hist_a[:, l, :, :], tail[:, :, 0, :], AF.Copy), reads=[ktl], writes=[("hist_a", l)])
                outs = [(0, st_a_p[l])] if st["final"] else []
            else:
                outs = [(s, st_a_s[l, s]) for s in range(2)]
            for si, dst in outs:
                trt, trk = trbank(rot_tr.next())
                for c in range(3):
                    P.pe(tp(trt[0:30, c * 128:(c + 1) * 128], tail[:, c, si, :], identF[:, :]), reads=[ktl, "identF"],
                         writes=[trk])
                P.act(actf(stB[0:30, :], trt[0:30, 0:384], AF.Copy), reads=[trk], writes=["stB"])
                P.dma("sp", dst, stB[0:30, :], reads=["stB"], semkey="stB")
            yield
            cbank = [single(conv_banks[c]) for c in range(3)]
            for c in range(3):
                cb, cbk = cbank[c]
                for k in range(KA):
                    for q in range(4):
                        P.pe(lambda e, cb=cb, c=c, k=k, q=q, nco=nco: e.matmul(
                            cb[32 * q:32 * q + 32, 0:nco], dq[32 * q:32 * q + 32, c, k, :],
                            aext[32 * q:32 * q + 32, c, k:k + nco], start=(k == 0), stop=(k == KA - 1),
                            tile_position=(32 * q, 32 * q)),
                            reads=[kext + ("h",), kext + ("d",)] + dq_keys, writes=[cbk])
            yield
            s1t, s1k = trbank(0)
            s2t, s2k = trbank(1)
            for c in range(3):
                cb, cbk = cbank[c]
                sqi = 1 + c % 2
                P.act(actf(co[:, c, 0:nco], cb[:, 0:nco], AF.Identity, bias=pva[:, c, 0:1]), reads=[cbk, "pva"],
                      writes=[kco + (c,)])
                P.act(actf(fbuf[sqi][:, 0:nco], cb[:, 0:nco], AF.Square, bias=pva[:, c, 0:1]), reads=[cbk, "pva"],
                      writes=[fkey(sqi)])
                P.pe(mm(s1t[:, 0:nco], onesF[:, :], co[:, c, 0:nco], c == 0, c == 2),
                     reads=["onesF", kco + (c,)], writes=[s1k])
                P.pe(mm(s2t[:, 0:nco], onesF[:, :], fbuf[sqi][:, 0:nco], c == 0, c == 2),
                     reads=["onesF", fkey(sqi)], writes=[s2k])
            mt, rt = fbuf[3], fbuf[4]
            tm = fbuf[0]
            KM, KR = fkey(3), fkey(4)
            P.dve(ts(mt[:, 0:nco], s1t[:, 0:nco], 1.0 / DA, None, ALU.mult), reads=[s1k], writes=[KM])
            P.dve(tt(tm[:, 0:nco], mt[:, 0:nco], mt[:, 0:nco], ALU.mult), reads=[KM], writes=[fkey(0)])
            P.dve(stt(tm[:, 0:nco], s2t[:, 0:nco], 1.0 / DA, tm[:, 0:nco], ALU.mult, ALU.subtract),
                  reads=[s2k, fkey(0)], writes=[fkey(0)])
            P.act(actf(rt[:, 0:nco], tm[:, 0:nco], AF.Sqrt, bias=epsb[:, 0:1]), reads=[fkey(0), "epsb"],
                  writes=[KR])
            P.dve(lambda e, rt=rt, nco=nco: e.reciprocal(rt[:, 0:nco], rt[:, 0:nco]), reads=[KR], writes=[KR])
            for c in range(3):
                P.dve(tt(co[:, c, 0:nco], co[:, c, 0:nco], mt[:, 0:nco], ALU.subtract), reads=[kco + (c,), KM],
                      writes=[kco + (c,)])
                P.dve(tt(co[:, c, 0:nco], co[:, c, 0:nco], rt[:, 0:nco], ALU.mult), reads=[kco + (c,), KR],
                      writes=[kco + (c,)])
                for (eo, to, ntok) in segs:
                    P.act(actf(hT[:, c, c0 + to:c0 + to + ntok], co[:, c, eo:eo + ntok], AF.Silu,
                               bias=pva[:, c, 2:3], scale=pva[:, c, 1:2]),
                          reads=[kco + (c,), "pva"], writes=[("hT", c, b0 + i) for i in range(nblk)])
        gens = [ma_body(st) for st in subs]
        step = lambda g: next(g, None)
        step(gens[0])
        step(gens[0])
        for s in range(1, len(gens)):
            step(gens[s])
            step(gens[s - 1])
            step(gens[s])
        step(gens[-1])
        w_release(2)
        fence()
        sa = ScrAlloc()
        o_u = sa.take(3 * NMAX)
        o_v32 = [sa.take(DB) for _ in range(3)]
        o_vbf = sa.take(4 * DB // 2)
        o_tb = [sa.take(NMAX) for _ in range(2)]
        uT = scr[:, o_u:o_u + 3 * NMAX].rearrange("p (c k) -> p c k", c=3)
        vn32 = [scr[:, o:o + DB] for o in o_v32]
        vnbf = scr[:, o_vbf:o_vbf + 4 * DB // 2].bitcast(BF16).rearrange("p (b k) -> p b k", b=4)
        tb = [scr[:, o:o + NMAX] for o in o_tb]
        wu, wuk = w_acquire("mix")
        wvv, wvk = w_acquire("mix")
        rot_v = Rot(3)
        rot_v32 = Rot(3)
        for st in subs:
            c0, n, b0, nblk = st["c0"], st["n"], st["b0"], st["nblk"]
            need_xT(range(b0, b0 + nblk))
            xkeys = [("xT", b0 + i) for i in range(nblk)]
            is_s = st["kind"] == "S"
            sbank = [single(c) for c in range(3)]
            wt_sb, wt_k = (WTs, "WTs") if is_s else (WT, "WT")
            pend_s = []

            def emit_s(bi, kvb, sbank=sbank, wt_sb=wt_sb, wt_k=wt_k):
                for c in range(3):
                    stt_, sk = sbank[c]
                    for h2 in range(2):
                        h = 2 * c + h2
                        P.pe(mm(stt_[h2 * 64:(h2 + 1) * 64, bi * 128:(bi + 1) * 128], vnbf[:, bi, h * 64:(h + 1) * 64],
                                wt_sb[:, h, :], True, True), reads=[kvb, wt_k], writes=[sk])

            def mb_chain(bi, b0=b0, is_s=is_s, emit_s=emit_s):
                b = b0 + bi
                vi = 3 + rot_v.next()
                vt, vk = single(vi)
                vbanks[bi] = (vt, vk)
                for kc in range(8):
                    P.pe(mm(vt[:, 0:DB], xT[:, kc, b * 128:(b + 1) * 128], wvv[:, kc, 0:DB], kc == 0, kc == 7),
                         reads=[wvk, ("xT", b)], writes=[vk])
                i = rot_small.next()
                sm = small[:, i, :]
                ks = ("small", i)
                P.dve(lambda e: e.bn_stats(sm[:, 0:6], vt[:, 0:DB]), reads=[vk], writes=[ks])
                P.dve(lambda e: e.bn_aggr(sm[:, 12:14], sm[:, 0:6]), reads=[ks], writes=[ks])
                P.act(actf(sm[:, 14:15], sm[:, 13:14], AF.Sqrt, bias=epsb[:, 0:1]), reads=[ks, "epsb"], writes=[ks])
                yield
                P.dve(lambda e: e.reciprocal(sm[:, 14:15], sm[:, 14:15]), reads=[ks], writes=[ks])
                P.dve(stt(sm[:, 15:16], sm[:, 12:13], -1.0, sm[:, 14:15], ALU.mult, ALU.mult), reads=[ks], writes=[ks])
                q = rot_v32.next()
                kv = ("scr", "B", "vn32", q)
                P.act(actf(vn32[q][:, :], vt[:, 0:DB], AF.Identity, bias=sm[:, 15:16], scale=sm[:, 14:15]),
                      reads=[vk, ks], writes=[kv])
                yield
                P.dve(tt(vn32[q][:, :], vn32[q][:, :], nbg[:, :], ALU.mult), reads=[kv, "nbg"], writes=[kv])
                kvb = ("scr", "B", "vnbf", bi)
                if is_s:
                    P.dve(tt(vn32[q][:, :], vn32[q][:, :], nbb[:, :], ALU.add), reads=[kv, "nbb"], writes=[kv])
                    P.dma("sp", st_v_s[l], vn32[q][:, :], reads=[kv], semkey=kv)
                    P.act(actf(vnbf[:, bi, :], vn32[q][:, :], AF.Copy), reads=[kv], writes=[kvb])
                else:
                    P.dve(tt(vnbf[:, bi, :], vn32[q][:, :], nbb[:, :], ALU.add), reads=[kv, "nbb"], writes=[kvb])
                yield
                emit_s(bi, kvb)

            mbp = Pipe()
            vbanks = {}
            assert nblk <= 3
            for bi in range(nblk):
                mbp.push(mb_chain(bi))
            if nblk >= 2:
                ubanks = [trbank(0), trbank(1), vbanks[0]]
            else:
                ubanks = sbank
            for c in range(3):
                ut, uk = ubanks[c]
                for kc in range(8):
                    P.pe(mm(ut[:, 0:n], wu[:, kc, c * 128:(c + 1) * 128], xT[:, kc, c0:c0 + n], kc == 0, kc == 7),
                         reads=[wuk] + xkeys, writes=[uk])
                if nblk < 2:
                    P.act(actf(uT[:, c, 0:n], ut[:, 0:n], AF.Copy), reads=[uk], writes=[("scr", "B", "uT", c)])
            mbp.drain()
            if nblk >= 2:
                for c in range(3):
                    ut, uk = ubanks[c]
                    P.act(actf(uT[:, c, 0:n], ut[:, 0:n], AF.Copy), reads=[uk], writes=[("scr", "B", "uT", c)])
            bs_sb, bs_k = (bsbs, "bsbs") if is_s else (bsb, "bsb")
            for c in range(3):
                stt_, sk = sbank[c]
                ti = c % 2
                ktb = ("scr", "B", "tb", ti)
                bsrc = bs_sb[:, c, :]
                bb = bass.AP(bs_sb, bsrc.offset, [list(bsrc.ap[0]), [0, nblk], [1, 128]])
                P.dve(tt(tb[ti][:, 0:n].rearrange("p (b i) -> p b i", b=nblk),
                         stt_[:, 0:n].rearrange("p (b i) -> p b i", b=nblk), bb, ALU.add),
                      reads=[sk, bs_k], writes=[ktb])
                P.dve(tt(hT[:, 3 + c, c0:c0 + n], tb[ti][:, 0:n], uT[:, c, 0:n], ALU.mult),
                      reads=[ktb, ("scr", "B", "uT", c)], writes=[("hT", 3 + c, b0 + i) for i in range(nblk)])
        w_release(2)
        fence()
        sa = ScrAlloc()
        o_g = [sa.take(NMAX) for _ in range(2)]
        NCX = NMAX + 8
        o_cx = sa.take(2 * NCX)
        o_gb = sa.take(2 * NMAX)
        o_ca = sa.take(2 * NCX)
        gct = [scr[:, o:o + NMAX] for o in o_g]
        cxext = scr[:, o_cx:o_cx + 2 * NCX].rearrange("p (c k) -> p c k", c=2)
        gbt = scr[:, o_gb:o_gb + 2 * NMAX].rearrange("p (c k) -> p c k", c=2)
        cacc = scr[:, o_ca:o_ca + 2 * NCX].rearrange("p (c k) -> p c k", c=2)
        wc0, wc0k = w_acquire("mix")
        wc1, wc1k = w_acquire("mix")
        rot_s6 = Rot(6)
        for st in subs:
            c0, n, b0, nblk = st["c0"], st["n"], st["b0"], st["nblk"]
            need_xT(range(b0, b0 + nblk))
            xkeys = [("xT", b0 + i) for i in range(nblk)]
            segs, nx = seg_layout(st, 2)
            nco = nx - 2
            kext = ("scr", "C", "cxext")
            hist_fill_and_state(st, l, cxext, 2, 2, segs, hist_c, cache_c, (stC, "stC"), None, st_c_p, st_c_s, "hist_c", kext)
            for c in range(2):
                gi = rot_s6.next()
                gt, gk = single(gi)
                for kc in range(8):
                    P.pe(mm(gt[:, 0:n], wc1[:, kc, c * 128:(c + 1) * 128], xT[:, kc, c0:c0 + n], kc == 0, kc == 7),
                         reads=[wc1k] + xkeys, writes=[gk])
                kg = ("scr", "C", "gct", c)
                P.act(actf(gct[c][:, 0:n], gt[:, 0:n], AF.Copy), reads=[gk], writes=[kg])
                xi = rot_s6.next()
                xt_, xk = single(xi)
                for kc in range(8):
                    P.pe(mm(xt_[:, 0:n], wc0[:, kc, c * 128:(c + 1) * 128], xT[:, kc, c0:c0 + n], kc == 0, kc == 7),
                         reads=[wc0k] + xkeys, writes=[xk])
                for (eo, to, ntok) in segs:
                    P.dve(tt(cxext[:, c, eo + 2:eo + 2 + ntok], xt_[:, to:to + ntok], gct[c][:, to:to + ntok], ALU.mult),
                          reads=[xk, kg], writes=[kext + ("d",)])
                bi_ = rot_s6.next()
                bt, bk = single(bi_)
                for kc in range(8):
                    P.pe(mm(bt[:, 0:n], wc0[:, kc, 256 + c * 128:256 + (c + 1) * 128], xT[:, kc, c0:c0 + n],
                            kc == 0, kc == 7), reads=[wc0k] + xkeys, writes=[bk])
                P.act(actf(gbt[:, c, 0:n], bt[:, 0:n], AF.Copy), reads=[bk], writes=[("scr", "C", "gb", c)])
            state_out(st, l, cxext, 2, 2, segs, hist_c, (stD, "stD"), st_c_p, st_c_s, "hist_c", kext)
            for c in range(2):
                kca = ("scr", "C", "cacc", c)
                rk = [kext + ("h",), kext + ("d",), "cwc"]
                P.dve(ts(cacc[:, c, 0:nco], cxext[:, c, 0:nco], cwc[:, c, 0:1], None, ALU.mult), reads=rk, writes=[kca])
                for k in range(1, KC):
                    P.dve(stt(cacc[:, c, 0:nco], cxext[:, c, k:k + nco], cwc[:, c, k:k + 1], cacc[:, c, 0:nco],
                              ALU.mult, ALU.add), reads=rk + [kca], writes=[kca])
                for (eo, to, ntok) in segs:
                    P.dve(tt(hT[:, 6 + c, c0 + to:c0 + to + ntok], cacc[:, c, eo:eo + ntok], gbt[:, c, to:to + ntok],
                             ALU.mult), reads=[kca, ("scr", "C", "gb", c)],
                          writes=[("hT", 6 + c, b0 + i) for i in range(nblk)])
        w_release(2)
        wo = [w_acquire("mo") for _ in range(2)]
        ln_params_ready()
        for b in range(nb):
            pi = rot_pair.next()
            pt, pk = pair(pi)
            for hc in range(2):
                wv, wk = wo[hc]
                for kc in range(8):
                    P.pe(mm(pt[:, hc * 512:(hc + 1) * 512], hT[:, kc, b * 128:(b + 1) * 128], wv[:, kc, :],
                            kc == 0, kc == 7), reads=[wk, ("hT", kc, b)], writes=[pk[hc]])
            sm, ks = ln_slot()
            P.dve(lambda e, b=b, pt=pt, sm=sm: e.scalar_tensor_tensor(
                xres[:, b, :], pt[:, :], 1.0 / ALPHA, xres[:, b, :], ALU.mult, ALU.add, accum_out=sm[:, 0:1]),
                reads=[pk[0], pk[1], ("xres", b)], writes=[("xres", b), ks])
            LNP.push(ln_chain(b, epsp, True, None, sm, ks))
        w_release(2)

    for tile in cfg["tiles"]:
        nb = tile["nb"]
        for b in range(nb):
            gb = tile["gb0"] + b
            P.dma("sp", xres[:, b, :], xin[gb * 128:(gb + 1) * 128, :], writes=[("xres", b)], semkey=("xres", b))
        for b in range(nb):
            cast_x(b)
            flush_xT(1)
        flush_xT(0)
        for l in range(L):
            ffn(tile, l, 1, last=False)
            mixer(tile, l)
            ffn(tile, l, 2, last=(l == L - 1))
    assert WS.acquired == len(units) and WS.released == len(units)

    P.emit(nc, es)
    es.close()
    return nc, P


WEIGHT_NAMES = ["w_ffn1_in", "w_ffn1_out", "ln1_g", "ln1_b", "w_in", "conv_a_w", "conv_a_b", "norm_a_g",
                "norm_a_b", "norm_b_g", "norm_b_b", "w_s", "b_s", "conv_c_w", "w_out", "ln2_g", "ln2_b",
                "w_ffn2_in", "w_ffn2_out", "ln3_g", "ln3_b"]


def const_inputs():
    ident = np.eye(128, dtype=np.float32)
    j = np.arange(128)[:, None]
    i = np.arange(128)[None, :]
    maskT = ((j // 64) <= (i // 64)).astype(np.float32)
    return ident, maskT


def full_cfg():
    P = lambda n, **kw: dict(kind="P", nblk=n, **kw)
    tiles = [
        [P(3), P(3), P(3)],
        [P(3), P(3), P(3)],
        [P(3), P(3), P(3)],
        [P(3), P(3), P(1, final=True), dict(kind="S")],
    ]
    return make_cfg(4, tiles, ring=7)


NPB = 34
SPLIT = NPB * 128


_CACHE = {}


def kernel(**inputs):
    cfg = full_cfg()
    if "nc" not in _CACHE:
        _CACHE["nc"] = build_program(cfg)[0]
    nc = _CACHE["nc"]
    x_prompt = np.asarray(inputs["x_prompt"], dtype=np.float32)
    x_sample = np.asarray(inputs["x_sample"], dtype=np.float32)
    ca = np.asarray(inputs["cache_conv_a"], dtype=np.float32)
    cc = np.asarray(inputs["cache_conv_c"], dtype=np.float32)
    ident, maskT = const_inputs()
    wts = {k: np.ascontiguousarray(np.asarray(inputs[k], dtype=np.float32)) for k in WEIGHT_NAMES}
    START1 = 8192 - SPLIT
    assert SPLIT - START1 >= 384 and START1 % 128 == 0
    in_maps = []
    for c in range(8):
        seq, half = c // 2, c % 2
        xp = x_prompt[seq, 0:SPLIT] if half == 0 else x_prompt[seq, START1:8192]
        xs = x_sample[2 * c:2 * c + 2].reshape(128, D)
        m = dict(wts)
        m["xin"] = np.ascontiguousarray(np.concatenate([xp, xs], axis=0))
        m["cache_a"] = np.ascontiguousarray(ca[:, 2 * c:2 * c + 2])
        m["cache_c"] = np.ascontiguousarray(cc[:, 2 * c:2 * c + 2])
        m["maskin"] = np.ones((128, 1), np.float32)
        m["maskT"] = maskT
        m["identin"] = ident
        in_maps.append(m)
    res = run_bass_kernel_spmd(nc, in_maps, core_ids=list(range(8)))
    r = res.results
    y_prompt = np.empty((4, 8192, D), np.float32)
    y_sample = np.empty((16, 64, D), np.float32)
    st_a_p = np.empty((4, 4, 30, DA), np.float32)
    st_c_p = np.empty((4, 4, 2, DC), np.float32)
    st_a_s = np.empty((4, 16, 30, DA), np.float32)
    st_c_s = np.empty((4, 16, 2, DC), np.float32)
    st_v_s = np.empty((4, 16, 64, DB), np.float32)
    for c in range(8):
        seq, half = c // 2, c % 2
        yo = r[c]["yout"]
        if half == 0:
            y_prompt[seq, 0:SPLIT] = yo[0:SPLIT]
        else:
            y_prompt[seq, SPLIT:8192] = yo[SPLIT - START1:SPLIT]
        y_sample[2 * c:2 * c + 2] = yo[SPLIT:SPLIT + 128].reshape(2, 64, D)
        if half == 1:
            st_a_p[:, seq] = r[c]["st_a_p"]
            st_c_p[:, seq] = r[c]["st_c_p"]
        st_a_s[:, 2 * c:2 * c + 2] = r[c]["st_a_s"]
        st_c_s[:, 2 * c:2 * c + 2] = r[c]["st_c_s"]
        st_v_s[:, 2 * c:2 * c + 2] = r[c]["st_v_s"].reshape(4, 2, 64, DB)
    return (y_prompt, y_sample, st_a_p, st_c_p, st_a_s, st_c_s, st_v_s)
```

```python
import math
from contextlib import ExitStack

import numpy as np
import concourse.bass as bass
import concourse.mybir as mybir
from concourse.bass_utils import run_bass_kernel_spmd

F32 = mybir.dt.float32
BF16 = mybir.dt.bfloat16
AF = mybir.ActivationFunctionType
ALU = mybir.AluOpType

D = 1024
DFF = 2816
DA = 384
DB = 384
DC = 256
DIN = 2304
NFC = 22
NFH = 11
ALPHA = 8.0 ** 0.25
EPS = 1e-5
KA = 31
KC = 3

ENGS = ("pe", "act", "dve", "pool", "sp")
EPOCH = 20000
XT_DEFER = 2
PA_HOLD = 2


class Op:
    __slots__ = ("eng", "idx", "fn", "deps", "is_dma", "semkey", "dcount", "signal", "seq")


class Prog:
    def __init__(self):
        self.ops = {e: [] for e in ENGS}
        self.lastw = {}
        self.readers = {}
        self.dma_counts = {}

    def _add(self, eng, fn, reads, writes, is_dma=False, semkey=None):
        op = Op()
        op.eng = eng
        op.idx = len(self.ops[eng])
        op.fn = fn
        op.is_dma = is_dma
        op.semkey = semkey
        op.signal = False
        op.seq = 0
        op.dcount = 0
        if any(isinstance(k, tuple) and k[0] == "scr" for k in list(reads) + list(writes)):
            reads = list(reads) + ["scrfence"]
        deps = {}
        for k in reads:
            w = self.lastw.get(k)
            if w is not None:
                deps[id(w)] = (w, "raw")
        for k in writes:
            w = self.lastw.get(k)
            if w is not None and id(w) not in deps:
                deps[id(w)] = (w, "waw")
            rd = self.readers.get(k)
            if rd is not None:
                for r in rd[0].values():
                    if id(r) not in deps:
                        deps[id(r)] = (r, "war")
                for r in rd[1]:
                    if id(r) not in deps:
                        deps[id(r)] = (r, "war")
        final = []
        for p, kind in deps.values():
            if p.is_dma:
                need = True
            elif p.eng != eng:
                need = True
            elif is_dma:
                need = True
            else:
                need = eng != "pe"
            if need:
                if not p.is_dma:
                    p.signal = True
                final.append(p)
        op.deps = final
        for k in reads:
            rd = self.readers.get(k)
            if rd is None:
                rd = ({}, [])
                self.readers[k] = rd
            if is_dma:
                rd[1].append(op)
            else:
                rd[0][eng] = op
        for k in writes:
            self.lastw[k] = op
            self.readers[k] = ({}, [])
        if is_dma:
            c = self.dma_counts.get(semkey, 0) + 1
            self.dma_counts[semkey] = c
            op.dcount = c
        self.ops[eng].append(op)
        return op

    def pe(self, fn, reads=(), writes=()):
        return self._add("pe", fn, reads, writes)

    def act(self, fn, reads=(), writes=()):
        return self._add("act", fn, reads, writes)

    def dve(self, fn, reads=(), writes=()):
        return self._add("dve", fn, reads, writes)

    def pool(self, fn, reads=(), writes=()):
        return self._add("pool", fn, reads, writes)

    def dma(self, eng, out_ap, in_ap, reads=(), writes=(), semkey=None):
        def fn(e, out_ap=out_ap, in_ap=in_ap):
            return e.dma_start(out=out_ap, in_=in_ap)
        return self._add(eng, fn, reads, writes, is_dma=True, semkey=semkey)

    def emit(self, nc, es):
        nsig = {}
        for eng in ENGS:
            cnt = 0
            for op in self.ops[eng]:
                if op.signal and not op.is_dma:
                    cnt += 1
                    op.seq = cnt
            nsig[eng] = cnt
        eng_sems = {}
        for eng in ENGS:
            n = max(1, math.ceil(nsig[eng] / EPOCH))
            eng_sems[eng] = [es.enter_context(nc.semaphore(f"s_{eng}_{i}")) for i in range(n)]
        dma_sems = {}
        for i, k in enumerate(self.dma_counts):
            dma_sems[k] = es.enter_context(nc.semaphore(f"d_{i}"))

        def resolve(p):
            if p.is_dma:
                return dma_sems[p.semkey], 16 * p.dcount
            ep = (p.seq - 1) // EPOCH
            return eng_sems[p.eng][ep], p.seq - ep * EPOCH

        def run(eng, e):
            waited = {}
            for op in self.ops[eng]:
                for p in op.deps:
                    sem, val = resolve(p)
                    if waited.get(id(sem), 0) < val:
                        e.wait_ge(sem, val)
                        waited[id(sem)] = val
                ins = op.fn(e)
                if op.is_dma:
                    ins.then_inc(dma_sems[op.semkey], 16)
                elif op.signal:
                    ep = (op.seq - 1) // EPOCH
                    ins.then_inc(eng_sems[eng][ep], 1)
            if eng == "sp":
                for k, c in self.dma_counts.items():
                    e.wait_ge(dma_sems[k], 16 * c)

        with nc.Block() as block:
            @block.tensor
            def _(e):
                run("pe", e)

            @block.scalar
            def _(e):
                run("act", e)

            @block.vector
            def _(e):
                run("dve", e)

            @block.gpsimd
            def _(e):
                run("pool", e)

            @block.sync
            def _(e):
                run("sp", e)


def make_cfg(L, tiles, ring=7):
    nb_max = 0
    gb = 0
    out = []
    n_s = 0
    for t in tiles:
        tt = []
        b = 0
        for st in t:
            d = dict(kind=st["kind"], nblk=st.get("nblk", 1), mask_after=st.get("mask_after", False),
                     final=st.get("final", False), b0=b)
            if d["kind"] == "S":
                d["nblk"] = 1
                n_s += 1
            d["n"] = d["nblk"] * 128
            d["c0"] = b * 128
            b += d["nblk"]
            tt.append(d)
        out.append(dict(subs=tt, nb=b, gb0=gb))
        gb += b
        nb_max = max(nb_max, b)
    nmax = max(st["n"] for t in out for st in t["subs"])
    assert n_s <= 1
    return dict(L=L, tiles=out, NB=nb_max, NBLK=gb, RING=ring, NMAX=nmax, has_s=(n_s == 1))


def build_program(cfg):
    L = cfg["L"]
    NB = cfg["NB"]
    NBLK = cfg["NBLK"]
    RING = cfg["RING"]
    NMAX = cfg["NMAX"]
    NT = NB * 128
    nc = bass.Bass("TRN2", target_bir_lowering=False)
    P = Prog()
    es = ExitStack()

    def din(name, shape):
        return nc.dram_tensor(name, list(shape), F32, kind="ExternalInput").ap()

    def dout(name, shape):
        return nc.dram_tensor(name, list(shape), F32, kind="ExternalOutput").ap()

    xin = din("xin", [NBLK * 128, D])
    cache_a = din("cache_a", [L, 2, 30, DA])
    cache_c = din("cache_c", [L, 2, 2, DC])
    maskin = din("maskin", [128, 1])
    maskT_in = din("maskT", [128, 128])
    W = {}
    for nm, shp in (("w_ffn1_in", [L, D, 2 * DFF]), ("w_ffn1_out", [L, DFF, D]), ("ln1_g", [L, D]),
                    ("ln1_b", [L, D]), ("w_in", [L, D, DIN]), ("conv_a_w", [L, KA, DA]),
                    ("conv_a_b", [L, DA]), ("norm_a_g", [L, DA]), ("norm_a_b", [L, DA]),
                    ("norm_b_g", [L, DB]), ("norm_b_b", [L, DB]), ("w_s", [L, 6, 128, 128]),
                    ("b_s", [L, 6, 128]), ("conv_c_w", [L, KC, DC]), ("w_out", [L, D, D]),
                    ("ln2_g", [L, D]), ("ln2_b", [L, D]), ("w_ffn2_in", [L, D, 2 * DFF]),
                    ("w_ffn2_out", [L, DFF, D]), ("ln3_g", [L, D]), ("ln3_b", [L, D])):
        W[nm] = din(nm, shp)
    yout = dout("yout", [NBLK * 128, D])
    st_a_p = dout("st_a_p", [L, 30, DA])
    st_c_p = dout("st_c_p", [L, 2, DC])
    st_a_s = dout("st_a_s", [L, 2, 30, DA])
    st_c_s = dout("st_c_s", [L, 2, 2, DC])
    st_v_s = dout("st_v_s", [L, 128, DB])

    def sb(name, shape, dt=F32):
        return es.enter_context(nc.sbuf_tensor(name, list(shape), dt))

    def ps(name, shape, dt=F32):
        return es.enter_context(nc.psum_tensor(name, list(shape), dt))

    xres = sb("xres", [128, NB, D])
    xT = sb("xT", [128, 8, NT], BF16)
    hT = sb("hT", [128, NFH, NT], BF16)
    ring = sb("ring", [128, RING, 4096], BF16)
    NA = NMAX + 60
    SCR = max((3 * NA + 1) // 2 + 180 + 8 * NA + 8, 5 * NMAX + 2000, 8 * NMAX + 64)
    scr = sb("scr", [128, SCR])
    lng = sb("lng", [128, D])
    lnb = sb("lnb", [128, D])
    tmpA = [sb(f"tmpA{i}", [128, D]) for i in range(2)]
    xbf = [sb(f"xbf{i}", [128, D], BF16) for i in range(XT_DEFER + 1)]
    sgt = [sb(f"sgt{i}", [128, 512]) for i in range(2)]
    small = sb("small", [128, 8, 16])
    sqjunk = sb("sqjunk", [128, D], BF16)
    identB = sb("identB", [128, 128], BF16)
    identF = sb("identF", [128, 128])
    onesF = sb("onesF", [128, 128])
    maskT = sb("maskTs", [128, 128])
    maskc = sb("maskc", [128, 1])
    Wn = sb("Wn", [128, 6, 128])
    WT = sb("WT", [128, 6, 128], BF16)
    WTs = sb("WTs", [128, 6, 128], BF16)
    bsb = sb("bsb", [128, 3, 128])
    bsbs = sb("bsbs", [128, 3, 128])
    nbg = sb("nbg", [128, DB])
    nbb = sb("nbb", [128, DB])
    cwa = sb("cwa", [128, 3, KA])
    pva = sb("pva", [128, 3, 3])
    cwc = sb("cwc", [128, 2, KC])
    hist_a = sb("hist_a", [128, L, 3, 30])
    hist_c = sb("hist_c", [128, L, 2, 2])
    stA = sb("stA", [32, DA])
    stB = sb("stB", [32, DA])
    stC = sb("stC", [4, DC])
    stD = sb("stD", [4, DC])
    Iq = sb("Iq", [128, 32])
    dqt = sb("dq", [128, 3, KA, 32], BF16)
    dq = dqt[:, :, :, :]
    dq_keys = ["dq"]
    dummy = sb("dummyt", [128, 2])
    epsb = sb("epsb", [128, 2])

    acc = [ps(f"acc{i}", [128, 1024]) for i in range(3)]
    trb = [ps(f"trb{i}", [128, 512]) for i in range(2)]

    class Rot:
        def __init__(self, n):
            self.n = n
            self.i = 0

        def next(self):
            i = self.i
            self.i = (i + 1) % self.n
            return i

    rot_pair = Rot(3)
    rot_tr = Rot(2)
    rot_tmpA = Rot(2)
    rot_xbf = Rot(XT_DEFER + 1)
    rot_sgt = Rot(2)
    rot_small = Rot(8)

    def single(i):
        return acc[i // 2][:, (i % 2) * 512:(i % 2 + 1) * 512], ("ps", i)

    def pair(i):
        return acc[i], [("ps", 2 * i), ("ps", 2 * i + 1)]

    def trbank(i):
        return trb[i], ("ps", 6 + i)

    def bcast_rows(ap2d_row, nparts):
        n = ap2d_row.shape[-1]
        return bass.AP(ap2d_row.tensor, ap2d_row.offset, [[0, nparts], [1, n]])

    def mm(out, lhsT, rhs, start, stop):
        return lambda e: e.matmul(out, lhsT, rhs, start=start, stop=stop)

    def tp(out, in_, ident):
        return lambda e: e.transpose(out, in_, ident)

    def actf(out, in_, func, bias=None, scale=None):
        kw = {}
        if bias is not None:
            kw["bias"] = bias
        if scale is not None:
            kw["scale"] = scale
        return lambda e: e.activation(out, in_, func, **kw)

    def tt(out, in0, in1, op):
        return lambda e: e.tensor_tensor(out, in0, in1, op)

    def ts(out, in0, s1, s2, op0, op1=None):
        if op1 is None:
            return lambda e: e.tensor_scalar(out, in0, s1, None, op0)
        return lambda e: e.tensor_scalar(out, in0, s1, s2, op0, op1)

    def stt(out, in0, scalar, in1, op0, op1):
        return lambda e: e.scalar_tensor_tensor(out, in0, scalar, in1, op0, op1)

    def cp(out, in_):
        return lambda e: e.tensor_copy(out, in_)

    def mset(ap, v):
        return lambda e: e.memset(ap, v)

    identin = din("identin", [128, 128])
    P.dma("sp", identF[:, :], identin, writes=["identF"], semkey="identF")
    P.dma("sp", maskT[:, :], maskT_in, writes=["maskT"], semkey="maskT")
    P.dma("sp", maskc[:, :], maskin, writes=["maskc"], semkey="maskc")
    P.act(actf(identB[:, :], identF[:, :], AF.Copy), reads=["identF"], writes=["identB"])
    for q in range(4):
        P.dve(cp(Iq[32 * q:32 * q + 32, :], identF[32 * q:32 * q + 32, 32 * q:32 * q + 32]), reads=["identF"],
              writes=["Iq"])
    P.dve(mset(onesF[:, :], 1.0), writes=["onesF"])
    P.dve(mset(epsb[:, 0:1], EPS), writes=["epsb"])
    P.dve(mset(epsb[:, 1:2], EPS / (ALPHA * ALPHA)), writes=["epsb"])
    P.dve(mset(hist_a[:, :, :, :].rearrange("p a b c -> p (a b c)"), 0.0), writes=[("hist_a", l) for l in range(L)])
    P.dve(mset(hist_c[:, :, :, :].rearrange("p a b c -> p (a b c)"), 0.0), writes=[("hist_c", l) for l in range(L)])
    P.dve(mset(dummy[:, :], 0.0), writes=["scrfence"])

    units = []

    def ffn_units(l, which):
        wi = W["w_ffn%d_in" % which]
        wo = W["w_ffn%d_out" % which]
        for half in range(2):
            j = half * NFH
            while j < (half + 1) * NFH:
                nch = min(2, (half + 1) * NFH - j)
                units.append(("fin", wi, l, j, nch))
                j += nch
            k = 0
            while k < NFH:
                nk = min(4, NFH - k)
                units.append(("fout", wo, l, half * NFH + k, nk))
                k += nk

    MIXCOLS = [(0, 384), (384, 768), (768, 1152), (1152, 1536), (1536, 2048), (2048, 2304)]

    def mixer_units(l):
        for c0, c1 in MIXCOLS:
            units.append(("mix", W["w_in"], l, c0, c1))
        for h in range(2):
            units.append(("mo", W["w_out"], l, h))

    for _t in cfg["tiles"]:
        for l in range(L):
            ffn_units(l, 1)
            mixer_units(l)
            ffn_units(l, 2)

    class WS:
        issued = 0
        acquired = 0
        released = 0

    def slot_view(s, kind):
        flat = ring[:, s, :]
        if kind == "fin":
            return flat.rearrange("p (k t f) -> p k t f", k=8, t=2)
        if kind == "fout":
            return flat.rearrange("p (k m) -> p k m", k=4)
        return flat.rearrange("p (k c) -> p k c", k=8)

    def w_issue(u):
        s = u % RING
        d = units[u]
        kind = d[0]
        v = slot_view(s, kind)
        if kind == "fin":
            _, w, l, j, nch = d
            wsrc = w[l].rearrange("(kc p) (two f) -> p kc two f", p=128, two=2)
            for part in range(2):
                P.dma("pool", v[:, :, part, 0:nch * 128], wsrc[:, :, part, j * 128:(j + nch) * 128],
                      writes=[("ring", s)], semkey=("ring", s))
            return
        elif kind == "fout":
            _, w, l, k0, nk = d
            src = w[l].rearrange("(kc p) m -> p kc m", p=128)[:, k0:k0 + nk, :]
            dst = v[:, 0:nk, :]
        elif kind == "mix":
            _, w, l, c0, c1 = d
            src = w[l].rearrange("(kc p) c -> p kc c", p=128)[:, :, c0:c1]
            dst = v[:, :, 0:c1 - c0]
        else:
            _, w, l, h = d
            src = w[l].rearrange("(kc p) c -> p kc c", p=128)[:, :, h * 512:(h + 1) * 512]
            dst = v[:, :, :]
        P.dma("pool", dst, src, writes=[("ring", s)], semkey=("ring", s))

    def w_prefetch():
        while WS.issued < len(units) and WS.issued < WS.released + RING:
            w_issue(WS.issued)
            WS.issued += 1

    def w_acquire(kind):
        u = WS.acquired
        assert u < WS.issued, "weight ring too small"
        assert units[u][0] == kind, (units[u][0], kind)
        WS.acquired += 1
        s = u % RING
        return slot_view(s, kind), ("ring", s)

    def w_release(n=1):
        WS.released += n
        w_prefetch()

    w_prefetch()

    pending = []

    def cast_x(b):
        i = rot_xbf.next()
        assert all(pi != i for _, pi in pending)
        P.act(actf(xbf[i][:, :], xres[:, b, :], AF.Copy), reads=[("xres", b)], writes=[("xbf", i)])
        pending.append((b, i))

    def flush_xT(keep=0):
        while len(pending) > keep:
            b, i = pending.pop(0)
            finish_xT(b, i)

    chain_live = set()

    def need_xT(blocks):
        if any(b in chain_live for b in blocks):
            LNP.drain()
        if any(pb in blocks for pb, _ in pending):
            flush_xT(0)

    def finish_xT(b, i):
        j = rot_tr.next()
        trt, trk = trbank(j)
        trv = trt[:, :].bitcast(BF16).rearrange("p (a b) -> p a b", a=8)
        for kc in range(8):
            P.pe(tp(trv[:, kc, :], xbf[i][:, kc * 128:(kc + 1) * 128], identB[:, :]),
                 reads=[("xbf", i), "identB"], writes=[trk])
        P.act(actf(xT[:, :, b * 128:(b + 1) * 128], trv, AF.Copy), reads=[trk], writes=[("xT", b)])

    def make_xT(b):
        cast_x(b)
        flush_xT(0)

    class Pipe:
        def __init__(self):
            self.live = []

        def push(self, gen):
            self.live.insert(0, gen)
            self._advance()

        def _advance(self):
            nxt = []
            for g in self.live:
                try:
                    next(g)
                    nxt.append(g)
                except StopIteration:
                    pass
            self.live = nxt

        def drain(self):
            while self.live:
                self._advance()

    LNP = Pipe()
    LN_NEXT = [None]

    def ln_tick():
        if LNP.live:
            LNP._advance()
        if LN_NEXT[0] is not None and not LNP.live:
            load_ln(*LN_NEXT[0])
            LN_NEXT[0] = None

    def ln_params_ready():
        if LN_NEXT[0] is not None:
            LNP.drain()
            load_ln(*LN_NEXT[0])
            LN_NEXT[0] = None

    def ln_slot():
        i = rot_small.next()
        return small[:, i, :], ("small", i)

    def ln_chain(b, epsp, need_xT_, out_gb, sm, ks):
        kx = ("xres", b)
        chain_live.add(b)
        P.act(lambda e: e.activation(sqjunk[:, :], xres[:, b, :], AF.Square, scale=1.0 / 32.0,
                                     accum_out=sm[:, 1:2]), reads=[kx], writes=["sqjunk", ks])
        yield
        P.dve(tt(sm[:, 2:3], sm[:, 0:1], sm[:, 0:1], ALU.mult), reads=[ks], writes=[ks])
        P.dve(stt(sm[:, 13:14], sm[:, 2:3], -1.0 / (1024.0 * 1024.0), sm[:, 1:2], ALU.mult, ALU.add),
              reads=[ks], writes=[ks])
        eb = epsb[:, 0:1] if epsp == EPS else epsb[:, 1:2]
        P.act(actf(sm[:, 14:15], sm[:, 13:14], AF.Sqrt, bias=eb), reads=[ks, "epsb"], writes=[ks])
        yield
        P.dve(lambda e: e.reciprocal(sm[:, 14:15], sm[:, 14:15]), reads=[ks], writes=[ks])
        P.dve(stt(sm[:, 15:16], sm[:, 0:1], -1.0 / 1024.0, sm[:, 14:15], ALU.mult, ALU.mult), reads=[ks], writes=[ks])
        t = rot_tmpA.next()
        kt = ("tmpA", t)
        P.act(actf(tmpA[t][:, :], xres[:, b, :], AF.Identity, bias=sm[:, 15:16], scale=sm[:, 14:15]),
              reads=[kx, ks], writes=[kt])
        yield
        P.dve(tt(tmpA[t][:, :], tmpA[t][:, :], lng[:, :], ALU.mult), reads=[kt, "lng"], writes=[kt])
        P.dve(tt(xres[:, b, :], tmpA[t][:, :], lnb[:, :], ALU.add), reads=[kt, "lnb"], writes=[kx])
        chain_live.discard(b)
        if need_xT_:
            cast_x(b)
            flush_xT(XT_DEFER)
        if out_gb is not None:
            P.dma("sp", yout[out_gb * 128:(out_gb + 1) * 128, :], xres[:, b, :], reads=[kx], semkey=kx)

    def load_ln(gname, bname, l):
        P.dma("sp", lng[:, :], bcast_rows(W[gname][l], 128), writes=["lng"], semkey="lng")
        P.dma("sp", lnb[:, :], bcast_rows(W[bname][l], 128), writes=["lnb"], semkey="lnb")

    def ffn(tile, l, which, last):
        subs = tile["subs"]
        nb = tile["nb"]
        cscale = 0.5 / ALPHA
        epsp = EPS / (ALPHA * ALPHA)
        LN_NEXT[0] = ("ln1_g" if which == 1 else "ln3_g", "ln1_b" if which == 1 else "ln3_b", l)
        if not LNP.live:
            ln_tick()
        if which == 1:
            issue_param_dmas(tile, l)
        for half in range(2):
            if which == 1 and half == 1:
                finish_params(tile, l)
            def phase_a(wv, wk, jl, nch, sts):
                for st in sts:
                    c0, n, b0, nblk = st["c0"], st["n"], st["b0"], st["nblk"]
                    need_xT(range(b0, b0 + nblk))
                    xkeys = [("xT", b0 + i) for i in range(nblk)]
                    for ci in range(nch):
                        pi = rot_pair.next()
                        pt, pk = pair(pi)
                        for part in range(2):
                            for kc in range(8):
                                P.pe(mm(pt[:, part * 512:part * 512 + n], wv[:, kc, part, ci * 128:(ci + 1) * 128],
                                        xT[:, kc, c0:c0 + n], kc == 0, kc == 7),
                                     reads=[wk] + xkeys, writes=[pk[part]])
                        si = rot_sgt.next()
                        P.act(actf(sgt[si][:, 0:n], pt[:, 0:n], AF.Silu), reads=[pk[0]], writes=[("sgt", si)])
                        P.dve(tt(hT[:, jl + ci, c0:c0 + n], sgt[si][:, 0:n], pt[:, 512:512 + n], ALU.mult),
                              reads=[("sgt", si), pk[1]], writes=[("hT", jl + ci, b0 + i) for i in range(nblk)])
                    ln_tick()

            ulist = []
            jl = 0
            while jl < NFH:
                nch = min(2, NFH - jl)
                ulist.append((jl, nch))
                jl += nch
            ui = 0
            if half == 0 and LNP.live and len(subs) > 1:
                held = []
                for (jl, nch) in ulist[:PA_HOLD]:
                    wv, wk = w_acquire("fin")
                    held.append((wv, wk, jl, nch))
                    phase_a(wv, wk, jl, nch, subs[:-1])
                for (wv, wk, jl, nch) in held:
                    phase_a(wv, wk, jl, nch, subs[-1:])
                w_release(len(held))
                ui = len(held)
            for (jl, nch) in ulist[ui:]:
                wv, wk = w_acquire("fin")
                phase_a(wv, wk, jl, nch, subs)
                w_release()
            wviews = []
            k = 0
            while k < NFH:
                nk = min(4, NFH - k)
                wviews.append(w_acquire("fout"))
                k += nk
            if half == 1:
                ln_params_ready()
            for b in range(nb):
                pi = rot_pair.next()
                pt, pk = pair(pi)
                for hc in range(2):
                    for k in range(NFH):
                        wv, wk = wviews[k // 4]
                        P.pe(mm(pt[:, hc * 512:(hc + 1) * 512], hT[:, k, b * 128:(b + 1) * 128],
                                wv[:, k % 4, hc * 512:(hc + 1) * 512], k == 0, k == NFH - 1),
                             reads=[wk, ("hT", k, b)], writes=[pk[hc]])
                if half == 0:
                    P.dve(stt(xres[:, b, :], pt[:, :], cscale, xres[:, b, :], ALU.mult, ALU.add),
                          reads=[pk[0], pk[1], ("xres", b)], writes=[("xres", b)])
                else:
                    sm, ks = ln_slot()
                    P.dve(lambda e, b=b, pt=pt, sm=sm: e.scalar_tensor_tensor(
                        xres[:, b, :], pt[:, :], cscale, xres[:, b, :], ALU.mult, ALU.add, accum_out=sm[:, 0:1]),
                        reads=[pk[0], pk[1], ("xres", b)], writes=[("xres", b), ks])
                    LNP.push(ln_chain(b, epsp, not last, (tile["gb0"] + b) if last else None, sm, ks))
            if last:
                LNP.drain()
            w_release(len(wviews))

    class ScrAlloc:
        def __init__(self):
            self.off = 0

        def take(self, n):
            o = self.off
            self.off += n
            assert self.off <= SCR, (self.off, SCR)
            return o

    def fence():
        P.dve(mset(dummy[:, 0:1], 0.0), writes=["scrfence"])

    def seg_layout(st, hl):
        if st["kind"] == "S":
            return [(0, 0, 64), (hl + 64, 64, 64)], 2 * (hl + 64)
        return [(0, 0, st["n"])], hl + st["n"]

    def issue_param_dmas(tile, l):
        has_p = any(s["kind"] == "P" for s in tile["subs"])
        has_s = any(s["kind"] == "S" for s in tile["subs"])

        def row(ap1d):
            n = ap1d.shape[-1]
            return bass.AP(ap1d.tensor, ap1d.offset, [[n, 1], [1, n]])
        P.dma("sp", stA[0:KA, :], W["conv_a_w"][l], writes=["stA"], semkey="stA")
        for r, nm in enumerate(("conv_a_b", "norm_a_g", "norm_a_b")):
            P.dma("sp", stB[r:r + 1, :], row(W[nm][l]), writes=["stB"], semkey="stB")
        P.dma("sp", stC[0:KC, :], W["conv_c_w"][l], writes=["stC"], semkey="stC")
        P.dma("sp", nbg[:, :], bcast_rows(W["norm_b_g"][l], 128), writes=["nbg"], semkey="nbg")
        P.dma("sp", nbb[:, :], bcast_rows(W["norm_b_b"][l], 128), writes=["nbb"], semkey="nbb")
        bs_t = W["b_s"].tensor
        if has_p:
            P.dma("sp", Wn[:, :, :], W["w_s"][l].rearrange("h i j -> i h j"), writes=["Wn"], semkey="Wn")
            for h2 in range(2):
                src_ = bass.AP(bs_t, W["b_s"][l].offset + h2 * 128, [[0, 64], [256, 3], [1, 128]])
                P.dma("sp", bsb[h2 * 64:(h2 + 1) * 64, :, :], src_, writes=["bsb"], semkey="bsb")
        if has_s:
            for h2 in range(2):
                src_ = bass.AP(bs_t, W["b_s"][l].offset + h2 * 128, [[0, 64], [256, 3], [1, 64]])
                for r in range(2):
                    P.dma("sp", bsbs[h2 * 64:(h2 + 1) * 64, :, r * 64:(r + 1) * 64], src_,
                          writes=["bsbs"], semkey="bsbs")

    def finish_params(tile, l):
        has_p = any(s["kind"] == "P" for s in tile["subs"])
        has_s = any(s["kind"] == "S" for s in tile["subs"])
        trt, trk = trbank(rot_tr.next())
        for c in range(3):
            P.pe(tp(trt[:, c * 32:c * 32 + KA], stA[0:KA, c * 128:(c + 1) * 128], identF[0:KA, 0:KA]),
                 reads=["stA", "identF"], writes=[trk])
        P.act(actf(cwa[:, :, :], trt[:, 0:96].rearrange("p (c k) -> p c k", c=3)[:, :, 0:KA], AF.Copy),
              reads=[trk], writes=["cwa"])
        trt, trk = trbank(rot_tr.next())
        for c in range(3):
            P.pe(tp(trt[:, c * 32:c * 32 + 3], stB[0:3, c * 128:(c + 1) * 128], identF[0:3, 0:3]),
                 reads=["stB", "identF"], writes=[trk])
        P.act(actf(pva[:, :, :], trt[:, 0:96].rearrange("p (c k) -> p c k", c=3)[:, :, 0:3], AF.Copy),
              reads=[trk], writes=["pva"])
        trt, trk = trbank(rot_tr.next())
        for c in range(2):
            P.pe(tp(trt[:, c * 32:c * 32 + KC], stC[0:KC, c * 128:(c + 1) * 128], identF[0:KC, 0:KC]),
                 reads=["stC", "identF"], writes=[trk])
        P.act(actf(cwc[:, :, :], trt[:, 0:64].rearrange("p (c k) -> p c k", c=2)[:, :, 0:KC], AF.Copy),
              reads=[trk], writes=["cwc"])
        for c in range(3):
            cw = cwa[:, c, :]
            i0 = bass.AP(Iq, 0, [list(Iq[:, :].ap[0]), [0, KA], [1, 32]])
            i1 = bass.AP(cwa, cw.offset, [list(cw.ap[0]), [1, KA], [0, 32]])
            P.dve(tt(dq[:, c, :, :], i0, i1, ALU.mult), reads=["Iq", "cwa"], writes=dq_keys)
        if has_p:
            for h in range(6):
                trt, trk = trbank(rot_tr.next())
                P.pe(tp(trt[:, 0:128], Wn[:, h, :], identF[:, :]), reads=["Wn", "identF"], writes=[trk])
                P.dve(tt(WT[:, h, :], trt[:, 0:128], maskT[:, :], ALU.mult), reads=[trk, "maskT"], writes=["WT"])
        if has_s:
            P.dve(mset(Wn[0:64, :, 64:128], 0.0), reads=["Wn"], writes=["Wn"])
            P.dve(mset(Wn[64:128, :, 0:64], 0.0), reads=["Wn"], writes=["Wn"])
            for q in range(2):
                P.dma("sp", Wn[q * 64:(q + 1) * 64, :, q * 64:(q + 1) * 64],
                      W["w_s"][l][:, 0:64, 0:64].rearrange("h i j -> i h j"), writes=["Wn"], semkey="Wn")
            for h in range(6):
                trt, trk = trbank(rot_tr.next())
                P.pe(tp(trt[:, 0:128], Wn[:, h, :], identF[:, :]), reads=["Wn", "identF"], writes=[trk])
                P.act(actf(WTs[:, h, :], trt[:, 0:128], AF.Copy), reads=[trk], writes=["WTs"])

    def hist_fill_and_state(st, l, ext, nch, hl, segs, hist, cache, cst, stg, out_p, out_s, kname, kext):
        if st["kind"] == "P":
            P.act(actf(ext[:, :, 0:hl], hist[:, l, :, :], AF.Copy), reads=[(kname, l)], writes=[kext + ("h",)])
        else:
            for s in range(2):
                cstt, cstk = cst
                P.dma("sp", cstt[0:hl, :], cache[l, s], writes=[cstk], semkey=cstk)
                j = rot_tr.next()
                trt, trk = trbank(j)
                for c in range(nch):
                    P.pe(tp(trt[:, c * 32:c * 32 + hl], cstt[0:hl, c * 128:(c + 1) * 128], identF[0:hl, 0:hl]),
                         reads=[cstk, "identF"], writes=[trk])
                eo = segs[s][0]
                P.act(actf(ext[:, :, eo:eo + hl], trt[:, 0:nch * 32].rearrange("p (c k) -> p c k", c=nch)[:, :, 0:hl],
                           AF.Copy), reads=[trk], writes=[kext + ("h",)])

    def state_out(st, l, ext, nch, hl, segs, hist, stg, out_p, out_s, kname, kext):
        rk = [kext + ("h",), kext + ("d",)]
        if st["kind"] == "P":
            eo, _, ntok = segs[0]
            src = ext[:, :, eo + ntok:eo + ntok + hl]
            if st["mask_after"]:
                P.act(actf(hist[:, l, :, :], src, AF.Copy, scale=maskc[:, 0:1]), reads=rk + ["maskc"],
                      writes=[(kname, l)])
            else:
                P.act(actf(hist[:, l, :, :], src, AF.Copy), reads=rk, writes=[(kname, l)])
            outs = [(src, out_p[l])] if st["final"] else []
        else:
            outs = []
            for s in range(2):
                eo, _, ntok = segs[s]
                outs.append((ext[:, :, eo + ntok:eo + ntok + hl], out_s[l, s]))
        for src, dst in outs:
            j = rot_tr.next()
            trt, trk = trbank(j)
            for c in range(nch):
                P.pe(tp(trt[0:hl, c * 128:(c + 1) * 128], src[:, c, :], identF[:, :]), reads=rk + ["identF"],
                     writes=[trk])
            stgt, stgk = stg
            P.act(actf(stgt[0:hl, :], trt[0:hl, 0:nch * 128], AF.Copy), reads=[trk], writes=[stgk])
            P.dma("sp", dst, stgt[0:hl, :], reads=[stgk], semkey=stgk)

    def mixer(tile, l):
        subs = tile["subs"]
        nb = tile["nb"]
        epsp = EPS / (ALPHA * ALPHA)
        LN_NEXT[0] = ("ln2_g", "ln2_b", l)
        if not LNP.live:
            ln_tick()
        fence()
        sa = ScrAlloc()
        o_aext = sa.take((3 * NA + 1) // 2)
        o_tail = sa.take(3 * 2 * 30)
        o_co = sa.take(3 * (NMAX + 60))
        o_f = [sa.take(NMAX + 60) for _ in range(5)]
        aext = scr[:, o_aext:o_aext + (3 * NA + 1) // 2].bitcast(BF16)[:, 0:3 * NA].rearrange("p (c k) -> p c k", c=3)
        tail = scr[:, o_tail:o_tail + 180].rearrange("p (c s k) -> p c s k", c=3, s=2)
        co = scr[:, o_co:o_co + 3 * (NMAX + 60)].rearrange("p (c k) -> p c k", c=3)
        fbuf = [scr[:, o:o + NMAX + 60] for o in o_f]
        fkey = lambda i: ("scr", "A", "f", i)
        wp, wpk = w_acquire("mix")
        wg, wgk = w_acquire("mix")
        rot_s3 = Rot(3)
        kp = rot_pair.i
        border = [2 * ((kp + d) % 3) + h for d in range(3) for h in range(2)]
        proj_banks, conv_banks = border[0:3], border[3:6]
        kext = ("scr", "A", "aext")
        ktl = ("scr", "A", "tail")
        kco = ("scr", "A", "co")
        def ma_body(st):
            c0, n, b0, nblk = st["c0"], st["n"], st["b0"], st["nblk"]
            need_xT(range(b0, b0 + nblk))
            xkeys = [("xT", b0 + i) for i in range(nblk)]
            segs, na = seg_layout(st, 30)
            nco = na - 30
            is_s = st["kind"] == "S"
            if not is_s:
                P.act(actf(aext[:, :, 0:30], hist_a[:, l, :, :], AF.Copy), reads=[("hist_a", l)],
                      writes=[kext + ("h",)])
            else:
                for s in range(2):
                    P.dma("sp", stA[0:30, :], cache_a[l, s], writes=["stA"], semkey="stA")
                    trt, trk = trbank(rot_tr.next())
                    for c in range(3):
                        P.pe(tp(trt[:, c * 32:c * 32 + 30], stA[0:30, c * 128:(c + 1) * 128], identF[0:30, 0:30]),
                             reads=["stA", "identF"], writes=[trk])
                    eo = segs[s][0]
                    P.act(actf(aext[:, :, eo:eo + 30],
                               trt[:, 0:96].rearrange("p (c k) -> p c k", c=3)[:, :, 0:30], AF.Copy),
                          reads=[trk], writes=[kext + ("h",)])
            for c in range(3):
                gi = proj_banks[rot_s3.next()]
                gt, gk = single(gi)
                for kc in range(8):
                    P.pe(mm(gt[:, 0:n], wg[:, kc, c * 128:(c + 1) * 128], xT[:, kc, c0:c0 + n], kc == 0, kc == 7),
                         reads=[wgk] + xkeys, writes=[gk])
                sgi = c % 2
                P.act(actf(fbuf[sgi][:, 0:n], gt[:, 0:n], AF.Sigmoid), reads=[gk], writes=[fkey(sgi)])
                pi = proj_banks[rot_s3.next()]
                ptt, ppk = single(pi)
                for kc in range(8):
                    P.pe(mm(ptt[:, 0:n], wp[:, kc, c * 128:(c + 1) * 128], xT[:, kc, c0:c0 + n], kc == 0, kc == 7),
                         reads=[wpk] + xkeys, writes=[ppk])
                ln_tick()
                for si, (eo, to, ntok) in enumerate(segs):
                    P.dve(tt(aext[:, c, eo + 30:eo + 30 + ntok], ptt[:, to:to + ntok], fbuf[sgi][:, to:to + ntok],
                             ALU.mult), reads=[ppk, fkey(sgi)], writes=[kext + ("d",)])
                    P.dve(tt(tail[:, c, si, :], ptt[:, to + ntok - 30:to + ntok], fbuf[sgi][:, to + ntok - 30:to + ntok],
                             ALU.mult), reads=[ppk, fkey(sgi)], writes=[ktl])
            if not is_s:
                if st["mask_after"]:
                    P.act(actf(hist_a[:, l, :, :], tail[:, :, 0, :], AF.Copy, scale=maskc[:, 0:1]),
                          reads=[ktl, "maskc"], writes=[("hist_a", l)])
                else:
                    P.act(actf(hist_a[:, l, :, :], tail[:, :, 0, :], AF.Copy), reads=[ktl], writes=[("hist_a", l)])
                outs = [(0, st_a_p[l])] if st["final"] else []
            else:
                outs = [(s, st_a_s[l, s]) for s in range(2)]
            for si, dst in outs:
                trt, trk = trbank(rot_tr.next())
                for c in range(3):
                    P.pe(tp(trt[0:30, c * 128:(c + 1) * 128], tail[:, c, si, :], identF[:, :]), reads=[ktl, "identF"],
                         writes=[trk])
                P.act(actf(stB[0:30, :], trt[0:30, 0:384], AF.Copy), reads=[trk], writes=["stB"])
                P.dma("sp", dst, stB[0:30, :], reads=["stB"], semkey="stB")
            yield
            cbank = [single(conv_banks[c]) for c in range(3)]
            for c in range(3):
                cb, cbk = cbank[c]
                for k in range(KA):
                    for q in range(4):
                        P.pe(lambda e, cb=cb, c=c, k=k, q=q, nco=nco: e.matmul(
                            cb[32 * q:32 * q + 32, 0:nco], dq[32 * q:32 * q + 32, c, k, :],
                            aext[32 * q:32 * q + 32, c, k:k + nco], start=(k == 0), stop=(k == KA - 1),
                            tile_position=(32 * q, 32 * q)),
                            reads=[kext + ("h",), kext + ("d",)] + dq_keys, writes=[cbk])
            yield
            s1t, s1k = trbank(0)
            s2t, s2k = trbank(1)
            for c in range(3):
                cb, cbk = cbank[c]
                sqi = 1 + c % 2
                P.act(actf(co[:, c, 0:nco], cb[:, 0:nco], AF.Identity, bias=pva[:, c, 0:1]), reads=[cbk, "pva"],
                      writes=[kco + (c,)])
                P.act(actf(fbuf[sqi][:, 0:nco], cb[:, 0:nco], AF.Square, bias=pva[:, c, 0:1]), reads=[cbk, "pva"],
                      writes=[fkey(sqi)])
                P.pe(mm(s1t[:, 0:nco], onesF[:, :], co[:, c, 0:nco], c == 0, c == 2),
                     reads=["onesF", kco + (c,)], writes=[s1k])
                P.pe(mm(s2t[:, 0:nco], onesF[:, :], fbuf[sqi][:, 0:nco], c == 0, c == 2),
                     reads=["onesF", fkey(sqi)], writes=[s2k])
            mt, rt = fbuf[3], fbuf[4]
            tm = fbuf[0]
            KM, KR = fkey(3), fkey(4)
            P.dve(ts(mt[:, 0:nco], s1t[:, 0:nco], 1.0 / DA, None, ALU.mult), reads=[s1k], writes=[KM])
            P.dve(tt(tm[:, 0:nco], mt[:, 0:nco], mt[:, 0:nco], ALU.mult), reads=[KM], writes=[fkey(0)])
            P.dve(stt(tm[:, 0:nco], s2t[:, 0:nco], 1.0 / DA, tm[:, 0:nco], ALU.mult, ALU.subtract),
                  reads=[s2k, fkey(0)], writes=[fkey(0)])
            P.act(actf(rt[:, 0:nco], tm[:, 0:nco], AF.Sqrt, bias=epsb[:, 0:1]), reads=[fkey(0), "epsb"],
                  writes=[KR])
            P.dve(lambda e, rt=rt, nco=nco: e.reciprocal(rt[:, 0:nco], rt[:, 0:nco]), reads=[KR], writes=[KR])
            for c in range(3):
                P.dve(tt(co[:, c, 0:nco], co[:, c, 0:nco], mt[:, 0:nco], ALU.subtract), reads=[kco + (c,), KM],
                      writes=[kco + (c,)])
                P.dve(tt(co[:, c, 0:nco], co[:, c, 0:nco], rt[:, 0:nco], ALU.mult), reads=[kco + (c,), KR],
                      writes=[kco + (c,)])
                for (eo, to, ntok) in segs:
                    P.act(actf(hT[:, c, c0 + to:c0 + to + ntok], co[:, c, eo:eo + ntok], AF.Silu,
                               bias=pva[:, c, 2:3], scale=pva[:, c, 1:2]),
                          reads=[kco + (c,), "pva"], writes=[("hT", c, b0 + i) for i in range(nblk)])
        gens = [ma_body(st) for st in subs]
        step = lambda g: next(g, None)
        step(gens[0])
        step(gens[0])
        for s in range(1, len(gens)):
            step(gens[s])
            step(gens[s - 1])
            step(gens[s])
        step(gens[-1])
        w_release(2)
        fence()
        sa = ScrAlloc()
        o_u = sa.take(3 * NMAX)
        o_v32 = [sa.take(DB) for _ in range(3)]
        o_vbf = sa.take(4 * DB // 2)
        o_tb = [sa.take(NMAX) for _ in range(2)]
        uT = scr[:, o_u:o_u + 3 * NMAX].rearrange("p (c k) -> p c k", c=3)
        vn32 = [scr[:, o:o + DB] for o in o_v32]
        vnbf = scr[:, o_vbf:o_vbf + 4 * DB // 2].bitcast(BF16).rearrange("p (b k) -> p b k", b=4)
        tb = [scr[:, o:o + NMAX] for o in o_tb]
        wu, wuk = w_acquire("mix")
        wvv, wvk = w_acquire("mix")
        rot_v = Rot(3)
        rot_v32 = Rot(3)
        for st in subs:
            c0, n, b0, nblk = st["c0"], st["n"], st["b0"], st["nblk"]
            need_xT(range(b0, b0 + nblk))
            xkeys = [("xT", b0 + i) for i in range(nblk)]
            is_s = st["kind"] == "S"
            sbank = [single(c) for c in range(3)]
            wt_sb, wt_k = (WTs, "WTs") if is_s else (WT, "WT")
            pend_s = []

            def emit_s(bi, kvb, sbank=sbank, wt_sb=wt_sb, wt_k=wt_k):
                for c in range(3):
                    stt_, sk = sbank[c]
                    for h2 in range(2):
                        h = 2 * c + h2
                        P.pe(mm(stt_[h2 * 64:(h2 + 1) * 64, bi * 128:(bi + 1) * 128], vnbf[:, bi, h * 64:(h + 1) * 64],
                                wt_sb[:, h, :], True, True), reads=[kvb, wt_k], writes=[sk])

            def mb_chain(bi, b0=b0, is_s=is_s, emit_s=emit_s):
                b = b0 + bi
                vi = 3 + rot_v.next()
                vt, vk = single(vi)
                vbanks[bi] = (vt, vk)
                for kc in range(8):
                    P.pe(mm(vt[:, 0:DB], xT[:, kc, b * 128:(b + 1) * 128], wvv[:, kc, 0:DB], kc == 0, kc == 7),
                         reads=[wvk, ("xT", b)], writes=[vk])
                i = rot_small.next()
                sm = small[:, i, :]
                ks = ("small", i)
                P.dve(lambda e: e.bn_stats(sm[:, 0:6], vt[:, 0:DB]), reads=[vk], writes=[ks])
                P.dve(lambda e: e.bn_aggr(sm[:, 12:14], sm[:, 0:6]), reads=[ks], writes=[ks])
                P.act(actf(sm[:, 14:15], sm[:, 13:14], AF.Sqrt, bias=epsb[:, 0:1]), reads=[ks, "epsb"], writes=[ks])
                yield
                P.dve(lambda e: e.reciprocal(sm[:, 14:15], sm[:, 14:15]), reads=[ks], writes=[ks])
                P.dve(stt(sm[:, 15:16], sm[:, 12:13], -1.0, sm[:, 14:15], ALU.mult, ALU.mult), reads=[ks], writes=[ks])
                q = rot_v32.next()
                kv = ("scr", "B", "vn32", q)
                P.act(actf(vn32[q][:, :], vt[:, 0:DB], AF.Identity, bias=sm[:, 15:16], scale=sm[:, 14:15]),
                      reads=[vk, ks], writes=[kv])
                yield
                P.dve(tt(vn32[q][:, :], vn32[q][:, :], nbg[:, :], ALU.mult), reads=[kv, "nbg"], writes=[kv])
                kvb = ("scr", "B", "vnbf", bi)
                if is_s:
                    P.dve(tt(vn32[q][:, :], vn32[q][:, :], nbb[:, :], ALU.add), reads=[kv, "nbb"], writes=[kv])
                    P.dma("sp", st_v_s[l], vn32[q][:, :], reads=[kv], semkey=kv)
                    P.act(actf(vnbf[:, bi, :], vn32[q][:, :], AF.Copy), reads=[kv], writes=[kvb])
                else:
                    P.dve(tt(vnbf[:, bi, :], vn32[q][:, :], nbb[:, :], ALU.add), reads=[kv, "nbb"], writes=[kvb])
                yield
                emit_s(bi, kvb)

            mbp = Pipe()
            vbanks = {}
            assert nblk <= 3
            for bi in range(nblk):
                mbp.push(mb_chain(bi))
            if nblk >= 2:
                ubanks = [trbank(0), trbank(1), vbanks[0]]
            else:
                ubanks = sbank
            for c in range(3):
                ut, uk = ubanks[c]
                for kc in range(8):
                    P.pe(mm(ut[:, 0:n], wu[:, kc, c * 128:(c + 1) * 128], xT[:, kc, c0:c0 + n], kc == 0, kc == 7),
                         reads=[wuk] + xkeys, writes=[uk])
                if nblk < 2:
                    P.act(actf(uT[:, c, 0:n], ut[:, 0:n], AF.Copy), reads=[uk], writes=[("scr", "B", "uT", c)])
            mbp.drain()
            if nblk >= 2:
                for c in range(3):
                    ut, uk = ubanks[c]
                    P.act(actf(uT[:, c, 0:n], ut[:, 0:n], AF.Copy), reads=[uk], writes=[("scr", "B", "uT", c)])
            bs_sb, bs_k = (bsbs, "bsbs") if is_s else (bsb, "bsb")
            for c in range(3):
                stt_, sk = sbank[c]
                ti = c % 2
                ktb = ("scr", "B", "tb", ti)
                bsrc = bs_sb[:, c, :]
                bb = bass.AP(bs_sb, bsrc.offset, [list(bsrc.ap[0]), [0, nblk], [1, 128]])
                P.dve(tt(tb[ti][:, 0:n].rearrange("p (b i) -> p b i", b=nblk),
                         stt_[:, 0:n].rearrange("p (b i) -> p b i", b=nblk), bb, ALU.add),
                      reads=[sk, bs_k], writes=[ktb])
                P.dve(tt(hT[:, 3 + c, c0:c0 + n], tb[ti][:, 0:n], uT[:, c, 0:n], ALU.mult),
                      reads=[ktb, ("scr", "B", "uT", c)], writes=[("hT", 3 + c, b0 + i) for i in range(nblk)])
        w_release(2)
        fence()
        sa = ScrAlloc()
        o_g = [sa.take(NMAX) for _ in range(2)]
        NCX = NMAX + 8
        o_cx = sa.take(2 * NCX)
        o_gb = sa.take(2 * NMAX)
        o_ca = sa.take(2 * NCX)
        gct = [scr[:, o:o + NMAX] for o in o_g]
        cxext = scr[:, o_cx:o_cx + 2 * NCX].rearrange("p (c k) -> p c k", c=2)
        gbt = scr[:, o_gb:o_gb + 2 * NMAX].rearrange("p (c k) -> p c k", c=2)
        cacc = scr[:, o_ca:o_ca + 2 * NCX].rearrange("p (c k) -> p c k", c=2)
        wc0, wc0k = w_acquire("mix")
        wc1, wc1k = w_acquire("mix")
        rot_s6 = Rot(6)
        for st in subs:
            c0, n, b0, nblk = st["c0"], st["n"], st["b0"], st["nblk"]
            need_xT(range(b0, b0 + nblk))
            xkeys = [("xT", b0 + i) for i in range(nblk)]
            segs, nx = seg_layout(st, 2)
            nco = nx - 2
            kext = ("scr", "C", "cxext")
            hist_fill_and_state(st, l, cxext, 2, 2, segs, hist_c, cache_c, (stC, "stC"), None, st_c_p, st_c_s, "hist_c", kext)
            for c in range(2):
                gi = rot_s6.next()
                gt, gk = single(gi)
                for kc in range(8):
                    P.pe(mm(gt[:, 0:n], wc1[:, kc, c * 128:(c + 1) * 128], xT[:, kc, c0:c0 + n], kc == 0, kc == 7),
                         reads=[wc1k] + xkeys, writes=[gk])
                kg = ("scr", "C", "gct", c)
                P.act(actf(gct[c][:, 0:n], gt[:, 0:n], AF.Copy), reads=[gk], writes=[kg])
                xi = rot_s6.next()
                xt_, xk = single(xi)
                for kc in range(8):
                    P.pe(mm(xt_[:, 0:n], wc0[:, kc, c * 128:(c + 1) * 128], xT[:, kc, c0:c0 + n], kc == 0, kc == 7),
                         reads=[wc0k] + xkeys, writes=[xk])
                for (eo, to, ntok) in segs:
                    P.dve(tt(cxext[:, c, eo + 2:eo + 2 + ntok], xt_[:, to:to + ntok], gct[c][:, to:to + ntok], ALU.mult),
                          reads=[xk, kg], writes=[kext + ("d",)])
                bi_ = rot_s6.next()
                bt, bk = single(bi_)
                for kc in range(8):
                    P.pe(mm(bt[:, 0:n], wc0[:, kc, 256 + c * 128:256 + (c + 1) * 128], xT[:, kc, c0:c0 + n],
                            kc == 0, kc == 7), reads=[wc0k] + xkeys, writes=[bk])
                P.act(actf(gbt[:, c, 0:n], bt[:, 0:n], AF.Copy), reads=[bk], writes=[("scr", "C", "gb", c)])
            state_out(st, l, cxext, 2, 2, segs, hist_c, (stD, "stD"), st_c_p, st_c_s, "hist_c", kext)
            for c in range(2):
                kca = ("scr", "C", "cacc", c)
                rk = [kext + ("h",), kext + ("d",), "cwc"]
                P.dve(ts(cacc[:, c, 0:nco], cxext[:, c, 0:nco], cwc[:, c, 0:1], None, ALU.mult), reads=rk, writes=[kca])
                for k in range(1, KC):
                    P.dve(stt(cacc[:, c, 0:nco], cxext[:, c, k:k + nco], cwc[:, c, k:k + 1], cacc[:, c, 0:nco],
                              ALU.mult, ALU.add), reads=rk + [kca], writes=[kca])
                for (eo, to, ntok) in segs:
                    P.dve(tt(hT[:, 6 + c, c0 + to:c0 + to + ntok], cacc[:, c, eo:eo + ntok], gbt[:, c, to:to + ntok],
                             ALU.mult), reads=[kca, ("scr", "C", "gb", c)],
                          writes=[("hT", 6 + c, b0 + i) for i in range(nblk)])
        w_release(2)
        wo = [w_acquire("mo") for _ in range(2)]
        ln_params_ready()
        for b in range(nb):
            pi = rot_pair.next()
            pt, pk = pair(pi)
            for hc in range(2):
                wv, wk = wo[hc]
                for kc in range(8):
                    P.pe(mm(pt[:, hc * 512:(hc + 1) * 512], hT[:, kc, b * 128:(b + 1) * 128], wv[:, kc, :],
                            kc == 0, kc == 7), reads=[wk, ("hT", kc, b)], writes=[pk[hc]])
            sm, ks = ln_slot()
            P.dve(lambda e, b=b, pt=pt, sm=sm: e.scalar_tensor_tensor(
                xres[:, b, :], pt[:, :], 1.0 / ALPHA, xres[:, b, :], ALU.mult, ALU.add, accum_out=sm[:, 0:1]),
                reads=[pk[0], pk[1], ("xres", b)], writes=[("xres", b), ks])
            LNP.push(ln_chain(b, epsp, True, None, sm, ks))
        w_release(2)

    for tile in cfg["tiles"]:
        nb = tile["nb"]
        for b in range(nb):
            gb = tile["gb0"] + b
            P.dma("sp", xres[:, b, :], xin[gb * 128:(gb + 1) * 128, :], writes=[("xres", b)], semkey=("xres", b))
        for b in range(nb):
            cast_x(b)
            flush_xT(1)
        flush_xT(0)
        for l in range(L):
            ffn(tile, l, 1, last=False)
            mixer(tile, l)
            ffn(tile, l, 2, last=(l == L - 1))
    assert WS.acquired == len(units) and WS.released == len(units)

    P.emit(nc, es)
    es.close()
    return nc, P


WEIGHT_NAMES = ["w_ffn1_in", "w_ffn1_out", "ln1_g", "ln1_b", "w_in", "conv_a_w", "conv_a_b", "norm_a_g",
                "norm_a_b", "norm_b_g", "norm_b_b", "w_s", "b_s", "conv_c_w", "w_out", "ln2_g", "ln2_b",
                "w_ffn2_in", "w_ffn2_out", "ln3_g", "ln3_b"]


def const_inputs():
    ident = np.eye(128, dtype=np.float32)
    j = np.arange(128)[:, None]
    i = np.arange(128)[None, :]
    maskT = ((j // 64) <= (i // 64)).astype(np.float32)
    return ident, maskT


def full_cfg():
    P = lambda n, **kw: dict(kind="P", nblk=n, **kw)
    tiles = [
        [P(3), P(3), P(3)],
        [P(3), P(3), P(3)],
        [P(3), P(3), P(3)],
        [P(3), P(3), P(1, final=True), dict(kind="S")],
    ]
    return make_cfg(4, tiles, ring=7)


NPB = 34
SPLIT = NPB * 128


_CACHE = {}


def kernel(**inputs):
    cfg = full_cfg()
    if "nc" not in _CACHE:
        _CACHE["nc"] = build_program(cfg)[0]
    nc = _CACHE["nc"]
    x_prompt = np.asarray(inputs["x_prompt"], dtype=np.float32)
    x_sample = np.asarray(inputs["x_sample"], dtype=np.float32)
    ca = np.asarray(inputs["cache_conv_a"], dtype=np.float32)
    cc = np.asarray(inputs["cache_conv_c"], dtype=np.float32)
    ident, maskT = const_inputs()
    wts = {k: np.ascontiguousarray(np.asarray(inputs[k], dtype=np.float32)) for k in WEIGHT_NAMES}
    START1 = 8192 - SPLIT
    assert SPLIT - START1 >= 384 and START1 % 128 == 0
    in_maps = []
    for c in range(8):
        seq, half = c // 2, c % 2
        xp = x_prompt[seq, 0:SPLIT] if half == 0 else x_prompt[seq, START1:8192]
        xs = x_sample[2 * c:2 * c + 2].reshape(128, D)
        m = dict(wts)
        m["xin"] = np.ascontiguousarray(np.concatenate([xp, xs], axis=0))
        m["cache_a"] = np.ascontiguousarray(ca[:, 2 * c:2 * c + 2])
        m["cache_c"] = np.ascontiguousarray(cc[:, 2 * c:2 * c + 2])
        m["maskin"] = np.ones((128, 1), np.float32)
        m["maskT"] = maskT
        m["identin"] = ident
        in_maps.append(m)
    res = run_bass_kernel_spmd(nc, in_maps, core_ids=list(range(8)))
    r = res.results
    y_prompt = np.empty((4, 8192, D), np.float32)
    y_sample = np.empty((16, 64, D), np.float32)
    st_a_p = np.empty((4, 4, 30, DA), np.float32)
    st_c_p = np.empty((4, 4, 2, DC), np.float32)
    st_a_s = np.empty((4, 16, 30, DA), np.float32)
    st_c_s = np.empty((4, 16, 2, DC), np.float32)
    st_v_s = np.empty((4, 16, 64, DB), np.float32)
    for c in range(8):
        seq, half = c // 2, c % 2
        yo = r[c]["yout"]
        if half == 0:
            y_prompt[seq, 0:SPLIT] = yo[0:SPLIT]
        else:
            y_prompt[seq, SPLIT:8192] = yo[SPLIT - START1:SPLIT]
        y_sample[2 * c:2 * c + 2] = yo[SPLIT:SPLIT + 128].reshape(2, 64, D)
        if half == 1:
            st_a_p[:, seq] = r[c]["st_a_p"]
            st_c_p[:, seq] = r[c]["st_c_p"]
        st_a_s[:, 2 * c:2 * c + 2] = r[c]["st_a_s"]
        st_c_s[:, 2 * c:2 * c + 2] = r[c]["st_c_s"]
        st_v_s[:, 2 * c:2 * c + 2] = r[c]["st_v_s"].reshape(4, 2, 64, DB)
    return (y_prompt, y_sample, st_a_p, st_c_p, st_a_s, st_c_s, st_v_s)
```

```python
import math
from contextlib import ExitStack

import numpy as np
import concourse.bass as bass
import concourse.mybir as mybir
from concourse.bass_utils import run_bass_kernel_spmd

F32 = mybir.dt.float32
BF16 = mybir.dt.bfloat16
AF = mybir.ActivationFunctionType
ALU = mybir.AluOpType

D = 1024
DFF = 2816
DA = 384
DB = 384
DC = 256
DIN = 2304
NFC = 22
NFH = 11
ALPHA = 8.0 ** 0.25
EPS = 1e-5
KA = 31
KC = 3

ENGS = ("pe", "act", "dve", "pool", "sp")
EPOCH = 20000
XT_DEFER = 2
PA_HOLD = 2


class Op:
    __slots__ = ("eng", "idx", "fn", "deps", "is_dma", "semkey", "dcount", "signal", "seq")


class Prog:
    def __init__(self):
        self.ops = {e: [] for e in ENGS}
        self.lastw = {}
        self.readers = {}
        self.dma_counts = {}

    def _add(self, eng, fn, reads, writes, is_dma=False, semkey=None):
        op = Op()
        op.eng = eng
        op.idx = len(self.ops[eng])
        op.fn = fn
        op.is_dma = is_dma
        op.semkey = semkey
        op.signal = False
        op.seq = 0
        op.dcount = 0
        if any(isinstance(k, tuple) and k[0] == "scr" for k in list(reads) + list(writes)):
            reads = list(reads) + ["scrfence"]
        deps = {}
        for k in reads:
            w = self.lastw.get(k)
            if w is not None:
                deps[id(w)] = (w, "raw")
        for k in writes:
            w = self.lastw.get(k)
            if w is not None and id(w) not in deps:
                deps[id(w)] = (w, "waw")
            rd = self.readers.get(k)
            if rd is not None:
                for r in rd[0].values():
                    if id(r) not in deps:
                        deps[id(r)] = (r, "war")
                for r in rd[1]:
                    if id(r) not in deps:
                        deps[id(r)] = (r, "war")
        final = []
        for p, kind in deps.values():
            if p.is_dma:
                need = True
            elif p.eng != eng:
                need = True
            elif is_dma:
                need = True
            else:
                need = eng != "pe"
            if need:
                if not p.is_dma:
                    p.signal = True
                final.append(p)
        op.deps = final
        for k in reads:
            rd = self.readers.get(k)
            if rd is None:
                rd = ({}, [])
                self.readers[k] = rd
            if is_dma:
                rd[1].append(op)
            else:
                rd[0][eng] = op
        for k in writes:
            self.lastw[k] = op
            self.readers[k] = ({}, [])
        if is_dma:
            c = self.dma_counts.get(semkey, 0) + 1
            self.dma_counts[semkey] = c
            op.dcount = c
        self.ops[eng].append(op)
        return op

    def pe(self, fn, reads=(), writes=()):
        return self._add("pe", fn, reads, writes)

    def act(self, fn, reads=(), writes=()):
        return self._add("act", fn, reads, writes)

    def dve(self, fn, reads=(), writes=()):
        return self._add("dve", fn, reads, writes)

    def pool(self, fn, reads=(), writes=()):
        return self._add("pool", fn, reads, writes)

    def dma(self, eng, out_ap, in_ap, reads=(), writes=(), semkey=None):
        def fn(e, out_ap=out_ap, in_ap=in_ap):
            return e.dma_start(out=out_ap, in_=in_ap)
        return self._add(eng, fn, reads, writes, is_dma=True, semkey=semkey)

    def emit(self, nc, es):
        nsig = {}
        for eng in ENGS:
            cnt = 0
            for op in self.ops[eng]:
                if op.signal and not op.is_dma:
                    cnt += 1
                    op.seq = cnt
            nsig[eng] = cnt
        eng_sems = {}
        for eng in ENGS:
            n = max(1, math.ceil(nsig[eng] / EPOCH))
            eng_sems[eng] = [es.enter_context(nc.semaphore(f"s_{eng}_{i}")) for i in range(n)]
        dma_sems = {}
        for i, k in enumerate(self.dma_counts):
            dma_sems[k] = es.enter_context(nc.semaphore(f"d_{i}"))

        def resolve(p):
            if p.is_dma:
                return dma_sems[p.semkey], 16 * p.dcount
            ep = (p.seq - 1) // EPOCH
            return eng_sems[p.eng][ep], p.seq - ep * EPOCH

        def run(eng, e):
            waited = {}
            for op in self.ops[eng]:
                for p in op.deps:
                    sem, val = resolve(p)
                    if waited.get(id(sem), 0) < val:
                        e.wait_ge(sem, val)
                        waited[id(sem)] = val
                ins = op.fn(e)
                if op.is_dma:
                    ins.then_inc(dma_sems[op.semkey], 16)
                elif op.signal:
                    ep = (op.seq - 1) // EPOCH
                    ins.then_inc(eng_sems[eng][ep], 1)
            if eng == "sp":
                for k, c in self.dma_counts.items():
                    e.wait_ge(dma_sems[k], 16 * c)

        with nc.Block() as block:
            @block.tensor
            def _(e):
                run("pe", e)

            @block.scalar
            def _(e):
                run("act", e)

            @block.vector
            def _(e):
                run("dve", e)

            @block.gpsimd
            def _(e):
                run("pool", e)

            @block.sync
            def _(e):
                run("sp", e)


def make_cfg(L, tiles, ring=7):
    nb_max = 0
    gb = 0
    out = []
    n_s = 0
    for t in tiles:
        tt = []
        b = 0
        for st in t:
            d = dict(kind=st["kind"], nblk=st.get("nblk", 1), mask_after=st.get("mask_after", False),
                     final=st.get("final", False), b0=b)
            if d["kind"] == "S":
                d["nblk"] = 1
                n_s += 1
            d["n"] = d["nblk"] * 128
            d["c0"] = b * 128
            b += d["nblk"]
            tt.append(d)
        out.append(dict(subs=tt, nb=b, gb0=gb))
        gb += b
        nb_max = max(nb_max, b)
    nmax = max(st["n"] for t in out for st in t["subs"])
    assert n_s <= 1
    return dict(L=L, tiles=out, NB=nb_max, NBLK=gb, RING=ring, NMAX=nmax, has_s=(n_s == 1))


def build_program(cfg):
    L = cfg["L"]
    NB = cfg["NB"]
    NBLK = cfg["NBLK"]
    RING = cfg["RING"]
    NMAX = cfg["NMAX"]
    NT = NB * 128
    nc = bass.Bass("TRN2", target_bir_lowering=False)
    P = Prog()
    es = ExitStack()

    def din(name, shape):
        return nc.dram_tensor(name, list(shape), F32, kind="ExternalInput").ap()

    def dout(name, shape):
        return nc.dram_tensor(name, list(shape), F32, kind="ExternalOutput").ap()

    xin = din("xin", [NBLK * 128, D])
    cache_a = din("cache_a", [L, 2, 30, DA])
    cache_c = din("cache_c", [L, 2, 2, DC])
    maskin = din("maskin", [128, 1])
    maskT_in = din("maskT", [128, 128])
    W = {}
    for nm, shp in (("w_ffn1_in", [L, D, 2 * DFF]), ("w_ffn1_out", [L, DFF, D]), ("ln1_g", [L, D]),
                    ("ln1_b", [L, D]), ("w_in", [L, D, DIN]), ("conv_a_w", [L, KA, DA]),
                    ("conv_a_b", [L, DA]), ("norm_a_g", [L, DA]), ("norm_a_b", [L, DA]),
                    ("norm_b_g", [L, DB]), ("norm_b_b", [L, DB]), ("w_s", [L, 6, 128, 128]),
                    ("b_s", [L, 6, 128]), ("conv_c_w", [L, KC, DC]), ("w_out", [L, D, D]),
                    ("ln2_g", [L, D]), ("ln2_b", [L, D]), ("w_ffn2_in", [L, D, 2 * DFF]),
                    ("w_ffn2_out", [L, DFF, D]), ("ln3_g", [L, D]), ("ln3_b", [L, D])):
        W[nm] = din(nm, shp)
    yout = dout("yout", [NBLK * 128, D])
    st_a_p = dout("st_a_p", [L, 30, DA])
    st_c_p = dout("st_c_p", [L, 2, DC])
    st_a_s = dout("st_a_s", [L, 2, 30, DA])
    st_c_s = dout("st_c_s", [L, 2, 2, DC])
    st_v_s = dout("st_v_s", [L, 128, DB])

    def sb(name, shape, dt=F32):
        return es.enter_context(nc.sbuf_tensor(name, list(shape), dt))

    def ps(name, shape, dt=F32):
        return es.enter_context(nc.psum_tensor(name, list(shape), dt))

    xres = sb("xres", [128, NB, D])
    xT = sb("xT", [128, 8, NT], BF16)
    hT = sb("hT", [128, NFH, NT], BF16)
    ring = sb("ring", [128, RING, 4096], BF16)
    NA = NMAX + 60
    SCR = max((3 * NA + 1) // 2 + 180 + 8 * NA + 8, 5 * NMAX + 2000, 8 * NMAX + 64)
    scr = sb("scr", [128, SCR])
    lng = sb("lng", [128, D])
    lnb = sb("lnb", [128, D])
    tmpA = [sb(f"tmpA{i}", [128, D]) for i in range(2)]
    xbf = [sb(f"xbf{i}", [128, D], BF16) for i in range(XT_DEFER + 1)]
    sgt = [sb(f"sgt{i}", [128, 512]) for i in range(2)]
    small = sb("small", [128, 8, 16])
    sqjunk = sb("sqjunk", [128, D], BF16)
    identB = sb("identB", [128, 128], BF16)
    identF = sb("identF", [128, 128])
    onesF = sb("onesF", [128, 128])
    maskT = sb("maskTs", [128, 128])
    maskc = sb("maskc", [128, 1])
    Wn = sb("Wn", [128, 6, 128])
    WT = sb("WT", [128, 6, 128], BF16)
    WTs = sb("WTs", [128, 6, 128], BF16)
    bsb = sb("bsb", [128, 3, 128])
    bsbs = sb("bsbs", [128, 3, 128])
    nbg = sb("nbg", [128, DB])
    nbb = sb("nbb", [128, DB])
    cwa = sb("cwa", [128, 3, KA])
    pva = sb("pva", [128, 3, 3])
    cwc = sb("cwc", [128, 2, KC])
    hist_a = sb("hist_a", [128, L, 3, 30])
    hist_c = sb("hist_c", [128, L, 2, 2])
    stA = sb("stA", [32, DA])
    stB = sb("stB", [32, DA])
    stC = sb("stC", [4, DC])
    stD = sb("stD", [4, DC])
    Iq = sb("Iq", [128, 32])
    dqt = sb("dq", [128, 3, KA, 32], BF16)
    dq = dqt[:, :, :, :]
    dq_keys = ["dq"]
    dummy = sb("dummyt", [128, 2])
    epsb = sb("epsb", [128, 2])

    acc = [ps(f"acc{i}", [128, 1024]) for i in range(3)]
    trb = [ps(f"trb{i}", [128, 512]) for i in range(2)]

    class Rot:
        def __init__(self, n):
            self.n = n
            self.i = 0

        def next(self):
            i = self.i
            self.i = (i + 1) % self.n
            return i

    rot_pair = Rot(3)
    rot_tr = Rot(2)
    rot_tmpA = Rot(2)
    rot_xbf = Rot(XT_DEFER + 1)
    rot_sgt = Rot(2)
    rot_small = Rot(8)

    def single(i):
        return acc[i // 2][:, (i % 2) * 512:(i % 2 + 1) * 512], ("ps", i)

    def pair(i):
        return acc[i], [("ps", 2 * i), ("ps", 2 * i + 1)]

    def trbank(i):
        return trb[i], ("ps", 6 + i)

    def bcast_rows(ap2d_row, nparts):
        n = ap2d_row.shape[-1]
        return bass.AP(ap2d_row.tensor, ap2d_row.offset, [[0, nparts], [1, n]])

    def mm(out, lhsT, rhs, start, stop):
        return lambda e: e.matmul(out, lhsT, rhs, start=start, stop=stop)

    def tp(out, in_, ident):
        return lambda e: e.transpose(out, in_, ident)

    def actf(out, in_, func, bias=None, scale=None):
        kw = {}
        if bias is not None:
            kw["bias"] = bias
        if scale is not None:
            kw["scale"] = scale
        return lambda e: e.activation(out, in_, func, **kw)

    def tt(out, in0, in1, op):
        return lambda e: e.tensor_tensor(out, in0, in1, op)

    def ts(out, in0, s1, s2, op0, op1=None):
        if op1 is None:
            return lambda e: e.tensor_scalar(out, in0, s1, None, op0)
        return lambda e: e.tensor_scalar(out, in0, s1, s2, op0, op1)

    def stt(out, in0, scalar, in1, op0, op1):
        return lambda e: e.scalar_tensor_tensor(out, in0, scalar, in1, op0, op1)

    def cp(out, in_):
        return lambda e: e.tensor_copy(out, in_)

    def mset(ap, v):
        return lambda e: e.memset(ap, v)

    identin = din("identin", [128, 128])
    P.dma("sp", identF[:, :], identin, writes=["identF"], semkey="identF")
    P.dma("sp", maskT[:, :], maskT_in, writes=["maskT"], semkey="maskT")
    P.dma("sp", maskc[:, :], maskin, writes=["maskc"], semkey="maskc")
    P.act(actf(identB[:, :], identF[:, :], AF.Copy), reads=["identF"], writes=["identB"])
    for q in range(4):
        P.dve(cp(Iq[32 * q:32 * q + 32, :], identF[32 * q:32 * q + 32, 32 * q:32 * q + 32]), reads=["identF"],
              writes=["Iq"])
    P.dve(mset(onesF[:, :], 1.0), writes=["onesF"])
    P.dve(mset(epsb[:, 0:1], EPS), writes=["epsb"])
    P.dve(mset(epsb[:, 1:2], EPS / (ALPHA * ALPHA)), writes=["epsb"])
    P.dve(mset(hist_a[:, :, :, :].rearrange("p a b c -> p (a b c)"), 0.0), writes=[("hist_a", l) for l in range(L)])
    P.dve(mset(hist_c[:, :, :, :].rearrange("p a b c -> p (a b c)"), 0.0), writes=[("hist_c", l) for l in range(L)])
    P.dve(mset(dummy[:, :], 0.0), writes=["scrfence"])

    units = []

    def ffn_units(l, which):
        wi = W["w_ffn%d_in" % which]
        wo = W["w_ffn%d_out" % which]
        for half in range(2):
            j = half * NFH
            while j < (half + 1) * NFH:
                nch = min(2, (half + 1) * NFH - j)
                units.append(("fin", wi, l, j, nch))
                j += nch
            k = 0
            while k < NFH:
                nk = min(4, NFH - k)
                units.append(("fout", wo, l, half * NFH + k, nk))
                k += nk

    MIXCOLS = [(0, 384), (384, 768), (768, 1152), (1152, 1536), (1536, 2048), (2048, 2304)]

    def mixer_units(l):
        for c0, c1 in MIXCOLS:
            units.append(("mix", W["w_in"], l, c0, c1))
        for h in range(2):
            units.append(("mo", W["w_out"], l, h))

    for _t in cfg["tiles"]:
        for l in range(L):
            ffn_units(l, 1)
            mixer_units(l)
            ffn_units(l, 2)

    class WS:
        issued = 0
        acquired = 0
        released = 0

    def slot_view(s, kind):
        flat = ring[:, s, :]
        if kind == "fin":
            return flat.rearrange("p (k t f) -> p k t f", k=8, t=2)
        if kind == "fout":
            return flat.rearrange("p (k m) -> p k m", k=4)
        return flat.rearrange("p (k c) -> p k c", k=8)

    def w_issue(u):
        s = u % RING
        d = units[u]
        kind = d[0]
        v = slot_view(s, kind)
        if kind == "fin":
            _, w, l, j, nch = d
            wsrc = w[l].rearrange("(kc p) (two f) -> p kc two f", p=128, two=2)
            for part in range(2):
                P.dma("pool", v[:, :, part, 0:nch * 128], wsrc[:, :, part, j * 128:(j + nch) * 128],
                      writes=[("ring", s)], semkey=("ring", s))
            return
        elif kind == "fout":
            _, w, l, k0, nk = d
            src = w[l].rearrange("(kc p) m -> p kc m", p=128)[:, k0:k0 + nk, :]
            dst = v[:, 0:nk, :]
        elif kind == "mix":
            _, w, l, c0, c1 = d
            src = w[l].rearrange("(kc p) c -> p kc c", p=128)[:, :, c0:c1]
            dst = v[:, :, 0:c1 - c0]
        else:
            _, w, l, h = d
            src = w[l].rearrange("(kc p) c -> p kc c", p=128)[:, :, h * 512:(h + 1) * 512]
            dst = v[:, :, :]
        P.dma("pool", dst, src, writes=[("ring", s)], semkey=("ring", s))

    def w_prefetch():
        while WS.issued < len(units) and WS.issued < WS.released + RING:
            w_issue(WS.issued)
            WS.issued += 1

    def w_acquire(kind):
        u = WS.acquired
        assert u < WS.issued, "weight ring too small"
        assert units[u][0] == kind, (units[u][0], kind)
        WS.acquired += 1
        s = u % RING
        return slot_view(s, kind), ("ring", s)

    def w_release(n=1):
        WS.released += n
        w_prefetch()

    w_prefetch()

    pending = []

    def cast_x(b):
        i = rot_xbf.next()
        assert all(pi != i for _, pi in pending)
        P.act(actf(xbf[i][:, :], xres[:, b, :], AF.Copy), reads=[("xres", b)], writes=[("xbf", i)])
        pending.append((b, i))

    def flush_xT(keep=0):
        while len(pending) > keep:
            b, i = pending.pop(0)
            finish_xT(b, i)

    chain_live = set()

    def need_xT(blocks):
        if any(b in chain_live for b in blocks):
            LNP.drain()
        if any(pb in blocks for pb, _ in pending):
            flush_xT(0)

    def finish_xT(b, i):
        j = rot_tr.next()
        trt, trk = trbank(j)
        trv = trt[:, :].bitcast(BF16).rearrange("p (a b) -> p a b", a=8)
        for kc in range(8):
            P.pe(tp(trv[:, kc, :], xbf[i][:, kc * 128:(kc + 1) * 128], identB[:, :]),
                 reads=[("xbf", i), "identB"], writes=[trk])
        P.act(actf(xT[:, :, b * 128:(b + 1) * 128], trv, AF.Copy), reads=[trk], writes=[("xT", b)])

    def make_xT(b):
        cast_x(b)
        flush_xT(0)

    class Pipe:
        def __init__(self):
            self.live = []

        def push(self, gen):
            self.live.insert(0, gen)
            self._advance()

        def _advance(self):
            nxt = []
            for g in self.live:
                try:
                    next(g)
                    nxt.append(g)
                except StopIteration:
                    pass
            self.live = nxt

        def drain(self):
            while self.live:
                self._advance()

    LNP = Pipe()
    LN_NEXT = [None]

    def ln_tick():
        if LNP.live:
            LNP._advance()
        if LN_NEXT[0] is not None and not LNP.live:
            load_ln(*LN_NEXT[0])
            LN_NEXT[0] = None

    def ln_params_ready():
        if LN_NEXT[0] is not None:
            LNP.drain()
            load_ln(*LN_NEXT[0])
            LN_NEXT[0] = None

    def ln_slot():
        i = rot_small.next()
        return small[:, i, :], ("small", i)

    def ln_chain(b, epsp, need_xT_, out_gb, sm, ks):
        kx = ("xres", b)
        chain_live.add(b)
        P.act(lambda e: e.activation(sqjunk[:, :], xres[:, b, :], AF.Square, scale=1.0 / 32.0,
                                     accum_out=sm[:, 1:2]), reads=[kx], writes=["sqjunk", ks])
        yield
        P.dve(tt(sm[:, 2:3], sm[:, 0:1], sm[:, 0:1], ALU.mult), reads=[ks], writes=[ks])
        P.dve(stt(sm[:, 13:14], sm[:, 2:3], -1.0 / (1024.0 * 1024.0), sm[:, 1:2], ALU.mult, ALU.add),
              reads=[ks], writes=[ks])
        eb = epsb[:, 0:1] if epsp == EPS else epsb[:, 1:2]
        P.act(actf(sm[:, 14:15], sm[:, 13:14], AF.Sqrt, bias=eb), reads=[ks, "epsb"], writes=[ks])
        yield
        P.dve(lambda e: e.reciprocal(sm[:, 14:15], sm[:, 14:15]), reads=[ks], writes=[ks])
        P.dve(stt(sm[:, 15:16], sm[:, 0:1], -1.0 / 1024.0, sm[:, 14:15], ALU.mult, ALU.mult), reads=[ks], writes=[ks])
        t = rot_tmpA.next()
        kt = ("tmpA", t)
        P.act(actf(tmpA[t][:, :], xres[:, b, :], AF.Identity, bias=sm[:, 15:16], scale=sm[:, 14:15]),
              reads=[kx, ks], writes=[kt])
        yield
        P.dve(tt(tmpA[t][:, :], tmpA[t][:, :], lng[:, :], ALU.mult), reads=[kt, "lng"], writes=[kt])
        P.dve(tt(xres[:, b, :], tmpA[t][:, :], lnb[:, :], ALU.add), reads=[kt, "lnb"], writes=[kx])
        chain_live.discard(b)
        if need_xT_:
            cast_x(b)
            flush_xT(XT_DEFER)
        if out_gb is not None:
            P.dma("sp", yout[out_gb * 128:(out_gb + 1) * 128, :], xres[:, b, :], reads=[kx], semkey=kx)

    def load_ln(gname, bname, l):
        P.dma("sp", lng[:, :], bcast_rows(W[gname][l], 128), writes=["lng"], semkey="lng")
        P.dma("sp", lnb[:, :], bcast_rows(W[bname][l], 128), writes=["lnb"], semkey="lnb")

    def ffn(tile, l, which, last):
        subs = tile["subs"]
        nb = tile["nb"]
        cscale = 0.5 / ALPHA
        epsp = EPS / (ALPHA * ALPHA)
        LN_NEXT[0] = ("ln1_g" if which == 1 else "ln3_g", "ln1_b" if which == 1 else "ln3_b", l)
        if not LNP.live:
            ln_tick()
        if which == 1:
            issue_param_dmas(tile, l)
        for half in range(2):
            if which == 1 and half == 1:
                finish_params(tile, l)
            def phase_a(wv, wk, jl, nch, sts):
                for st in sts:
                    c0, n, b0, nblk = st["c0"], st["n"], st["b0"], st["nblk"]
                    need_xT(range(b0, b0 + nblk))
                    xkeys = [("xT", b0 + i) for i in range(nblk)]
                    for ci in range(nch):
                        pi = rot_pair.next()
                        pt, pk = pair(pi)
                        for part in range(2):
                            for kc in range(8):
                                P.pe(mm(pt[:, part * 512:part * 512 + n], wv[:, kc, part, ci * 128:(ci + 1) * 128],
                                        xT[:, kc, c0:c0 + n], kc == 0, kc == 7),
                                     reads=[wk] + xkeys, writes=[pk[part]])
                        if ci == 0:
                            ln_tick()
                        si = rot_sgt.next()
                        P.act(actf(sgt[si][:, 0:n], pt[:, 0:n], AF.Silu), reads=[pk[0]], writes=[("sgt", si)])
                        P.dve(tt(hT[:, jl + ci, c0:c0 + n], sgt[si][:, 0:n], pt[:, 512:512 + n], ALU.mult),
                              reads=[("sgt", si), pk[1]], writes=[("hT", jl + ci, b0 + i) for i in range(nblk)])

            ulist = []
            jl = 0
            while jl < NFH:
                nch = min(2, NFH - jl)
                ulist.append((jl, nch))
                jl += nch
            ui = 0
            if half == 0 and LNP.live and len(subs) > 1:
                held = []
                for (jl, nch) in ulist[:PA_HOLD]:
                    wv, wk = w_acquire("fin")
                    held.append((wv, wk, jl, nch))
                    phase_a(wv, wk, jl, nch, subs[:-1])
                for (wv, wk, jl, nch) in held:
                    phase_a(wv, wk, jl, nch, subs[-1:])
                w_release(len(held))
                ui = len(held)
            for (jl, nch) in ulist[ui:]:
                wv, wk = w_acquire("fin")
                phase_a(wv, wk, jl, nch, subs)
                w_release()
            wviews = []
            k = 0
            while k < NFH:
                nk = min(4, NFH - k)
                wviews.append(w_acquire("fout"))
                k += nk
            if half == 1:
                ln_params_ready()
            for b in range(nb):
                pi = rot_pair.next()
                pt, pk = pair(pi)
                for hc in range(2):
                    for k in range(NFH):
                        wv, wk = wviews[k // 4]
                        P.pe(mm(pt[:, hc * 512:(hc + 1) * 512], hT[:, k, b * 128:(b + 1) * 128],
                                wv[:, k % 4, hc * 512:(hc + 1) * 512], k == 0, k == NFH - 1),
                             reads=[wk, ("hT", k, b)], writes=[pk[hc]])
                if half == 0:
                    P.dve(stt(xres[:, b, :], pt[:, :], cscale, xres[:, b, :], ALU.mult, ALU.add),
                          reads=[pk[0], pk[1], ("xres", b)], writes=[("xres", b)])
                else:
                    sm, ks = ln_slot()
                    P.dve(lambda e, b=b, pt=pt, sm=sm: e.scalar_tensor_tensor(
                        xres[:, b, :], pt[:, :], cscale, xres[:, b, :], ALU.mult, ALU.add, accum_out=sm[:, 0:1]),
                        reads=[pk[0], pk[1], ("xres", b)], writes=[("xres", b), ks])
                    LNP.push(ln_chain(b, epsp, not last, (tile["gb0"] + b) if last else None, sm, ks))
            if last:
                LNP.drain()
            w_release(len(wviews))

    class ScrAlloc:
        def __init__(self):
            self.off = 0

        def take(self, n):
            o = self.off
            self.off += n
            assert self.off <= SCR, (self.off, SCR)
            return o

    def fence():
        P.dve(mset(dummy[:, 0:1], 0.0), writes=["scrfence"])

    def seg_layout(st, hl):
        if st["kind"] == "S":
            return [(0, 0, 64), (hl + 64, 64, 64)], 2 * (hl + 64)
        return [(0, 0, st["n"])], hl + st["n"]

    def issue_param_dmas(tile, l):
        has_p = any(s["kind"] == "P" for s in tile["subs"])
        has_s = any(s["kind"] == "S" for s in tile["subs"])

        def row(ap1d):
            n = ap1d.shape[-1]
            return bass.AP(ap1d.tensor, ap1d.offset, [[n, 1], [1, n]])
        P.dma("sp", stA[0:KA, :], W["conv_a_w"][l], writes=["stA"], semkey="stA")
        for r, nm in enumerate(("conv_a_b", "norm_a_g", "norm_a_b")):
            P.dma("sp", stB[r:r + 1, :], row(W[nm][l]), writes=["stB"], semkey="stB")
        P.dma("sp", stC[0:KC, :], W["conv_c_w"][l], writes=["stC"], semkey="stC")
        P.dma("sp", nbg[:, :], bcast_rows(W["norm_b_g"][l], 128), writes=["nbg"], semkey="nbg")
        P.dma("sp", nbb[:, :], bcast_rows(W["norm_b_b"][l], 128), writes=["nbb"], semkey="nbb")
        bs_t = W["b_s"].tensor
        if has_p:
            P.dma("sp", Wn[:, :, :], W["w_s"][l].rearrange("h i j -> i h j"), writes=["Wn"], semkey="Wn")
            for h2 in range(2):
                src_ = bass.AP(bs_t, W["b_s"][l].offset + h2 * 128, [[0, 64], [256, 3], [1, 128]])
                P.dma("sp", bsb[h2 * 64:(h2 + 1) * 64, :, :], src_, writes=["bsb"], semkey="bsb")
        if has_s:
            for h2 in range(2):
                src_ = bass.AP(bs_t, W["b_s"][l].offset + h2 * 128, [[0, 64], [256, 3], [1, 64]])
                for r in range(2):
                    P.dma("sp", bsbs[h2 * 64:(h2 + 1) * 64, :, r * 64:(r + 1) * 64], src_,
                          writes=["bsbs"], semkey="bsbs")

    def finish_params(tile, l):
        has_p = any(s["kind"] == "P" for s in tile["subs"])
        has_s = any(s["kind"] == "S" for s in tile["subs"])
        trt, trk = trbank(rot_tr.next())
        for c in range(3):
            P.pe(tp(trt[:, c * 32:c * 32 + KA], stA[0:KA, c * 128:(c + 1) * 128], identF[0:KA, 0:KA]),
                 reads=["stA", "identF"], writes=[trk])
        P.act(actf(cwa[:, :, :], trt[:, 0:96].rearrange("p (c k) -> p c k", c=3)[:, :, 0:KA], AF.Copy),
              reads=[trk], writes=["cwa"])
        trt, trk = trbank(rot_tr.next())
        for c in range(3):
            P.pe(tp(trt[:, c * 32:c * 32 + 3], stB[0:3, c * 128:(c + 1) * 128], identF[0:3, 0:3]),
                 reads=["stB", "identF"], writes=[trk])
        P.act(actf(pva[:, :, :], trt[:, 0:96].rearrange("p (c k) -> p c k", c=3)[:, :, 0:3], AF.Copy),
              reads=[trk], writes=["pva"])
        trt, trk = trbank(rot_tr.next())
        for c in range(2):
            P.pe(tp(trt[:, c * 32:c * 32 + KC], stC[0:KC, c * 128:(c + 1) * 128], identF[0:KC, 0:KC]),
                 reads=["stC", "identF"], writes=[trk])
        P.act(actf(cwc[:, :, :], trt[:, 0:64].rearrange("p (c k) -> p c k", c=2)[:, :, 0:KC], AF.Copy),
              reads=[trk], writes=["cwc"])
        for c in range(3):
            cw = cwa[:, c, :]
            i0 = bass.AP(Iq, 0, [list(Iq[:, :].ap[0]), [0, KA], [1, 32]])
            i1 = bass.AP(cwa, cw.offset, [list(cw.ap[0]), [1, KA], [0, 32]])
            P.dve(tt(dq[:, c, :, :], i0, i1, ALU.mult), reads=["Iq", "cwa"], writes=dq_keys)
        if has_p:
            for h in range(6):
                trt, trk = trbank(rot_tr.next())
                P.pe(tp(trt[:, 0:128], Wn[:, h, :], identF[:, :]), reads=["Wn", "identF"], writes=[trk])
                P.dve(tt(WT[:, h, :], trt[:, 0:128], maskT[:, :], ALU.mult), reads=[trk, "maskT"], writes=["WT"])
        if has_s:
            P.dve(mset(Wn[0:64, :, 64:128], 0.0), reads=["Wn"], writes=["Wn"])
            P.dve(mset(Wn[64:128, :, 0:64], 0.0), reads=["Wn"], writes=["Wn"])
            for q in range(2):
                P.dma("sp", Wn[q * 64:(q + 1) * 64, :, q * 64:(q + 1) * 64],
                      W["w_s"][l][:, 0:64, 0:64].rearrange("h i j -> i h j"), writes=["Wn"], semkey="Wn")
            for h in range(6):
                trt, trk = trbank(rot_tr.next())
                P.pe(tp(trt[:, 0:128], Wn[:, h, :], identF[:, :]), reads=["Wn", "identF"], writes=[trk])
                P.act(actf(WTs[:, h, :], trt[:, 0:128], AF.Copy), reads=[trk], writes=["WTs"])

    def hist_fill_and_state(st, l, ext, nch, hl, segs, hist, cache, cst, stg, out_p, out_s, kname, kext):
        if st["kind"] == "P":
            P.act(actf(ext[:, :, 0:hl], hist[:, l, :, :], AF.Copy), reads=[(kname, l)], writes=[kext + ("h",)])
        else:
            for s in range(2):
                cstt, cstk = cst
                P.dma("sp", cstt[0:hl, :], cache[l, s], writes=[cstk], semkey=cstk)
                j = rot_tr.next()
                trt, trk = trbank(j)
                for c in range(nch):
                    P.pe(tp(trt[:, c * 32:c * 32 + hl], cstt[0:hl, c * 128:(c + 1) * 128], identF[0:hl, 0:hl]),
                         reads=[cstk, "identF"], writes=[trk])
                eo = segs[s][0]
                P.act(actf(ext[:, :, eo:eo + hl], trt[:, 0:nch * 32].rearrange("p (c k) -> p c k", c=nch)[:, :, 0:hl],
                           AF.Copy), reads=[trk], writes=[kext + ("h",)])

    def state_out(st, l, ext, nch, hl, segs, hist, stg, out_p, out_s, kname, kext):
        rk = [kext + ("h",), kext + ("d",)]
        if st["kind"] == "P":
            eo, _, ntok = segs[0]
            src = ext[:, :, eo + ntok:eo + ntok + hl]
            if st["mask_after"]:
                P.act(actf(hist[:, l, :, :], src, AF.Copy, scale=maskc[:, 0:1]), reads=rk + ["maskc"],
                      writes=[(kname, l)])
            else:
                P.act(actf(hist[:, l, :, :], src, AF.Copy), reads=rk, writes=[(kname, l)])
            outs = [(src, out_p[l])] if st["final"] else []
        else:
            outs = []
            for s in range(2):
                eo, _, ntok = segs[s]
                outs.append((ext[:, :, eo + ntok:eo + ntok + hl], out_s[l, s]))
        for src, dst in outs:
            j = rot_tr.next()
            trt, trk = trbank(j)
            for c in range(nch):
                P.pe(tp(trt[0:hl, c * 128:(c + 1) * 128], src[:, c, :], identF[:, :]), reads=rk + ["identF"],
                     writes=[trk])
            stgt, stgk = stg
            P.act(actf(stgt[0:hl, :], trt[0:hl, 0:nch * 128], AF.Copy), reads=[trk], writes=[stgk])
            P.dma("sp", dst, stgt[0:hl, :], reads=[stgk], semkey=stgk)

    def mixer(tile, l):
        subs = tile["subs"]
        nb = tile["nb"]
        epsp = EPS / (ALPHA * ALPHA)
        LN_NEXT[0] = ("ln2_g", "ln2_b", l)
        if not LNP.live:
            ln_tick()
        fence()
        sa = ScrAlloc()
        o_aext = sa.take((3 * NA + 1) // 2)
        o_tail = sa.take(3 * 2 * 30)
        o_co = sa.take(3 * (NMAX + 60))
        o_f = [sa.take(NMAX + 60) for _ in range(5)]
        aext = scr[:, o_aext:o_aext + (3 * NA + 1) // 2].bitcast(BF16)[:, 0:3 * NA].rearrange("p (c k) -> p c k", c=3)
        tail = scr[:, o_tail:o_tail + 180].rearrange("p (c s k) -> p c s k", c=3, s=2)
        co = scr[:, o_co:o_co + 3 * (NMAX + 60)].rearrange("p (c k) -> p c k", c=3)
        fbuf = [scr[:, o:o + NMAX + 60] for o in o_f]
        fkey = lambda i: ("scr", "A", "f", i)
        wp, wpk = w_acquire("mix")
        wg, wgk = w_acquire("mix")
        rot_s3 = Rot(3)
        kp = rot_pair.i
        border = [2 * ((kp + d) % 3) + h for d in range(3) for h in range(2)]
        proj_banks, conv_banks = border[0:3], border[3:6]
        kext = ("scr", "A", "aext")
        ktl = ("scr", "A", "tail")
        kco = ("scr", "A", "co")
        def ma_body(st):
            c0, n, b0, nblk = st["c0"], st["n"], st["b0"], st["nblk"]
            need_xT(range(b0, b0 + nblk))
            xkeys = [("xT", b0 + i) for i in range(nblk)]
            segs, na = seg_layout(st, 30)
            nco = na - 30
            is_s = st["kind"] == "S"
            if not is_s:
                P.act(actf(aext[:, :, 0:30], hist_a[:, l, :, :], AF.Copy), reads=[("hist_a", l)],
                      writes=[kext + ("h",)])
            else:
                for s in range(2):
                    P.dma("sp", stA[0:30, :], cache_a[l, s], writes=["stA"], semkey="stA")
                    trt, trk = trbank(rot_tr.next())
                    for c in range(3):
                        P.pe(tp(trt[:, c * 32:c * 32 + 30], stA[0:30, c * 128:(c + 1) * 128], identF[0:30, 0:30]),
                             reads=["stA", "identF"], writes=[trk])
                    eo = segs[s][0]
                    P.act(actf(aext[:, :, eo:eo + 30],
                               trt[:, 0:96].rearrange("p (c k) -> p c k", c=3)[:, :, 0:30], AF.Copy),
                          reads=[trk], writes=[kext + ("h",)])
            for c in range(3):
                gi = proj_banks[rot_s3.next()]
                gt, gk = single(gi)
                for kc in range(8):
                    P.pe(mm(gt[:, 0:n], wg[:, kc, c * 128:(c + 1) * 128], xT[:, kc, c0:c0 + n], kc == 0, kc == 7),
                         reads=[wgk] + xkeys, writes=[gk])
                sgi = c % 2
                P.act(actf(fbuf[sgi][:, 0:n], gt[:, 0:n], AF.Sigmoid), reads=[gk], writes=[fkey(sgi)])
                pi = proj_banks[rot_s3.next()]
                ptt, ppk = single(pi)
                for kc in range(8):
                    P.pe(mm(ptt[:, 0:n], wp[:, kc, c * 128:(c + 1) * 128], xT[:, kc, c0:c0 + n], kc == 0, kc == 7),
                         reads=[wpk] + xkeys, writes=[ppk])
                ln_tick()
                for si, (eo, to, ntok) in enumerate(segs):
                    P.dve(tt(aext[:, c, eo + 30:eo + 30 + ntok], ptt[:, to:to + ntok], fbuf[sgi][:, to:to + ntok],
                             ALU.mult), reads=[ppk, fkey(sgi)], writes=[kext + ("d",)])
                    P.dve(tt(tail[:, c, si, :], ptt[:, to + ntok - 30:to + ntok], fbuf[sgi][:, to + ntok - 30:to + ntok],
                             ALU.mult), reads=[ppk, fkey(sgi)], writes=[ktl])
            if not is_s:
                if st["mask_after"]:
                    P.act(actf(hist_a[:, l, :, :], tail[:, :, 0, :], AF.Copy, scale=maskc[:, 0:1]),
                          reads=[ktl, "maskc"], writes=[("hist_a", l)])
                else:
                    P.act(actf(hist_a[:, l, :, :], tail[:, :, 0, :], AF.Copy), reads=[ktl], writes=[("hist_a", l)])
                outs = [(0, st_a_p[l])] if st["final"] else []
            else:
                outs = [(s, st_a_s[l, s]) for s in range(2)]
            for si, dst in outs:
                trt, trk = trbank(rot_tr.next())
                for c in range(3):
                    P.pe(tp(trt[0:30, c * 128:(c + 1) * 128], tail[:, c, si, :], identF[:, :]), reads=[ktl, "identF"],
                         writes=[trk])
                P.act(actf(stB[0:30, :], trt[0:30, 0:384], AF.Copy), reads=[trk], writes=["stB"])
                P.dma("sp", dst, stB[0:30, :], reads=["stB"], semkey="stB")
            yield
            cbank = [single(conv_banks[c]) for c in range(3)]
            for c in range(3):
                cb, cbk = cbank[c]
                for k in range(KA):
                    for q in range(4):
                        P.pe(lambda e, cb=cb, c=c, k=k, q=q, nco=nco: e.matmul(
                            cb[32 * q:32 * q + 32, 0:nco], dq[32 * q:32 * q + 32, c, k, :],
                            aext[32 * q:32 * q + 32, c, k:k + nco], start=(k == 0), stop=(k == KA - 1),
                            tile_position=(32 * q, 32 * q)),
                            reads=[kext + ("h",), kext + ("d",)] + dq_keys, writes=[cbk])
            yield
            s1t, s1k = trbank(0)
            s2t, s2k = trbank(1)
            for c in range(3):
                cb, cbk = cbank[c]
                sqi = 1 + c % 2
                P.act(actf(co[:, c, 0:nco], cb[:, 0:nco], AF.Identity, bias=pva[:, c, 0:1]), reads=[cbk, "pva"],
                      writes=[kco + (c,)])
                P.act(actf(fbuf[sqi][:, 0:nco], cb[:, 0:nco], AF.Square, bias=pva[:, c, 0:1]), reads=[cbk, "pva"],
                      writes=[fkey(sqi)])
                P.pe(mm(s1t[:, 0:nco], onesF[:, :], co[:, c, 0:nco], c == 0, c == 2),
                     reads=["onesF", kco + (c,)], writes=[s1k])
                P.pe(mm(s2t[:, 0:nco], onesF[:, :], fbuf[sqi][:, 0:nco], c == 0, c == 2),
                     reads=["onesF", fkey(sqi)], writes=[s2k])
            mt, rt = fbuf[3], fbuf[4]
            tm = fbuf[0]
            KM, KR = fkey(3), fkey(4)
            P.dve(ts(mt[:, 0:nco], s1t[:, 0:nco], 1.0 / DA, None, ALU.mult), reads=[s1k], writes=[KM])
            P.dve(tt(tm[:, 0:nco], mt[:, 0:nco], mt[:, 0:nco], ALU.mult), reads=[KM], writes=[fkey(0)])
            P.dve(stt(tm[:, 0:nco], s2t[:, 0:nco], 1.0 / DA, tm[:, 0:nco], ALU.mult, ALU.subtract),
                  reads=[s2k, fkey(0)], writes=[fkey(0)])
            P.act(actf(rt[:, 0:nco], tm[:, 0:nco], AF.Sqrt, bias=epsb[:, 0:1]), reads=[fkey(0), "epsb"],
                  writes=[KR])
            P.dve(lambda e, rt=rt, nco=nco: e.reciprocal(rt[:, 0:nco], rt[:, 0:nco]), reads=[KR], writes=[KR])
            for c in range(3):
                P.dve(tt(co[:, c, 0:nco], co[:, c, 0:nco], mt[:, 0:nco], ALU.subtract), reads=[kco + (c,), KM],
                      writes=[kco + (c,)])
                P.dve(tt(co[:, c, 0:nco], co[:, c, 0:nco], rt[:, 0:nco], ALU.mult), reads=[kco + (c,), KR],
                      writes=[kco + (c,)])
                for (eo, to, ntok) in segs:
                    P.act(actf(hT[:, c, c0 + to:c0 + to + ntok], co[:, c, eo:eo + ntok], AF.Silu,
                               bias=pva[:, c, 2:3], scale=pva[:, c, 1:2]),
                          reads=[kco + (c,), "pva"], writes=[("hT", c, b0 + i) for i in range(nblk)])
        gens = [ma_body(st) for st in subs]
        step = lambda g: next(g, None)
        step(gens[0])
        step(gens[0])
        for s in range(1, len(gens)):
            step(gens[s])
            step(gens[s - 1])
            step(gens[s])
        step(gens[-1])
        w_release(2)
        fence()
        sa = ScrAlloc()
        o_u = sa.take(3 * NMAX)
        o_v32 = [sa.take(DB) for _ in range(3)]
        o_vbf = sa.take(4 * DB // 2)
        o_tb = [sa.take(NMAX) for _ in range(2)]
        uT = scr[:, o_u:o_u + 3 * NMAX].rearrange("p (c k) -> p c k", c=3)
        vn32 = [scr[:, o:o + DB] for o in o_v32]
        vnbf = scr[:, o_vbf:o_vbf + 4 * DB // 2].bitcast(BF16).rearrange("p (b k) -> p b k", b=4)
        tb = [scr[:, o:o + NMAX] for o in o_tb]
        wu, wuk = w_acquire("mix")
        wvv, wvk = w_acquire("mix")
        rot_v = Rot(3)
        rot_v32 = Rot(3)
        for st in subs:
            c0, n, b0, nblk = st["c0"], st["n"], st["b0"], st["nblk"]
            need_xT(range(b0, b0 + nblk))
            xkeys = [("xT", b0 + i) for i in range(nblk)]
            is_s = st["kind"] == "S"
            sbank = [single(c) for c in range(3)]
            wt_sb, wt_k = (WTs, "WTs") if is_s else (WT, "WT")
            pend_s = []

            def emit_s(bi, kvb, sbank=sbank, wt_sb=wt_sb, wt_k=wt_k):
                for c in range(3):
                    stt_, sk = sbank[c]
                    for h2 in range(2):
                        h = 2 * c + h2
                        P.pe(mm(stt_[h2 * 64:(h2 + 1) * 64, bi * 128:(bi + 1) * 128], vnbf[:, bi, h * 64:(h + 1) * 64],
                                wt_sb[:, h, :], True, True), reads=[kvb, wt_k], writes=[sk])

            def mb_chain(bi, b0=b0, is_s=is_s, emit_s=emit_s):
                b = b0 + bi
                vi = 3 + rot_v.next()
                vt, vk = single(vi)
                vbanks[bi] = (vt, vk)
                for kc in range(8):
                    P.pe(mm(vt[:, 0:DB], xT[:, kc, b * 128:(b + 1) * 128], wvv[:, kc, 0:DB], kc == 0, kc == 7),
                         reads=[wvk, ("xT", b)], writes=[vk])
                i = rot_small.next()
                sm = small[:, i, :]
                ks = ("small", i)
                P.dve(lambda e: e.bn_stats(sm[:, 0:6], vt[:, 0:DB]), reads=[vk], writes=[ks])
                P.dve(lambda e: e.bn_aggr(sm[:, 12:14], sm[:, 0:6]), reads=[ks], writes=[ks])
                P.act(actf(sm[:, 14:15], sm[:, 13:14], AF.Sqrt, bias=epsb[:, 0:1]), reads=[ks, "epsb"], writes=[ks])
                yield
                P.dve(lambda e: e.reciprocal(sm[:, 14:15], sm[:, 14:15]), reads=[ks], writes=[ks])
                P.dve(stt(sm[:, 15:16], sm[:, 12:13], -1.0, sm[:, 14:15], ALU.mult, ALU.mult), reads=[ks], writes=[ks])
                q = rot_v32.next()
                kv = ("scr", "B", "vn32", q)
                P.act(actf(vn32[q][:, :], vt[:, 0:DB], AF.Identity, bias=sm[:, 15:16], scale=sm[:, 14:15]),
                      reads=[vk, ks], writes=[kv])
                yield
                P.dve(tt(vn32[q][:, :], vn32[q][:, :], nbg[:, :], ALU.mult), reads=[kv, "nbg"], writes=[kv])
                kvb = ("scr", "B", "vnbf", bi)
                if is_s:
                    P.dve(tt(vn32[q][:, :], vn32[q][:, :], nbb[:, :], ALU.add), reads=[kv, "nbb"], writes=[kv])
                    P.dma("sp", st_v_s[l], vn32[q][:, :], reads=[kv], semkey=kv)
                    P.act(actf(vnbf[:, bi, :], vn32[q][:, :], AF.Copy), reads=[kv], writes=[kvb])
                else:
                    P.dve(tt(vnbf[:, bi, :], vn32[q][:, :], nbb[:, :], ALU.add), reads=[kv, "nbb"], writes=[kvb])
                yield
                emit_s(bi, kvb)

            mbp = Pipe()
            vbanks = {}
            assert nblk <= 3
            for bi in range(nblk):
                mbp.push(mb_chain(bi))
            if nblk >= 2:
                ubanks = [trbank(0), trbank(1), vbanks[0]]
            else:
                ubanks = sbank
            for c in range(3):
                ut, uk = ubanks[c]
                for kc in range(8):
                    P.pe(mm(ut[:, 0:n], wu[:, kc, c * 128:(c + 1) * 128], xT[:, kc, c0:c0 + n], kc == 0, kc == 7),
                         reads=[wuk] + xkeys, writes=[uk])
                if nblk < 2:
                    P.act(actf(uT[:, c, 0:n], ut[:, 0:n], AF.Copy), reads=[uk], writes=[("scr", "B", "uT", c)])
            mbp.drain()
            if nblk >= 2:
                for c in range(3):
                    ut, uk = ubanks[c]
                    P.act(actf(uT[:, c, 0:n], ut[:, 0:n], AF.Copy), reads=[uk], writes=[("scr", "B", "uT", c)])
            bs_sb, bs_k = (bsbs, "bsbs") if is_s else (bsb, "bsb")
            for c in range(3):
                stt_, sk = sbank[c]
                ti = c % 2
                ktb = ("scr", "B", "tb", ti)
                bsrc = bs_sb[:, c, :]
                bb = bass.AP(bs_sb, bsrc.offset, [list(bsrc.ap[0]), [0, nblk], [1, 128]])
                P.dve(tt(tb[ti][:, 0:n].rearrange("p (b i) -> p b i", b=nblk),
                         stt_[:, 0:n].rearrange("p (b i) -> p b i", b=nblk), bb, ALU.add),
                      reads=[sk, bs_k], writes=[ktb])
                P.dve(tt(hT[:, 3 + c, c0:c0 + n], tb[ti][:, 0:n], uT[:, c, 0:n], ALU.mult),
                      reads=[ktb, ("scr", "B", "uT", c)], writes=[("hT", 3 + c, b0 + i) for i in range(nblk)])
        w_release(2)
        fence()
        sa = ScrAlloc()
        o_g = [sa.take(NMAX) for _ in range(2)]
        NCX = NMAX + 8
        o_cx = sa.take(2 * NCX)
        o_gb = sa.take(2 * NMAX)
        o_ca = sa.take(2 * NCX)
        gct = [scr[:, o:o + NMAX] for o in o_g]
        cxext = scr[:, o_cx:o_cx + 2 * NCX].rearrange("p (c k) -> p c k", c=2)
        gbt = scr[:, o_gb:o_gb + 2 * NMAX].rearrange("p (c k) -> p c k", c=2)
        cacc = scr[:, o_ca:o_ca + 2 * NCX].rearrange("p (c k) -> p c k", c=2)
        wc0, wc0k = w_acquire("mix")
        wc1, wc1k = w_acquire("mix")
        rot_s6 = Rot(6)
        for st in subs:
            c0, n, b0, nblk = st["c0"], st["n"], st["b0"], st["nblk"]
            need_xT(range(b0, b0 + nblk))
            xkeys = [("xT", b0 + i) for i in range(nblk)]
            segs, nx = seg_layout(st, 2)
            nco = nx - 2
            kext = ("scr", "C", "cxext")
            hist_fill_and_state(st, l, cxext, 2, 2, segs, hist_c, cache_c, (stC, "stC"), None, st_c_p, st_c_s, "hist_c", kext)
            for c in range(2):
                gi = rot_s6.next()
                gt, gk = single(gi)
                for kc in range(8):
                    P.pe(mm(gt[:, 0:n], wc1[:, kc, c * 128:(c + 1) * 128], xT[:, kc, c0:c0 + n], kc == 0, kc == 7),
                         reads=[wc1k] + xkeys, writes=[gk])
                kg = ("scr", "C", "gct", c)
                P.act(actf(gct[c][:, 0:n], gt[:, 0:n], AF.Copy), reads=[gk], writes=[kg])
                xi = rot_s6.next()
                xt_, xk = single(xi)
                for kc in range(8):
                    P.pe(mm(xt_[:, 0:n], wc0[:, kc, c * 128:(c + 1) * 128], xT[:, kc, c0:c0 + n], kc == 0, kc == 7),
                         reads=[wc0k] + xkeys, writes=[xk])
                for (eo, to, ntok) in segs:
                    P.dve(tt(cxext[:, c, eo + 2:eo + 2 + ntok], xt_[:, to:to + ntok], gct[c][:, to:to + ntok], ALU.mult),
                          reads=[xk, kg], writes=[kext + ("d",)])
                bi_ = rot_s6.next()
                bt, bk = single(bi_)
                for kc in range(8):
                    P.pe(mm(bt[:, 0:n], wc0[:, kc, 256 + c * 128:256 + (c + 1) * 128], xT[:, kc, c0:c0 + n],
                            kc == 0, kc == 7), reads=[wc0k] + xkeys, writes=[bk])
                P.act(actf(gbt[:, c, 0:n], bt[:, 0:n], AF.Copy), reads=[bk], writes=[("scr", "C", "gb", c)])
            state_out(st, l, cxext, 2, 2, segs, hist_c, (stD, "stD"), st_c_p, st_c_s, "hist_c", kext)
            for c in range(2):
                kca = ("scr", "C", "cacc", c)
                rk = [kext + ("h",), kext + ("d",), "cwc"]
                P.dve(ts(cacc[:, c, 0:nco], cxext[:, c, 0:nco], cwc[:, c, 0:1], None, ALU.mult), reads=rk, writes=[kca])
                for k in range(1, KC):
                    P.dve(stt(cacc[:, c, 0:nco], cxext[:, c, k:k + nco], cwc[:, c, k:k + 1], cacc[:, c, 0:nco],
                              ALU.mult, ALU.add), reads=rk + [kca], writes=[kca])
                for (eo, to, ntok) in segs:
                    P.dve(tt(hT[:, 6 + c, c0 + to:c0 + to + ntok], cacc[:, c, eo:eo + ntok], gbt[:, c, to:to + ntok],
                             ALU.mult), reads=[kca, ("scr", "C", "gb", c)],
                          writes=[("hT", 6 + c, b0 + i) for i in range(nblk)])
        w_release(2)
        wo = [w_acquire("mo") for _ in range(2)]
        ln_params_ready()
        for b in range(nb):
            pi = rot_pair.next()
            pt, pk = pair(pi)
            for hc in range(2):
                wv, wk = wo[hc]
                for kc in range(8):
                    P.pe(mm(pt[:, hc * 512:(hc + 1) * 512], hT[:, kc, b * 128:(b + 1) * 128], wv[:, kc, :],
                            kc == 0, kc == 7), reads=[wk, ("hT", kc, b)], writes=[pk[hc]])
            sm, ks = ln_slot()
            P.dve(lambda e, b=b, pt=pt, sm=sm: e.scalar_tensor_tensor(
                xres[:, b, :], pt[:, :], 1.0 / ALPHA, xres[:, b, :], ALU.mult, ALU.add, accum_out=sm[:, 0:1]),
                reads=[pk[0], pk[1], ("xres", b)], writes=[("xres", b), ks])
            LNP.push(ln_chain(b, epsp, True, None, sm, ks))
        w_release(2)

    for tile in cfg["tiles"]:
        nb = tile["nb"]
        for b in range(nb):
            gb = tile["gb0"] + b
            P.dma("sp", xres[:, b, :], xin[gb * 128:(gb + 1) * 128, :], writes=[("xres", b)], semkey=("xres", b))
        for b in range(nb):
            cast_x(b)
            flush_xT(1)
        flush_xT(0)
        for l in range(L):
            ffn(tile, l, 1, last=False)
            mixer(tile, l)
            ffn(tile, l, 2, last=(l == L - 1))
    assert WS.acquired == len(units) and WS.released == len(units)

    P.emit(nc, es)
    es.close()
    return nc, P


WEIGHT_NAMES = ["w_ffn1_in", "w_ffn1_out", "ln1_g", "ln1_b", "w_in", "conv_a_w", "conv_a_b", "norm_a_g",
                "norm_a_b", "norm_b_g", "norm_b_b", "w_s", "b_s", "conv_c_w", "w_out", "ln2_g", "ln2_b",
                "w_ffn2_in", "w_ffn2_out", "ln3_g", "ln3_b"]


def const_inputs():
    ident = np.eye(128, dtype=np.float32)
    j = np.arange(128)[:, None]
    i = np.arange(128)[None, :]
    maskT = ((j // 64) <= (i // 64)).astype(np.float32)
    return ident, maskT


def full_cfg():
    P = lambda n, **kw: dict(kind="P", nblk=n, **kw)
    tiles = [
        [P(3), P(3), P(3)],
        [P(3), P(3), P(3)],
        [P(3), P(3), P(3)],
        [P(3), P(3), P(1, final=True), dict(kind="S")],
    ]
    return make_cfg(4, tiles, ring=7)


NPB = 34
SPLIT = NPB * 128


_CACHE = {}


def kernel(**inputs):
    cfg = full_cfg()
    if "nc" not in _CACHE:
        _CACHE["nc"] = build_program(cfg)[0]
    nc = _CACHE["nc"]
    x_prompt = np.asarray(inputs["x_prompt"], dtype=np.float32)
    x_sample = np.asarray(inputs["x_sample"], dtype=np.float32)
    ca = np.asarray(inputs["cache_conv_a"], dtype=np.float32)
    cc = np.asarray(inputs["cache_conv_c"], dtype=np.float32)
    ident, maskT = const_inputs()
    wts = {k: np.ascontiguousarray(np.asarray(inputs[k], dtype=np.float32)) for k in WEIGHT_NAMES}
    START1 = 8192 - SPLIT
    assert SPLIT - START1 >= 384 and START1 % 128 == 0
    in_maps = []
    for c in range(8):
        seq, half = c // 2, c % 2
        xp = x_prompt[seq, 0:SPLIT] if half == 0 else x_prompt[seq, START1:8192]
        xs = x_sample[2 * c:2 * c + 2].reshape(128, D)
        m = dict(wts)
        m["xin"] = np.ascontiguousarray(np.concatenate([xp, xs], axis=0))
        m["cache_a"] = np.ascontiguousarray(ca[:, 2 * c:2 * c + 2])
        m["cache_c"] = np.ascontiguousarray(cc[:, 2 * c:2 * c + 2])
        m["maskin"] = np.ones((128, 1), np.float32)
        m["maskT"] = maskT
        m["identin"] = ident
        in_maps.append(m)
    res = run_bass_kernel_spmd(nc, in_maps, core_ids=list(range(8)))
    r = res.results
    y_prompt = np.empty((4, 8192, D), np.float32)
    y_sample = np.empty((16, 64, D), np.float32)
    st_a_p = np.empty((4, 4, 30, DA), np.float32)
    st_c_p = np.empty((4, 4, 2, DC), np.float32)
    st_a_s = np.empty((4, 16, 30, DA), np.float32)
    st_c_s = np.empty((4, 16, 2, DC), np.float32)
    st_v_s = np.empty((4, 16, 64, DB), np.float32)
    for c in range(8):
        seq, half = c // 2, c % 2
        yo = r[c]["yout"]
        if half == 0:
            y_prompt[seq, 0:SPLIT] = yo[0:SPLIT]
        else:
            y_prompt[seq, SPLIT:8192] = yo[SPLIT - START1:SPLIT]
        y_sample[2 * c:2 * c + 2] = yo[SPLIT:SPLIT + 128].reshape(2, 64, D)
        if half == 1:
            st_a_p[:, seq] = r[c]["st_a_p"]
            st_c_p[:, seq] = r[c]["st_c_p"]
        st_a_s[:, 2 * c:2 * c + 2] = r[c]["st_a_s"]
        st_c_s[:, 2 * c:2 * c + 2] = r[c]["st_c_s"]
        st_v_s[:, 2 * c:2 * c + 2] = r[c]["st_v_s"].reshape(4, 2, 64, DB)
    return (y_prompt, y_sample, st_a_p, st_c_p, st_a_s, st_c_s, st_v_s)
```
